# Optimizing a Trainium2 kernel written in Bass

```python
import jax
import jax.numpy as jnp
from jax import lax
import numpy as np

D_MODEL = 2048
BATCH = 4
SEQ = 2048
DEPTH = 4

GRID_W = 64
CTX_LEN = 256
N_EVEN = (DEPTH + 1) // 2
N_ODD = DEPTH // 2
Q_BLOCK = 128
ROPE_THETA = 10000.0
NORM_EPS = 1e-6

N_FOURIER_GROUPS = 4
FOURIER_GROUP_W = D_MODEL // 16
FOURIER_W = N_FOURIER_GROUPS * FOURIER_GROUP_W
HEAD_DIM = 128
N_Q_HEADS = (D_MODEL - FOURIER_W) // HEAD_DIM
N_KV_HEADS = 4
Q_W = N_Q_HEADS * HEAD_DIM
KV_W = N_KV_HEADS * HEAD_DIM
EVEN_KV_START = FOURIER_W + Q_W
EVEN_IN_W = EVEN_KV_START + 2 * KV_W

CONV_W = D_MODEL // 4
CONV_K = 31
MLA_HEADS = 12
MLA_NOPE = 128
MLA_ROPE = 64
MLA_V = 128
Q_LORA = 512
KV_LORA = 512
ODD_KV_START = 2 * CONV_W + Q_LORA
ODD_IN_W = ODD_KV_START + KV_LORA + MLA_ROPE
ODD_OUT_IN = CONV_W + MLA_HEADS * MLA_V

D_FF = ((8 * D_MODEL // 3 + 127) // 128) * 128
FFN_K = 3

kernel_name = 'hybrid_fourier_gqa_conformer_mla_dit'


def rms_norm(x, gain=None):
    xf = x.astype(jnp.float32)
    y = xf * lax.rsqrt(jnp.mean(xf * xf, axis=-1, keepdims=True) + NORM_EPS)
    if gain is not None:
        y = y * gain.astype(jnp.float32)
    return y.astype(x.dtype)


def layer_norm(x, gain, bias):
    xf = x.astype(jnp.float32)
    mu = jnp.mean(xf, axis=-1, keepdims=True)
    var = jnp.mean(jnp.square(xf - mu), axis=-1, keepdims=True)
    y = (xf - mu) * lax.rsqrt(var + NORM_EPS) * gain.astype(jnp.float32) + bias.astype(jnp.float32)
    return y.astype(x.dtype)


def modulate(x, shift, scale):
    return rms_norm(x) * (1 + scale) + shift


def depthwise_conv(x, w, b):
    K, C = w.shape
    y = lax.conv_general_dilated(
        x, w[:, None, :].astype(x.dtype), window_strides=(1,), padding=[(K // 2, K // 2)],
        dimension_numbers=('NWC', 'WIO', 'NWC'), feature_group_count=C)
    return y + b.astype(x.dtype)


def axial_rope_tables(L, rot_dim):
    rows = L // GRID_W
    row = jnp.repeat(jnp.arange(rows, dtype=jnp.float32), GRID_W)
    col = jnp.tile(jnp.arange(GRID_W, dtype=jnp.float32), rows)
    n = rot_dim // 4
    inv = jnp.power(ROPE_THETA, -jnp.arange(n, dtype=jnp.float32) / n)
    ang = jnp.concatenate([row[:, None] * inv, col[:, None] * inv], axis=-1)
    return jnp.cos(ang), jnp.sin(ang)


def apply_rope(x, cos, sin):
    h = x.shape[-1] // 2
    x1, x2 = x[..., :h], x[..., h:]
    c = cos[None, :, None, :].astype(x.dtype)
    s = sin[None, :, None, :].astype(x.dtype)
    return jnp.concatenate([x1 * c - x2 * s, x2 * c + x1 * s], axis=-1)


def attention(q, k, v):
    B, L, Hk, G, dk = q.shape
    dv = v.shape[-1]
    nb = L // Q_BLOCK
    scale = dk ** -0.5
    qb = jnp.moveaxis(q.reshape(B, nb, Q_BLOCK, Hk, G, dk), 1, 0)

    def one_block(qblk):
        s = jnp.einsum('bqhgd,bkhd->bhgqk', qblk, k).astype(jnp.float32) * scale
        p = jax.nn.softmax(s, axis=-1).astype(v.dtype)
        return jnp.einsum('bhgqk,bkhd->bqhgd', p, v)

    ob = lax.map(one_block, qb)
    return jnp.moveaxis(ob, 0, 1).reshape(B, L, Hk, G, dv)


def split_heads(z, n_heads):
    B, L, _ = z.shape
    return z.reshape(B, L, n_heads, -1)


def group_queries(q):
    B, L, Hq, d = q.shape
    return q.reshape(B, L, N_KV_HEADS, Hq // N_KV_HEADS, d)


def fourier_mix(f):
    B, L, _ = f.shape
    fg = f.reshape(B, L, N_FOURIER_GROUPS, FOURIER_GROUP_W).astype(jnp.float32)
    out = jnp.fft.fftn(fg, axes=(1, 3), norm='ortho').real
    return out.reshape(B, L, FOURIER_W).astype(f.dtype)


def gqa_fourier_mixer(hl, hc, w_in, q_gain, k_gain, w_out, cos, sin, need_ctx_out):
    B, S, _ = hl.shape
    CL = hc.shape[1]
    zl = hl @ w_in
    f_l = zl[..., :FOURIER_W]
    q_l = apply_rope(rms_norm(split_heads(zl[..., FOURIER_W:EVEN_KV_START], N_Q_HEADS), q_gain), cos, sin)
    k_l = apply_rope(rms_norm(split_heads(zl[..., EVEN_KV_START:EVEN_KV_START + KV_W], N_KV_HEADS), k_gain), cos, sin)
    v_l = split_heads(zl[..., EVEN_KV_START + KV_W:], N_KV_HEADS)
    zc_kv = hc @ w_in[:, EVEN_KV_START:]
    k_c = rms_norm(split_heads(zc_kv[..., :KV_W], N_KV_HEADS), k_gain)
    v_c = split_heads(zc_kv[..., KV_W:], N_KV_HEADS)
    o_l = attention(group_queries(q_l),
                    jnp.concatenate([k_l, k_c], axis=1),
                    jnp.concatenate([v_l, v_c], axis=1))
    y_l = jnp.concatenate([fourier_mix(f_l), o_l.reshape(B, S, Q_W)], axis=-1) @ w_out
    if not need_ctx_out:
        return y_l, None
    zc = hc @ w_in[:, :EVEN_KV_START]
    q_c = rms_norm(split_heads(zc[..., FOURIER_W:], N_Q_HEADS), q_gain)
    o_c = attention(group_queries(q_c), k_c, v_c)
    y_c = jnp.concatenate([fourier_mix(zc[..., :FOURIER_W]), o_c.reshape(B, CL, Q_W)], axis=-1) @ w_out
    return y_l, y_c


def conformer_conv(glu_in, conv_w, conv_b, ln_g, ln_b):
    a, g = jnp.split(glu_in, 2, axis=-1)
    u = depthwise_conv(a * jax.nn.sigmoid(g), conv_w, conv_b)
    return jax.nn.silu(layer_norm(u, ln_g, ln_b))


def mla_kv(zkv, kv_norm, w_ukv, cos, sin):
    B, L, _ = zkv.shape
    kv = (rms_norm(zkv[..., :KV_LORA], kv_norm) @ w_ukv).reshape(B, L, MLA_HEADS, MLA_NOPE + MLA_V)
    k_nope, v = kv[..., :MLA_NOPE], kv[..., MLA_NOPE:]
    k_rope = zkv[..., KV_LORA:][:, :, None, :]
    if cos is not None:
        k_rope = apply_rope(k_rope, cos, sin)
    k = jnp.concatenate([k_nope, jnp.broadcast_to(k_rope, (B, L, MLA_HEADS, MLA_ROPE))], axis=-1)
    return k, v


def mla_q(cq, q_norm, w_uq, cos, sin):
    B, L, _ = cq.shape
    q = (rms_norm(cq, q_norm) @ w_uq).reshape(B, L, MLA_HEADS, MLA_NOPE + MLA_ROPE)
    if cos is not None:
        q = jnp.concatenate([q[..., :MLA_NOPE], apply_rope(q[..., MLA_NOPE:], cos, sin)], axis=-1)
    return q[:, :, :, None, :]


def conv_mla_mixer(hl, hc, w_in, conv_w, conv_b, ln_g, ln_b, q_norm, w_uq, kv_norm, w_ukv, w_out,
                   cos, sin, need_ctx_out):
    B, S, _ = hl.shape
    CL = hc.shape[1]
    zl = hl @ w_in
    k_l, v_l = mla_kv(zl[..., ODD_KV_START:], kv_norm, w_ukv, cos, sin)
    k_c, v_c = mla_kv(hc @ w_in[:, ODD_KV_START:], kv_norm, w_ukv, None, None)
    q_l = mla_q(zl[..., 2 * CONV_W:ODD_KV_START], q_norm, w_uq, cos, sin)
    o_l = attention(q_l, jnp.concatenate([k_l, k_c], axis=1), jnp.concatenate([v_l, v_c], axis=1))
    conv_l = conformer_conv(zl[..., :2 * CONV_W], conv_w, conv_b, ln_g, ln_b)
    y_l = jnp.concatenate([conv_l, o_l.reshape(B, S, MLA_HEADS * MLA_V)], axis=-1) @ w_out
    if not need_ctx_out:
        return y_l, None
    zc = hc @ w_in[:, :ODD_KV_START]
    o_c = attention(mla_q(zc[..., 2 * CONV_W:], q_norm, w_uq, None, None), k_c, v_c)
    conv_c = conformer_conv(zc[..., :2 * CONV_W], conv_w, conv_b, ln_g, ln_b)
    y_c = jnp.concatenate([conv_c, o_c.reshape(B, CL, MLA_HEADS * MLA_V)], axis=-1) @ w_out
    return y_l, y_c


def conv_glu_ffn(h, w_up, conv_w, conv_b, w_down):
    g, u = jnp.split(h @ w_up, 2, axis=-1)
    g = depthwise_conv(g, conv_w, conv_b)
    return (jax.nn.silu(g) * u) @ w_down


def setup_inputs(seed: int = 0) -> dict:
    key = jax.random.key(seed)
    keys = list(jax.random.split(key, 32))
    counter = [0]

    def nrm(shape, scale):
        k = keys[counter[0]]
        counter[0] += 1
        return scale * jax.random.normal(k, shape, jnp.float32)

    def gain(shape):
        return 1.0 + nrm(shape, 0.05)

    D = D_MODEL
    return {
        'x': nrm((BATCH, SEQ, D), 1.0),
        'c': nrm((BATCH, D), 1.0),
        'ctx': nrm((BATCH, CTX_LEN, D), 1.0),
        'c_ctx': nrm((D,), 1.0),
        'ada_w': nrm((DEPTH, D, 6 * D), 0.5 * D ** -0.5),
        'ada_b': nrm((DEPTH, 6 * D), 0.02),
        'ev_w_in': nrm((N_EVEN, D, EVEN_IN_W), D ** -0.5),
        'ev_q_gain': gain((N_EVEN, HEAD_DIM)),
        'ev_k_gain': gain((N_EVEN, HEAD_DIM)),
        'ev_w_out': nrm((N_EVEN, FOURIER_W + Q_W, D), (FOURIER_W + Q_W) ** -0.5),
        'od_w_in': nrm((N_ODD, D, ODD_IN_W), D ** -0.5),
        'od_conv_w': nrm((N_ODD, CONV_K, CONV_W), CONV_K ** -0.5),
        'od_conv_b': nrm((N_ODD, CONV_W), 0.02),
        'od_ln_g': gain((N_ODD, CONV_W)),
        'od_ln_b': nrm((N_ODD, CONV_W), 0.02),
        'od_q_norm': gain((N_ODD, Q_LORA)),
        'od_w_uq': nrm((N_ODD, Q_LORA, MLA_HEADS * (MLA_NOPE + MLA_ROPE)), Q_LORA ** -0.5),
        'od_kv_norm': gain((N_ODD, KV_LORA)),
        'od_w_ukv': nrm((N_ODD, KV_LORA, MLA_HEADS * (MLA_NOPE + MLA_V)), KV_LORA ** -0.5),
        'od_w_out': nrm((N_ODD, ODD_OUT_IN, D), ODD_OUT_IN ** -0.5),
        'ffn_w_up': nrm((DEPTH, D, 2 * D_FF), D ** -0.5),
        'ffn_conv_w': nrm((DEPTH, FFN_K, D_FF), FFN_K ** -0.5),
        'ffn_conv_b': nrm((DEPTH, D_FF), 0.02),
        'ffn_w_down': nrm((DEPTH, D_FF, D), D_FF ** -0.5),
        'final_norm': gain((D,)),
    }


def reference(x, c, ctx, c_ctx, ada_w, ada_b, ev_w_in, ev_q_gain, ev_k_gain, ev_w_out,
              od_w_in, od_conv_w, od_conv_b, od_ln_g, od_ln_b, od_q_norm, od_w_uq, od_kv_norm,
              od_w_ukv, od_w_out, ffn_w_up, ffn_conv_w, ffn_conv_b, ffn_w_down, final_norm):
    S = x.shape[1]
    cos_a, sin_a = axial_rope_tables(S, HEAD_DIM)
    cos_m, sin_m = axial_rope_tables(S, MLA_ROPE)
    silu_c = jax.nn.silu(c)
    silu_cc = jax.nn.silu(c_ctx)
    xl, xc = x, ctx
    for i in range(DEPTH):
        need_ctx = i < DEPTH - 1
        mod_l = (silu_c @ ada_w[i] + ada_b[i])[:, None, :]
        mod_c = silu_cc @ ada_w[i] + ada_b[i]
        sh1, sc1, g1, sh2, sc2, g2 = jnp.split(mod_l, 6, axis=-1)
        sh1c, sc1c, g1c, sh2c, sc2c, g2c = jnp.split(mod_c, 6, axis=-1)
        hl = modulate(xl, sh1, sc1)
        hc = modulate(xc, sh1c, sc1c)
        j = i // 2
        if i % 2 == 0:
            yl, yc = gqa_fourier_mixer(hl, hc, ev_w_in[j], ev_q_gain[j], ev_k_gain[j], ev_w_out[j],
                                       cos_a, sin_a, need_ctx)
        else:
            yl, yc = conv_mla_mixer(hl, hc, od_w_in[j], od_conv_w[j], od_conv_b[j], od_ln_g[j], od_ln_b[j],
                                    od_q_norm[j], od_w_uq[j], od_kv_norm[j], od_w_ukv[j], od_w_out[j],
                                    cos_m, sin_m, need_ctx)
        xl = xl + g1 * yl
        xl = xl + g2 * conv_glu_ffn(modulate(xl, sh2, sc2), ffn_w_up[i], ffn_conv_w[i], ffn_conv_b[i], ffn_w_down[i])
        if need_ctx:
            xc = xc + g1c * yc
            xc = xc + g2c * conv_glu_ffn(modulate(xc, sh2c, sc2c), ffn_w_up[i], ffn_conv_w[i], ffn_conv_b[i], ffn_w_down[i])
    return rms_norm(xl, final_norm)
```

```python
import numpy as np
import ml_dtypes
from contextlib import ExitStack
import concourse.bass as bass
import concourse.mybir as mybir
from concourse.bass_utils import run_bass_kernel_spmd

F32 = mybir.dt.float32
BF16 = mybir.dt.bfloat16
AF = mybir.ActivationFunctionType
ALU = mybir.AluOpType
AX = mybir.AxisListType

D = 2048
KC = 16
L = 2048
CL = 256
T = L + CL
NT = T // 128
DEPTH = 4
DFF = 5504
NJ = DFF // 128
EPS = 1e-6
TBS = [(0, 512), (512, 512), (1024, 512), (1536, 512), (2048, 256)]
TB2 = [(i * 256, 256) for i in range(9)]
WS_COLS = 22016
SAME_ENG_SYNC = True


class Buf:
    __slots__ = ('name', 'ap', 'w', 'r', 'sem', 'ndma', 'ws', 'phase')

    def __init__(self, name, ap=None):
        self.name = name
        self.ap = ap
        self.w = None
        self.r = {}
        self.ws = {}
        self.phase = False
        self.sem = None
        self.ndma = 0


class Op:
    __slots__ = ('eng', 'fn', 'deps', 'tok', 'sig', 'dma', 'waits', 'ph')


class Rec:
    def __init__(self):
        self.call = None

    def __getattr__(self, name):
        def f(*a, **k):
            self.call = (name, a, k)
            return self
        return f


class Prog:
    ENGS = ('pe', 'act', 'dve', 'pool', 'sp')

    def __init__(self, nc, stack):
        self.nc = nc
        self.stack = stack
        self.ops = []
        self.barrier = None
        self.esem = {e: [stack.enter_context(nc.semaphore('s_%s%d' % (e, r))) for r in range(3)] for e in self.ENGS}
        self.phase_no = 0
        self.nsem = 0
        self.phase_bufs = []
        self.scr_used = False
        self.h_used = False
        self.dummy = None
        self.sem_pool = []
        self.pool_idx = 0
        self.semcnt = {}

    def sbuf(self, name, cols, dt):
        t = self.stack.enter_context(self.nc.sbuf_tensor(name, [128, cols], dt))
        return Buf(name, t[:])

    def psum(self, name, cols, dt):
        t = self.stack.enter_context(self.nc.psum_tensor(name, [128, cols], dt))
        return Buf(name, t[:])

    def region(self, name, ap=None, phase=True):
        b = Buf(name, ap)
        b.w = self.barrier
        if phase:
            b.phase = True
            self.phase_bufs.append(b)
        return b

    def end_phase(self, dummy=None):
        dummy = dummy or self.dummy
        bufs = self.phase_bufs
        self.phase_bufs = []
        self.scr_used = False
        self.h_used = False
        i = self.add('dve', lambda e: e.memset(dummy.ap[:, 0:1], 0.0), writes=[dummy] + bufs)
        self.barrier = i
        self.phase_no += 1
        self.pool_idx = (self.phase_no % 4) * 12

    def add(self, eng, fn, reads=(), writes=(), dma=None, accum=()):
        i = len(self.ops)
        deps = set()
        for b in reads:
            if b.w is not None:
                deps.add(b.w)
            deps.update(b.ws.values())
        for b in writes:
            if b.w is not None:
                deps.add(b.w)
            deps.update(b.r.values())
            deps.update(b.ws.values())
        for b in accum:
            if b.w is not None:
                deps.add(b.w)
            deps.update(b.r.values())
        op = Op()
        op.eng = eng
        rec = Rec()
        fn(rec)
        op.fn = rec.call
        op.dma = dma
        op.ph = self.phase_no % 3
        op.sig = False
        op.tok = None
        op.waits = None
        if dma is not None:
            if dma.sem is None:
                if dma.phase:
                    while self.pool_idx >= len(self.sem_pool):
                        self.sem_pool.append(self.stack.enter_context(self.nc.semaphore('q%d' % len(self.sem_pool))))
                    assert self.pool_idx < (self.phase_no % 4) * 12 + 12
                    dma.sem = self.sem_pool[self.pool_idx]
                    self.pool_idx += 1
                else:
                    dma.sem = self.stack.enter_context(self.nc.semaphore('d%d' % self.nsem))
                    self.nsem += 1
            k_ = id(dma.sem)
            self.semcnt[k_] = self.semcnt.get(k_, 0) + 1
            op.tok = (dma.sem, 16 * self.semcnt[k_])
            key = ('dma', id(dma))
        else:
            key = eng
        for b in writes:
            b.w = i
            b.r = {}
            b.ws = {}
        for b in accum:
            b.ws[key] = i
        for b in reads:
            if b.w != i:
                b.r[key] = i
        deps.discard(i)
        op.deps = deps
        self.ops.append(op)
        return i

    def dma(self, eng, out_ap, in_ap, tile, reads=(), writes=(), accum=(), **kw):
        return self.add(eng, lambda e: e.dma_start(out=out_ap, in_=in_ap, **kw),
                        reads=reads, writes=writes, dma=tile, accum=accum)

    def finalize(self):
        ops = self.ops
        for op in ops:
            for d in op.deps:
                dop = ops[d]
                if dop.dma is None:
                    if dop.eng == op.eng and op.dma is None and (op.eng == 'pe' or not SAME_ENG_SYNC):
                        continue
                    dop.sig = True
        cnt = {(e, r): 0 for e in self.ENGS for r in range(3)}
        for op in ops:
            if op.dma is None and op.sig:
                cnt[(op.eng, op.ph)] += 1
                op.tok = (self.esem[op.eng][op.ph], cnt[(op.eng, op.ph)])
        self.max_cnt = max(cnt.values())
        known = {e: {} for e in self.ENGS}
        for op in ops:
            need = {}
            for d in op.deps:
                dop = ops[d]
                if dop.tok is None:
                    continue
                if dop.dma is None and dop.eng == op.eng and op.dma is None and (op.eng == 'pe' or not SAME_ENG_SYNC):
                    continue
                sem, val = dop.tok
                k = id(sem)
                if known[op.eng].get(k, 0) >= val:
                    continue
                if k not in need or need[k][1] < val:
                    need[k] = (sem, val)
            for k, (sem, val) in need.items():
                known[op.eng][k] = val
            op.waits = list(need.values())

    def emit(self):
        self.finalize()
        mx = {}
        for op in self.ops:
            if op.tok is not None:
                mx[op.tok[0].name] = max(mx.get(op.tok[0].name, 0), op.tok[1])
        self.max_tok = mx
        print("[kernel] ops=%d max_sem_value=%d nsems=%d" % (len(self.ops), max(mx.values()), len(mx)))
        nc = self.nc
        ops = self.ops

        def run(eng, e):
            for op in ops:
                if op.eng != eng:
                    continue
                for sem, val in op.waits:
                    e.wait_ge(sem, val)
                name, a, k = op.fn
                ins = getattr(e, name)(*a, **k)
                if op.dma is not None:
                    ins.then_inc(op.tok[0], 16)
                elif op.sig:
                    ins.then_inc(op.tok[0], 1)

        with nc.Block() as block:
            @block.tensor
            def _(e):
                run('pe', e)

            @block.scalar
            def _(e):
                run('act', e)

            @block.vector
            def _(e):
                run('dve', e)

            @block.gpsimd
            def _(e):
                run('pool', e)

            @block.sync
            def _(e):
                run('sp', e)


class Rot:
    def __init__(self, items):
        self.items = items
        self.i = 0

    def next(self):
        b = self.items[self.i % len(self.items)]
        self.i += 1
        return b


def build_program(NL=DEPTH, final=True, stop=None):
    nc = bass.Bass("TRN2", target_bir_lowering=False)
    dt_in = {}

    def ein(name, shape, dt=F32):
        dt_in[name] = dt
        return nc.dram_tensor(name, list(shape), dt, kind="ExternalInput").ap()

    xT_in = ein("xT_in", [KC, 128, T])
    c2 = ein("c2", [128, KC * 2])
    ada_w = ein("ada_w", [DEPTH, D, 6 * D])
    ada_bc = ein("ada_bc", [128, DEPTH * 96])
    ev_w_in = ein("ev_w_in", [2, D, 3072])
    ev_w_out = ein("ev_w_out", [2, D, D])
    gains = ein("gains", [128, 2 * 2 * 128])
    od_w_in = ein("od_w_in", [2, D, 2112])
    od_w_uq = ein("od_w_uq", [2, 512, 2304])
    od_w_ukv = ein("od_w_ukv", [2, 512, 3072])
    od_w_out = ein("od_w_out", [2, D, D])
    od_cw = ein("od_cw", [128, 2 * 4 * 31])
    od_vec = ein("od_vec", [128, 2 * 5 * 4])
    ffn_w_up = ein("ffn_w_up", [DEPTH, D, 2 * DFF])
    ffn_w_down = ein("ffn_w_down", [DEPTH, DFF, D])
    ffn_cw = ein("ffn_cw", [128, DEPTH * NJ * 3])
    ffn_cb = ein("ffn_cb", [128, DEPTH * NJ])
    fin_g = ein("fin_g", [128, KC])
    ident_bf_d = ein("ident_bf", [128, 128], BF16)
    ident_f_d = ein("ident_f", [128, 128])
    dft128_d = ein("dft128", [128, 256], BF16)
    dftL_d = ein("dftL", [2, L, L], BF16)
    dftC_d = ein("dftC", [2, CL, CL], BF16)
    ropeA_d = ein("ropeA", [16, 128, 128])
    ropeM_d = ein("ropeM", [2, 128, T])
    out_d = nc.dram_tensor("yT", [KC, 128, L], F32, kind="ExternalOutput").ap()

    xT_d = nc.dram_tensor("xT_s", [KC, 128, T], F32).ap()
    aT_d = nc.dram_tensor("aT_s", [KC, 128, T], BF16).ap()
    qT_d = nc.dram_tensor("qT_s", [12, 128, T], BF16).ap()
    kT_d = nc.dram_tensor("kT_s", [12, 128, T], BF16).ap()
    v_d = nc.dram_tensor("v_s", [NT, 128, 1536], BF16).ap()
    fT_d = nc.dram_tensor("fT_s", [4, 128, T], BF16).ap()
    ffa_d = nc.dram_tensor("ffa_s", [NJ, 128, T], BF16).ap()
    cqn_d = nc.dram_tensor("cqn_s", [8, 128, T], BF16).ap()
    gl_d = nc.dram_tensor("gl_s", [4, 128, T], BF16).ap()
    kr_d = nc.dram_tensor("kr_s", [128, T], BF16).ap()
    qr_d = nc.dram_tensor("qr_s", [6, 128, T], BF16).ap()

    with ExitStack() as stack:
        P = Prog(nc, stack)

        H = P.sbuf("H", 36864, BF16)
        WS = [P.sbuf("WS0", WS_COLS, BF16), P.sbuf("WS1", WS_COLS, BF16)]
        SCR = P.sbuf("SCR", 13312, BF16)
        wrot = Rot(WS)
        modc = P.sbuf("modc", DEPTH * 96 * 2, F32)
        modv = modc.ap.rearrange("p (i f s) -> p i f s", i=DEPTH, f=96)
        c2s = P.sbuf("c2s", KC * 2, F32)
        adab = P.sbuf("adab", DEPTH * 96, F32)
        gains_s = P.sbuf("gains_s", 512, F32)
        odcw_s = P.sbuf("odcw_s", 2 * 4 * 31, F32)
        odvec_s = P.sbuf("odvec_s", 40, F32)
        ffcw_s = P.sbuf("ffcw_s", DEPTH * NJ * 3, F32)
        ffcb_s = P.sbuf("ffcb_s", DEPTH * NJ, F32)
        fing_s = P.sbuf("fing_s", KC, F32)
        ident_bf = P.sbuf("ident_bf_s", 128, BF16)
        ident_f = P.sbuf("ident_f_s", 128, F32)
        dft128 = P.sbuf("dft128_s", 256, BF16)
        onesD = P.sbuf("onesD", 128, BF16)
        ones1 = P.sbuf("ones1", 128, BF16)
        ones512 = P.sbuf("ones512", 128, BF16)
        ones512f = P.sbuf("ones512f", 128, F32)
        epsb = P.sbuf("epsb", 4, F32)
        dummy = P.sbuf("dummy_t", 4, F32)
        consts = Buf("consts")
        P.dummy = dummy

        PS = [P.psum("ps%d" % i, 512, F32) for i in range(8)]

        for dst, src in ((c2s, c2), (adab, ada_bc), (gains_s, gains), (odcw_s, od_cw), (odvec_s, od_vec),
                         (ffcw_s, ffn_cw), (ffcb_s, ffn_cb), (fing_s, fin_g), (ident_bf, ident_bf_d),
                         (ident_f, ident_f_d), (dft128, dft128_d)):
            P.dma('sp', dst.ap, src, consts, writes=[consts, dst])
        P.add('dve', lambda e: e.memset(onesD.ap, 1.0 / D), writes=[onesD])
        P.add('dve', lambda e: e.memset(ones1.ap, 1.0), writes=[ones1])
        P.add('dve', lambda e: e.memset(ones512.ap, 1.0 / 512), writes=[ones512])
        P.add('dve', lambda e: e.memset(ones512f.ap, 1.0 / 512), writes=[ones512f])
        P.add('dve', lambda e: e.memset(epsb.ap, EPS), writes=[epsb])

        def scr_carve(specs):
            if P.scr_used:
                P.end_phase()
            P.scr_used = True
            out = {}
            off = 0
            for name, cols, dt in specs:
                n16 = cols * (2 if dt == F32 else 1)
                ap = SCR.ap[:, off:off + n16]
                if dt == F32:
                    ap = ap.bitcast(F32)
                out[name] = P.region(name, ap)
                off += n16
            assert off <= 13312, off
            return out

        def h_carve(specs):
            if P.h_used:
                P.end_phase()
            P.h_used = True
            out = {}
            off = 0
            for name, cols, dt in specs:
                n16 = cols * (2 if dt == F32 else 1)
                ap = H.ap[:, off:off + n16]
                if dt == F32:
                    ap = ap.bitcast(F32)
                out[name] = P.region(name, ap)
                off += n16
            assert off <= 36864, off
            return out

        def w_load(parts, f32=False):
            slab = wrot.next()
            for off, n, src, shape in parts:
                base = slab.ap.bitcast(F32) if f32 else slab.ap
                dst = base[:, off:off + n]
                if shape is not None:
                    dst = dst.rearrange(shape[0], **shape[1])
                P.dma('sp' if f32 else 'pool', dst, src, slab, writes=[slab])
            return slab

        sc = scr_carve([("silu", KC * 2, F32)])
        silu = sc["silu"]
        P.add('act', lambda e: e.activation(out=silu.ap, in_=c2s.ap, func=AF.Silu), reads=[c2s], writes=[silu])
        siluv = silu.ap.rearrange("p (k s) -> p k s", k=KC)
        adabv = adab.ap.rearrange("p (i f) -> p i f", i=DEPTH)
        for i in range(NL):
            for nb in range(24):
                src = ada_w[i, :, nb * 512:(nb + 1) * 512].rearrange("(k p) n -> p k n", p=128)
                slab = wrot.next()
                sv = slab.ap.bitcast(F32)[:, 0:8192].rearrange("p (k n) -> p k n", k=KC)
                for q in range(4):
                    P.dma('sp', sv[:, q * 4:(q + 1) * 4, :], src[:, q * 4:(q + 1) * 4, :], slab, writes=[slab])
                pt = PS[nb % 2]
                for q in range(4):
                    for k in range(KC):
                        P.add('pe', lambda e, pt=pt, k=k, q=q, sv=sv: e.matmul(pt.ap[:, q * 2:q * 2 + 2], sv[:, k, q * 128:(q + 1) * 128],
                                                                          siluv[:, k, :], start=(k == 0), stop=(k == KC - 1)),
                              reads=[silu, slab], writes=[pt])
                P.add('dve', lambda e, pt=pt, i=i, nb=nb: e.tensor_tensor(
                    out=modv[:, i, nb * 4:nb * 4 + 4, :],
                    in0=pt.ap[:, 0:8].rearrange("p (f s) -> p f s", f=4),
                    in1=adabv[:, i, nb * 4:nb * 4 + 4].unsqueeze(2).broadcast_to([128, 4, 2]), op=ALU.add),
                    reads=[pt, adab, consts], writes=[modc])
            for part in (1, 4):
                P.add('dve', lambda e, i=i, part=part: e.tensor_scalar_add(
                    out=modv[:, i, part * 16:(part + 1) * 16, :], in0=modv[:, i, part * 16:(part + 1) * 16, :],
                    scalar1=1.0), reads=[modc], writes=[modc])
        P.end_phase(dummy)

        xreg = [[Buf("x%d_%d" % (c, b)) for b in range(9)] for c in range(KC)]

        def xregs(c, t0, n):
            return [xreg[c][b] for b in range(t0 // 256, (t0 + n) // 256)]

        aTreg = [[Buf("a%d_%d" % (c, b)) for b in range(5)] for c in range(KC)]
        state = {"xsrc": xT_in}

        hT = H.ap.rearrange("p (k t) -> p k t", k=KC)

        def modulate(i, part_sh, part_sc, hreg):
            xsrc = state["xsrc"]
            sc = scr_carve([("x0", 512, F32), ("x1", 512, F32), ("x2", 512, F32), ("x3", 512, F32),
                            ("sq0", 512, BF16), ("sq1", 512, BF16),
                            ("rs0", 512, F32), ("rs1", 512, F32), ("t0", 512, F32), ("t1", 512, F32)])
            xs = Rot([sc["x0"], sc["x1"], sc["x2"], sc["x3"]])
            sqs = Rot([sc["sq0"], sc["sq1"]])
            rss = Rot([sc["rs0"], sc["rs1"]])
            ts = Rot([sc["t0"], sc["t1"]])
            for tb, (t0, n) in enumerate(TBS):
                s = 1 if tb == 4 else 0
                ps = PS[tb % 2]
                for c in range(KC):
                    x = xs.next()
                    P.dma('sp', x.ap[:, :n], xsrc[c, :, t0:t0 + n], x, reads=xregs(c, t0, n), writes=[x])
                    sq = sqs.next()
                    P.add('act', lambda e, x=x, sq=sq: e.activation(out=sq.ap[:, :n], in_=x.ap[:, :n], func=AF.Square),
                          reads=[x], writes=[sq])
                    P.add('pe', lambda e, ps=ps, sq=sq, c=c: e.matmul(ps.ap[:, :n], onesD.ap, sq.ap[:, :n],
                                                                     start=(c == 0), stop=(c == KC - 1)),
                          reads=[sq, onesD], writes=[ps])
                rs = rss.next()
                P.add('act', lambda e, ps=ps, rs=rs: e.activation(out=rs.ap[:, :n], in_=ps.ap[:, :n], func=AF.Sqrt, bias=epsb.ap[:, 0:1], scale=1.0), reads=[ps, epsb], writes=[rs])
                P.add('dve', lambda e, ps=ps, rs=rs: e.reciprocal(out=rs.ap[:, :n], in_=rs.ap[:, :n]), reads=[rs], writes=[rs])
                for c in range(KC):
                    x = xs.next()
                    P.dma('sp', x.ap[:, :n], xsrc[c, :, t0:t0 + n], x, reads=xregs(c, t0, n), writes=[x])
                    t = ts.next()
                    P.add('dve', lambda e, x=x, t=t, rs=rs, c=c, s=s: e.scalar_tensor_tensor(
                        out=t.ap[:, :n], in0=x.ap[:, :n], scalar=modv[:, i, part_sc * 16 + c, s:s + 1],
                        in1=rs.ap[:, :n], op0=ALU.mult, op1=ALU.mult), reads=[x, rs, modc], writes=[t])
                    P.add('act', lambda e, t=t, c=c, s=s: e.activation(
                        out=hT[:, c, t0:t0 + n], in_=t.ap[:, :n], func=AF.Identity,
                        bias=modv[:, i, part_sh * 16 + c, s:s + 1], scale=1.0),
                        reads=[t, modc], writes=[hreg[c][tb]])

        def new_hreg():
            if P.h_used:
                P.end_phase()
            P.h_used = True
            return [[P.region("h%d_%d" % (c, tb)) for tb in range(5)] for c in range(KC)]

        def wslab_cols(wsrc, c0, ncols, k_chunks=KC, dst_off=0, slab=None):
            pass

        def load_cols(slab, off, wsrc, c0, ncols, kc=KC, step=4):
            dstv = slab.ap[:, off:off + kc * ncols].rearrange("p (k n) -> p k n", k=kc)
            srcv = wsrc[:, c0:c0 + ncols].rearrange("(k p) n -> p k n", p=128)
            for q in range(0, kc, step):
                P.dma('pool', dstv[:, q:q + step, :], srcv[:, q:q + step, :], slab, writes=[slab])
            return dstv

        def even_inproj(i, j, hreg):
            w = ev_w_in[j]
            sc = scr_carve([("st0", 512, BF16), ("st1", 512, BF16), ("st2", 512, BF16),
                            ("sq", 512, F32), ("zn", 512, F32), ("ss", 8, F32), ("rstd", 8, F32),
                            ("ta", 256, F32), ("tb", 256, F32), ("zr0", 512, BF16), ("zr1", 512, BF16),
                            ("cs0", 128, F32), ("cs1", 128, F32)])
            sts = Rot([sc["st0"], sc["st1"], sc["st2"]])
            zrs = Rot([sc["zr0"], sc["zr1"]])
            css = Rot([sc["cs0"], sc["cs1"]])
            sq, zn, ss, rstd, ta, tbb = sc["sq"], sc["zn"], sc["ss"], sc["rstd"], sc["ta"], sc["tb"]
            gv = gains_s.ap.rearrange("p (j q d) -> p j q d", j=2, q=2)
            psr = Rot([PS[0], PS[1], PS[2]])
            ptr = Rot([PS[3], PS[4]])
            allh = [hreg[c][tb] for c in range(KC) for tb in range(5)]
            slab = wrot.next()
            wv = load_cols(slab, 0, w, 0, 512)
            for tb, (t0, n) in enumerate(TBS):
                for m in range(4):
                    ps = psr.next()
                    for k in range(KC):
                        P.add('pe', lambda e, ps=ps, k=k, m=m, wv=wv: e.matmul(
                            ps.ap[:, :n], wv[:, k, m * 128:(m + 1) * 128], hT[:, k, t0:t0 + n],
                            start=(k == 0), stop=(k == KC - 1)), reads=[slab, hreg[k][tb]], writes=[ps])
                    st = sts.next()
                    P.add('act', lambda e, ps=ps, st=st: e.copy(out=st.ap[:, :n], in_=ps.ap[:, :n]),
                          reads=[ps], writes=[st])
                    P.dma('sp', fT_d[m, :, t0:t0 + n], st.ap[:, :n], st, reads=[st], accum=[fTreg])
            for blk in range(1, 5):
                slab = wrot.next()
                wv = load_cols(slab, 0, w, blk * 512, 512)
                isk = (blk == 4)
                for tt in range(NT):
                    ps = psr.next()
                    tbi = min(tt // 4, 4)
                    for k in range(KC):
                        P.add('pe', lambda e, ps=ps, k=k, wv=wv, tt=tt: e.matmul(
                            ps.ap, hT[:, k, tt * 128:(tt + 1) * 128], wv[:, k, :],
                            start=(k == 0), stop=(k == KC - 1)), reads=[slab, hreg[k][tbi]], writes=[ps])
                    P.add('act', lambda e, ps=ps: e.activation(out=sq.ap, in_=ps.ap, func=AF.Square),
                          reads=[ps], writes=[sq])
                    P.add('dve', lambda e: e.reduce_sum(out=ss.ap[:, 0:4], in_=sq.ap.rearrange("p (h d) -> p h d", h=4),
                                                        axis=AX.X), reads=[sq], writes=[ss])
                    P.add('act', lambda e: e.activation(out=rstd.ap[:, 0:4], in_=ss.ap[:, 0:4], func=AF.Sqrt, bias=epsb.ap[:, 0:1], scale=1.0 / 128),
                          reads=[ss, epsb], writes=[rstd])
                    P.add('dve', lambda e: e.reciprocal(out=rstd.ap[:, 0:4], in_=rstd.ap[:, 0:4]), reads=[rstd], writes=[rstd])
                    zn3 = zn.ap.rearrange("p (h d) -> p h d", h=4)
                    P.add('dve', lambda e, ps=ps, zn3=zn3: e.tensor_tensor(
                        out=zn3, in0=ps.ap.rearrange("p (h d) -> p h d", h=4),
                        in1=rstd.ap[:, 0:4].unsqueeze(2).broadcast_to([128, 4, 128]), op=ALU.mult),
                        reads=[ps, rstd], writes=[zn])
                    zr = zrs.next()
                    zr3 = zr.ap.rearrange("p (h d) -> p h d", h=4)
                    gsel = gv[:, j, 1 if isk else 0, :].unsqueeze(1).broadcast_to([128, 4, 128])
                    if tt < 16:
                        P.add('dve', lambda e, zn3=zn3, gsel=gsel: e.tensor_tensor(out=zn3, in0=zn3, in1=gsel, op=ALU.mult),
                              reads=[zn, gains_s, consts], writes=[zn])
                        cs = css.next()
                        P.dma('sp', cs.ap, ropeA_d[tt], cs, writes=[cs])
                        cosb = cs.ap[:, 0:64].unsqueeze(1).broadcast_to([128, 4, 64])
                        sinb = cs.ap[:, 64:128].unsqueeze(1).broadcast_to([128, 4, 64])
                        ta3 = ta.ap.rearrange("p (h d) -> p h d", h=4)
                        tb3 = tbb.ap.rearrange("p (h d) -> p h d", h=4)
                        x1 = zn3[:, :, 0:64]
                        x2 = zn3[:, :, 64:128]
                        P.add('dve', lambda e, x1=x1, cosb=cosb, ta3=ta3: e.tensor_tensor(out=ta3, in0=x1, in1=cosb, op=ALU.mult),
                              reads=[zn, cs], writes=[ta])
                        P.add('dve', lambda e, x2=x2, sinb=sinb, tb3=tb3: e.tensor_tensor(out=tb3, in0=x2, in1=sinb, op=ALU.mult),
                              reads=[zn, cs], writes=[tbb])
                        P.add('dve', lambda e, zr3=zr3, ta3=ta3, tb3=tb3: e.tensor_tensor(out=zr3[:, :, 0:64], in0=ta3, in1=tb3, op=ALU.subtract),
                              reads=[ta, tbb], writes=[zr])
                        P.add('dve', lambda e, x2=x2, cosb=cosb, ta3=ta3: e.tensor_tensor(out=ta3, in0=x2, in1=cosb, op=ALU.mult),
                              reads=[zn, cs], writes=[ta])
                        P.add('dve', lambda e, x1=x1, sinb=sinb, tb3=tb3: e.tensor_tensor(out=tb3, in0=x1, in1=sinb, op=ALU.mult),
                              reads=[zn, cs], writes=[tbb])
                        P.add('dve', lambda e, zr3=zr3, ta3=ta3, tb3=tb3: e.tensor_tensor(out=zr3[:, :, 64:128], in0=ta3, in1=tb3, op=ALU.add),
                              reads=[ta, tbb], writes=[zr])
                    else:
                        P.add('dve', lambda e, zn3=zn3, gsel=gsel, zr3=zr3: e.tensor_tensor(out=zr3, in0=zn3, in1=gsel, op=ALU.mult),
                              reads=[zn, gains_s, consts], writes=[zr])
                    pt = ptr.next()
                    ptb = pt.ap.bitcast(BF16)
                    for h in range(4):
                        P.add('pe', lambda e, ptb=ptb, zr=zr, h=h: e.transpose(ptb[:, h * 128:(h + 1) * 128],
                                                                             zr.ap[:, h * 128:(h + 1) * 128], ident_bf.ap),
                              reads=[zr, ident_bf, consts], writes=[pt])
                    st = sts.next()
                    P.add('act', lambda e, ptb=ptb, st=st: e.copy(out=st.ap, in_=ptb[:, 0:512]), reads=[pt], writes=[st])
                    dstT = kT_d if isk else qT_d
                    h0 = 0 if isk else (blk - 1) * 4
                    P.dma('sp', dstT[h0:h0 + 4, :, tt * 128:(tt + 1) * 128].rearrange("h p t -> p h t"),
                          st.ap.rearrange("p (h t) -> p h t", h=4), st, reads=[st], accum=[qkreg])
            slab = wrot.next()
            wv = load_cols(slab, 0, w, 2560, 512)
            for tt in range(NT):
                ps = psr.next()
                tbi = min(tt // 4, 4)
                for k in range(KC):
                    P.add('pe', lambda e, ps=ps, k=k, wv=wv, tt=tt: e.matmul(
                        ps.ap, hT[:, k, tt * 128:(tt + 1) * 128], wv[:, k, :],
                        start=(k == 0), stop=(k == KC - 1)), reads=[slab, hreg[k][tbi]], writes=[ps])
                st = sts.next()
                P.add('act', lambda e, ps=ps, st=st: e.copy(out=st.ap, in_=ps.ap), reads=[ps], writes=[st])
                P.dma('sp', v_d[tt, :, 0:512], st.ap, st, reads=[st], accum=[qkreg])

        fTreg = Buf("fTreg")
        qkreg = Buf("qkreg")
        aTall = Buf("aTall")

        def fourier(i):
            hc = h_carve([("f", 4 * T, BF16), ("A", 4 * NT * 256, BF16)])
            fb, Ab = hc["f"], hc["A"]
            fv = fb.ap.rearrange("p (g t) -> p g t", g=4)
            Av = Ab.ap.rearrange("p (g n c) -> p g n c", g=4, n=NT)
            sc = scr_carve([("st0", 512, BF16), ("st1", 512, BF16)])
            sts = Rot([sc["st0"], sc["st1"]])
            for g in range(4):
                P.dma('sp', fv[:, g, :], fT_d[g], fb, reads=[fTreg], writes=[fb])
            psr = Rot([PS[0], PS[1], PS[2]])
            for g in range(4):
                for n in range(NT):
                    ps = psr.next()
                    P.add('pe', lambda e, ps=ps, g=g, n=n: e.matmul(ps.ap[:, 0:256], fv[:, g, n * 128:(n + 1) * 128], dft128.ap,
                                                                   start=True, stop=True),
                          reads=[fb, dft128, consts], writes=[ps])
                    P.add('act', lambda e, ps=ps, g=g, n=n: e.copy(out=Av[:, g, n, :], in_=ps.ap[:, 0:256]),
                          reads=[ps], writes=[Ab])
            for kb in range(5):
                slab = wrot.next()
                if kb < 4:
                    nn, ncol, n0 = 16, 512, 0
                    for cs_ in range(2):
                        dst = slab.ap[:, cs_ * 8192:(cs_ + 1) * 8192].rearrange("p (n k) -> p n k", n=16)
                        src = dftL_d[cs_, :, kb * 512:(kb + 1) * 512].rearrange("(n p) k -> p n k", p=128)
                        for q in range(0, 16, 4):
                            P.dma('sp', dst[:, q:q + 4, :], src[:, q:q + 4, :], slab, writes=[slab])
                else:
                    nn, ncol, n0 = 2, 256, 16
                    for cs_ in range(2):
                        dst = slab.ap[:, cs_ * 8192:cs_ * 8192 + 512].rearrange("p (n k) -> p n k", n=2)
                        src = dftC_d[cs_].rearrange("(n p) k -> p n k", p=128)
                        P.dma('sp', dst, src, slab, writes=[slab])
                t0 = kb * 512
                for g in range(4):
                    ps = psr.next()
                    cnt = 0
                    for cs_ in range(2):
                        sv = slab.ap[:, cs_ * 8192:cs_ * 8192 + nn * ncol].rearrange("p (n k) -> p n k", n=nn)
                        for n in range(nn):
                            P.add('pe', lambda e, ps=ps, g=g, n=n, cs_=cs_, sv=sv, cnt=cnt, nn=nn, ncol=ncol, n0=n0: e.matmul(
                                ps.ap[:, :ncol], Av[:, g, n0 + n, cs_ * 128:(cs_ + 1) * 128], sv[:, n, :],
                                start=(cnt == 0), stop=(cnt == 2 * nn - 1)), reads=[Ab, slab], writes=[ps])
                            cnt += 1
                    st = sts.next()
                    P.add('act', lambda e, ps=ps, st=st, ncol=ncol: e.copy(out=st.ap[:, :ncol], in_=ps.ap[:, :ncol]),
                          reads=[ps], writes=[st])
                    P.dma('sp', aT_d[g, :, t0:t0 + ncol], st.ap[:, :ncol], st, reads=[st], writes=[aTreg[g][kb]])

        def attention(nheads, kparts_of, qparts_of, v_of, kvkey_of, scale, out_chunk0):
            hc = h_carve([("k0", 2 * T, BF16), ("k1", 2 * T, BF16), ("v0", NT * 128, BF16), ("v1", NT * 128, BF16)])
            kbufs = Rot([hc["k0"], hc["k1"]])
            vbufs = Rot([hc["v0"], hc["v1"]])
            sc = scr_carve([("q0", 1024, BF16), ("q1", 1024, BF16), ("q2", 1024, BF16),
                            ("p0", 512, BF16), ("p1", 512, BF16), ("p2", 512, BF16),
                            ("o0", 512, BF16), ("o1", 512, BF16), ("r0", 512, F32), ("r1", 512, F32)])
            qbufs = Rot([sc["q0"], sc["q1"], sc["q2"]])
            pbufs = Rot([sc["p0"], sc["p1"], sc["p2"]])
            obufs = Rot([sc["o0"], sc["o1"]])
            rbufs = Rot([sc["r0"], sc["r1"]])
            pss = Rot([PS[0], PS[1], PS[2]])
            pso = Rot([PS[3], PS[4]])
            psm = Rot([PS[5], PS[6]])
            blocks = []
            cur_kv = None
            for h in range(nheads):
                for tb, (t0, n) in enumerate(TBS):
                    blocks.append((h, tb, t0, n))
            st = {"kb": None, "vb": None, "cur": None}
            loaded = {}

            def do_loads(bi):
                h, tb, t0, n = blocks[bi]
                key = kvkey_of(h)
                if key != st["cur"]:
                    st["cur"] = key
                    kb = kbufs.next()
                    vb = vbufs.next()
                    st["kb"], st["vb"] = kb, vb
                    for pi, (src, rows, base) in enumerate(kparts_of(h)):
                        P.dma('sp', kb.ap[base:base + rows, pi * T:(pi + 1) * T], src, kb, reads=[qkreg], writes=[kb])
                    P.dma('sp', vb.ap.rearrange("p (t d) -> p t d", t=NT), v_of(h).rearrange("t p d -> p t d"), vb,
                          reads=[qkreg], writes=[vb])
                qb = qbufs.next()
                for pi, (src, rows, base) in enumerate(qparts_of(h)):
                    P.dma('sp', qb.ap[base:base + rows, pi * 512:pi * 512 + n], src[:, t0:t0 + n], qb,
                          reads=[qkreg], writes=[qb])
                loaded[bi] = (st["kb"], st["vb"], qb)

            def block_steps(bi):
                h, tb, t0, n = blocks[bi]
                kb, vb, qb = loaded[bi]
                kts = list(range(NT)) if tb < 4 else [16, 17]
                po = pso.next()
                pm = psm.next()
                return [(h, tb, t0, n, kt, ki == 0, ki == len(kts) - 1, kb, vb, qb, po, pm) for ki, kt in enumerate(kts)]
            nparts = len(kparts_of(0))
            pend = None

            def emit_qk(s):
                h, tb, t0, n, kt, first, last, kb, vb, qb, po, pm = s
                ps = pss.next()
                parts = kparts_of(h)
                for pi, (src, rows, base) in enumerate(parts):
                    P.add('pe', lambda e, ps=ps, kb=kb, qb=qb, pi=pi, rows=rows, base=base, kt=kt, n=n: e.matmul(
                        ps.ap[:, :n], kb.ap[base:base + rows, pi * T + kt * 128:pi * T + (kt + 1) * 128],
                        qb.ap[base:base + rows, pi * 512:pi * 512 + n],
                        start=(pi == 0), stop=(pi == len(parts) - 1)), reads=[kb, qb], writes=[ps])
                pb = pbufs.next()
                P.add('act', lambda e, ps=ps, pb=pb, n=n: e.activation(out=pb.ap[:, :n], in_=ps.ap[:, :n], func=AF.Exp, scale=scale),
                      reads=[ps], writes=[pb])
                return pb

            def emit_pv(s, pb):
                h, tb, t0, n, kt, first, last, kb, vb, qb, po, pm = s
                P.add('pe', lambda e, po=po, vb=vb, pb=pb, kt=kt, n=n, first=first, last=last: e.matmul(
                    po.ap[:, :n], vb.ap[:, kt * 128:(kt + 1) * 128], pb.ap[:, :n], start=first, stop=last),
                    reads=[vb, pb], writes=[po])
                P.add('pe', lambda e, pm=pm, pb=pb, n=n, first=first, last=last: e.matmul(
                    pm.ap[:, :n], ones1.ap, pb.ap[:, :n], start=first, stop=last), reads=[ones1, pb], writes=[pm])
                if last:
                    rb = rbufs.next()
                    ob = obufs.next()
                    P.add('dve', lambda e, pm=pm, rb=rb, n=n: e.reciprocal(out=rb.ap[:, :n], in_=pm.ap[:, :n]),
                          reads=[pm], writes=[rb])
                    P.add('dve', lambda e, po=po, rb=rb, ob=ob, n=n: e.tensor_tensor(out=ob.ap[:, :n], in0=po.ap[:, :n],
                                                                                    in1=rb.ap[:, :n], op=ALU.mult),
                          reads=[po, rb], writes=[ob])
                    P.dma('sp', aT_d[out_chunk0 + h, :, t0:t0 + n], ob.ap[:, :n], ob, reads=[ob],
                          writes=[aTreg[out_chunk0 + h][tb]])

            do_loads(0)
            for bi in range(len(blocks)):
                if bi + 1 < len(blocks):
                    do_loads(bi + 1)
                for s in block_steps(bi):
                    pb = emit_qk(s)
                    if pend is not None:
                        emit_pv(*pend)
                    pend = (s, pb)
            emit_pv(*pend)

        def outproj(i, w, gpart):
            xsrc = state["xsrc"]
            hc = h_carve([("a0", KC * 512, BF16), ("a1", KC * 512, BF16)])
            abufs = Rot([hc["a0"], hc["a1"]])
            sc = scr_carve([("x0", 512, F32), ("x1", 512, F32), ("x2", 512, F32),
                            ("o0", 512, F32), ("o1", 512, F32), ("o2", 512, F32)])
            xs = Rot([sc["x0"], sc["x1"], sc["x2"]])
            os_ = Rot([sc["o0"], sc["o1"], sc["o2"]])
            psr = Rot([PS[0], PS[1], PS[2], PS[3]])
            for mb in range(4):
                slab = wrot.next()
                wv = load_cols(slab, 0, w, mb * 512, 512)
                for tb, (t0, n) in enumerate(TBS):
                    s = 1 if tb == 4 else 0
                    ab = abufs.next()
                    abv = ab.ap.rearrange("p (k t) -> p k t", k=KC)
                    P.dma('sp', abv[:, :, :n], aT_d[:, :, t0:t0 + n].rearrange("k p t -> p k t"), ab,
                          reads=[aTreg[c][tb] for c in range(KC)], writes=[ab])
                    for m4 in range(4):
                        m = mb * 4 + m4
                        ps = psr.next()
                        for k in range(KC):
                            P.add('pe', lambda e, ps=ps, k=k, m4=m4, wv=wv, abv=abv, n=n: e.matmul(
                                ps.ap[:, :n], wv[:, k, m4 * 128:(m4 + 1) * 128], abv[:, k, :n],
                                start=(k == 0), stop=(k == KC - 1)), reads=[slab, ab], writes=[ps])
                        x = xs.next()
                        P.dma('sp', x.ap[:, :n], xsrc[m, :, t0:t0 + n], x, reads=xregs(m, t0, n), writes=[x])
                        o = os_.next()
                        P.add('dve', lambda e, ps=ps, x=x, o=o, m=m, s=s, n=n: e.scalar_tensor_tensor(
                            out=o.ap[:, :n], in0=ps.ap[:, :n], scalar=modv[:, i, gpart * 16 + m, s:s + 1],
                            in1=x.ap[:, :n], op0=ALU.mult, op1=ALU.add), reads=[ps, x, modc], writes=[o])
                        P.dma('sp', xT_d[m, :, t0:t0 + n], o.ap[:, :n], o, reads=[o], writes=xregs(m, t0, n))
            state["xsrc"] = xT_d

        GW = 2308

        def ffn_up(i, hreg):
            w = ffn_w_up[i]
            sc = scr_carve([("g0", GW, BF16), ("g1", GW, BF16), ("sg0", T, BF16), ("sg1", T, BF16),
                            ("acc0", 512, F32), ("acc1", 512, F32), ("a0", 512, BF16), ("a1", 512, BF16), ("a2", 512, BF16)])
            gbufs = Rot([sc["g0"], sc["g1"]])
            sgbufs = Rot([sc["sg0"], sc["sg1"]])
            accs = Rot([sc["acc0"], sc["acc1"]])
            asts = Rot([sc["a0"], sc["a1"], sc["a2"]])
            for gb in (sc["g0"], sc["g1"]):
                P.add('dve', lambda e, gb=gb: e.memset(gb.ap, 0.0), writes=[gb])
            cwv = ffcw_s.ap.rearrange("p (i j k) -> p i j k", i=DEPTH, j=NJ)
            cbv = ffcb_s.ap.rearrange("p (i j) -> p i j", i=DEPTH)
            psg = Rot([PS[0], PS[1], PS[2]])
            psu = Rot([PS[3], PS[4], PS[5]])

            def goff(tb, t0):
                return 1 + t0 if tb < 4 else 2051

            slabs = {}

            def get_slab(jg):
                if jg not in slabs:
                    nj = min(4, NJ - jg * 4)
                    slab = wrot.next()
                    gv_ = load_cols(slab, 0, w, jg * 512, nj * 128)
                    uv_ = load_cols(slab, 8192, w, DFF + jg * 512, nj * 128)
                    slabs[jg] = (slab, gv_, uv_)
                return slabs[jg]

            def emit_g(j):
                slab, gv_, uv_ = get_slab(j // 4)
                jj = j % 4
                gb = gbufs.next()
                sg = sgbufs.next()
                for tb, (t0, n) in enumerate(TBS):
                    ps = psg.next()
                    for k in range(KC):
                        P.add('pe', lambda e, ps=ps, k=k, jj=jj, gv_=gv_, t0=t0, n=n: e.matmul(
                            ps.ap[:, :n], gv_[:, k, jj * 128:(jj + 1) * 128], hT[:, k, t0:t0 + n],
                            start=(k == 0), stop=(k == KC - 1)), reads=[slab, hreg[k][tb]], writes=[ps])
                    off = goff(tb, t0)
                    P.add('dve', lambda e, ps=ps, gb=gb, off=off, n=n: e.tensor_copy(out=gb.ap[:, off:off + n], in_=ps.ap[:, :n]),
                          reads=[ps], writes=[gb])
                for tb, (t0, n) in enumerate(TBS):
                    off = goff(tb, t0)
                    acc = accs.next()
                    P.add('dve', lambda e, gb=gb, acc=acc, off=off, n=n, j=j: e.tensor_scalar(
                        out=acc.ap[:, :n], in0=gb.ap[:, off - 1:off - 1 + n], scalar1=cwv[:, i, j, 0:1], scalar2=None,
                        op0=ALU.mult), reads=[gb, ffcw_s, consts], writes=[acc])
                    for kk in (1, 2):
                        P.add('dve', lambda e, gb=gb, acc=acc, off=off, n=n, j=j, kk=kk: e.scalar_tensor_tensor(
                            out=acc.ap[:, :n], in0=gb.ap[:, off - 1 + kk:off - 1 + kk + n], scalar=cwv[:, i, j, kk:kk + 1],
                            in1=acc.ap[:, :n], op0=ALU.mult, op1=ALU.add), reads=[gb, acc, ffcw_s], writes=[acc])
                    P.add('act', lambda e, acc=acc, sg=sg, t0=t0, n=n, j=j: e.activation(
                        out=sg.ap[:, t0:t0 + n], in_=acc.ap[:, :n], func=AF.Silu, bias=cbv[:, i, j:j + 1], scale=1.0),
                        reads=[acc, ffcb_s, consts], writes=[sg])
                return sg

            def emit_u(j, sg):
                slab, gv_, uv_ = get_slab(j // 4)
                jj = j % 4
                for tb, (t0, n) in enumerate(TBS):
                    ps = psu.next()
                    for k in range(KC):
                        P.add('pe', lambda e, ps=ps, k=k, jj=jj, uv_=uv_, t0=t0, n=n: e.matmul(
                            ps.ap[:, :n], uv_[:, k, jj * 128:(jj + 1) * 128], hT[:, k, t0:t0 + n],
                            start=(k == 0), stop=(k == KC - 1)), reads=[slab, hreg[k][tb]], writes=[ps])
                    a = asts.next()
                    P.add('dve', lambda e, ps=ps, sg=sg, a=a, t0=t0, n=n: e.tensor_tensor(
                        out=a.ap[:, :n], in0=ps.ap[:, :n], in1=sg.ap[:, t0:t0 + n], op=ALU.mult),
                        reads=[ps, sg], writes=[a])
                    P.dma('sp', ffa_d[j, :, t0:t0 + n], a.ap[:, :n], a, reads=[a], accum=[ffareg])

            pend = None
            for j in range(NJ):
                sg = emit_g(j)
                if pend is not None:
                    emit_u(*pend)
                pend = (j, sg)
            emit_u(*pend)

        ffareg = Buf("ffareg")

        def ffn_down(i, gpart):
            w = ffn_w_down[i]
            xsrc = state["xsrc"]
            hc = h_carve([("a0", NJ * 256, BF16), ("a1", NJ * 256, BF16)])
            abufs = Rot([hc["a0"], hc["a1"]])
            sc = scr_carve([("x0", 256, F32), ("x1", 256, F32), ("x2", 256, F32),
                            ("o0", 256, F32), ("o1", 256, F32), ("o2", 256, F32)])
            xs = Rot([sc["x0"], sc["x1"], sc["x2"]])
            os_ = Rot([sc["o0"], sc["o1"], sc["o2"]])
            psr = Rot([PS[0], PS[1], PS[2], PS[3]])
            for mb in range(4):
                slab = wrot.next()
                wv = slab.ap[:, 0:NJ * 512].rearrange("p (j n) -> p j n", j=NJ)
                srcv = w[:, mb * 512:(mb + 1) * 512].rearrange("(j p) n -> p j n", p=128)
                for q in range(0, NJ, 4):
                    q1 = min(q + 4, NJ)
                    P.dma('pool', wv[:, q:q1, :], srcv[:, q:q1, :], slab, writes=[slab])
                for tb, (t0, n) in enumerate(TB2):
                    s = 1 if tb == 8 else 0
                    ab = abufs.next()
                    abv = ab.ap.rearrange("p (j t) -> p j t", j=NJ)
                    for q in range(0, NJ, 11):
                        q1 = min(q + 11, NJ)
                        P.dma('sp', abv[:, q:q1, :], ffa_d[q:q1, :, t0:t0 + n].rearrange("j p t -> p j t"), ab,
                              reads=[ffareg], writes=[ab])
                    for m4 in range(4):
                        m = mb * 4 + m4
                        ps = psr.next()
                        for jx in range(NJ):
                            P.add('pe', lambda e, ps=ps, jx=jx, m4=m4, wv=wv, abv=abv, n=n: e.matmul(
                                ps.ap[:, :n], wv[:, jx, m4 * 128:(m4 + 1) * 128], abv[:, jx, :],
                                start=(jx == 0), stop=(jx == NJ - 1)), reads=[slab, ab], writes=[ps])
                        x = xs.next()
                        P.dma('sp', x.ap[:, :n], xsrc[m, :, t0:t0 + n], x, reads=xregs(m, t0, n), writes=[x])
                        o = os_.next()
                        P.add('dve', lambda e, ps=ps, x=x, o=o, m=m, s=s, n=n: e.scalar_tensor_tensor(
                            out=o.ap[:, :n], in0=ps.ap[:, :n], scalar=modv[:, i, gpart * 16 + m, s:s + 1],
                            in1=x.ap[:, :n], op0=ALU.mult, op1=ALU.add), reads=[ps, x, modc], writes=[o])
                        P.dma('sp', xT_d[m, :, t0:t0 + n], o.ap[:, :n], o, reads=[o], writes=xregs(m, t0, n))

        def odd_inproj(i, j, hreg):
            w = od_w_in[j]
            ov = odvec_s.ap.rearrange("p (j v c) -> p j v c", j=2, v=5)
            sc = scr_carve([("stg", 4 * 512, F32), ("sq", 4 * 512, BF16), ("rs", 512, F32),
                            ("o0", 512, BF16), ("o1", 512, BF16), ("o2", 512, BF16),
                            ("cs", 512, F32), ("sn", 512, F32), ("t1", 512, F32), ("t2", 512, F32)])
            stg, sqb, rs = sc["stg"], sc["sq"], sc["rs"]
            outs = Rot([sc["o0"], sc["o1"], sc["o2"]])
            sgm, csb, snb, t1, t2 = sc["t1"], sc["cs"], sc["sn"], sc["t1"], sc["t2"]
            stgv = stg.ap.rearrange("p (c t) -> p c t", c=4)
            sqv = sqb.ap.rearrange("p (c t) -> p c t", c=4)
            psr = Rot([PS[0], PS[1], PS[2], PS[3], PS[4], PS[5]])
            slab = wrot.next()
            av_ = load_cols(slab, 0, w, 0, 512)
            gv_ = load_cols(slab, 8192, w, 512, 512)
            for tb, (t0, n) in enumerate(TBS):
                for c in range(4):
                    pa = psr.next()
                    pg = psr.next()
                    for (pp, vv) in ((pa, av_), (pg, gv_)):
                        for k in range(KC):
                            P.add('pe', lambda e, pp=pp, vv=vv, k=k, c=c, t0=t0, n=n: e.matmul(
                                pp.ap[:, :n], vv[:, k, c * 128:(c + 1) * 128], hT[:, k, t0:t0 + n],
                                start=(k == 0), stop=(k == KC - 1)), reads=[slab, hreg[k][tb]], writes=[pp])
                    P.add('act', lambda e, pg=pg, n=n: e.activation(out=sgm.ap[:, :n], in_=pg.ap[:, :n], func=AF.Sigmoid),
                          reads=[pg], writes=[sgm])
                    o = outs.next()
                    P.add('dve', lambda e, pa=pa, o=o, n=n: e.tensor_tensor(out=o.ap[:, :n], in0=pa.ap[:, :n], in1=sgm.ap[:, :n], op=ALU.mult),
                          reads=[pa, sgm], writes=[o])
                    P.dma('sp', gl_d[c, :, t0:t0 + n], o.ap[:, :n], o, reads=[o], accum=[glreg])
            if stop == 'ip_a':
                return
            for which in range(2):
                if stop == 'ip_b1' and which == 1:
                    return
                slab = wrot.next()
                wv = load_cols(slab, 0, w, 1024 + which * 512, 512)
                for tb, (t0, n) in enumerate(TBS):
                    pss_ = psr.next()
                    for c in range(4):
                        ps = psr.next()
                        for k in range(KC):
                            P.add('pe', lambda e, ps=ps, wv=wv, k=k, c=c, t0=t0, n=n: e.matmul(
                                ps.ap[:, :n], wv[:, k, c * 128:(c + 1) * 128], hT[:, k, t0:t0 + n],
                                start=(k == 0), stop=(k == KC - 1)), reads=[slab, hreg[k][tb]], writes=[ps])
                        P.add('dve', lambda e, ps=ps, c=c, n=n: e.tensor_copy(out=stgv[:, c, :n], in_=ps.ap[:, :n]),
                              reads=[ps], writes=[stg])
                        P.add('act', lambda e, c=c, n=n: e.activation(out=sqv[:, c, :n], in_=stgv[:, c, :n], func=AF.Square),
                              reads=[stg], writes=[sqb])
                    if stop == 'ip_b2':
                        continue
                    for c in range(4):
                        P.add('pe', lambda e, pss_=pss_, c=c, n=n: e.matmul(pss_.ap[:, :n], ones512.ap, sqv[:, c, :n],
                                                                           start=(c == 0), stop=(c == 3)),
                              reads=[sqb, ones512], writes=[pss_])
                    P.add('act', lambda e, pss_=pss_, n=n: e.activation(out=rs.ap[:, :n], in_=pss_.ap[:, :n], func=AF.Sqrt, bias=epsb.ap[:, 0:1], scale=1.0), reads=[pss_, epsb], writes=[rs])
                    P.add('dve', lambda e, pss_=pss_, n=n: e.reciprocal(out=rs.ap[:, :n], in_=rs.ap[:, :n]), reads=[rs], writes=[rs])
                    for c in range(4):
                        o = outs.next()
                        P.add('dve', lambda e, o=o, c=c, n=n, which=which: e.scalar_tensor_tensor(
                            out=o.ap[:, :n], in0=stgv[:, c, :n], scalar=ov[:, j, 3 + which, c:c + 1], in1=rs.ap[:, :n],
                            op0=ALU.mult, op1=ALU.mult), reads=[stg, rs, odvec_s, consts], writes=[o])
                        P.dma('sp', cqn_d[which * 4 + c, :, t0:t0 + n], o.ap[:, :n], o, reads=[o], accum=[cqnreg])
            if stop in ('ip_b', 'ip_b2'):
                return
            slab = wrot.next()
            krv = slab.ap[:, 0:KC * 256].rearrange("p (k n) -> p k n", k=KC)
            wsrc = w.rearrange("(k p) n -> p k n", p=128)
            for (d0, s0, nn_) in ((0, 2048, 64), (64, 2048, 64), (128, 2080, 32), (160, 2048, 32), (192, 2080, 32), (224, 2048, 32)):
                for q in range(0, KC, 4):
                    P.dma('pool', krv[:, q:q + 4, d0:d0 + nn_], wsrc[:, q:q + 4, s0:s0 + nn_], slab, writes=[slab])
            for tb, (t0, n) in enumerate(TBS):
                pk = psr.next()
                pr = psr.next()
                for (pp, c0) in ((pk, 0), (pr, 128)):
                    for k in range(KC):
                        P.add('pe', lambda e, pp=pp, c0=c0, k=k, t0=t0, n=n: e.matmul(
                            pp.ap[:, :n], krv[:, k, c0:c0 + 128], hT[:, k, t0:t0 + n],
                            start=(k == 0), stop=(k == KC - 1)), reads=[slab, hreg[k][tb]], writes=[pp])
                P.dma('sp', csb.ap[:, :n], ropeM_d[0, :, t0:t0 + n], csb, writes=[csb])
                P.dma('sp', snb.ap[:, :n], ropeM_d[1, :, t0:t0 + n], snb, writes=[snb])
                P.add('dve', lambda e, pk=pk, n=n: e.tensor_tensor(out=t1.ap[:, :n], in0=pk.ap[:, :n], in1=csb.ap[:, :n], op=ALU.mult),
                      reads=[pk, csb], writes=[t1])
                P.add('dve', lambda e, pr=pr, n=n: e.tensor_tensor(out=t2.ap[:, :n], in0=pr.ap[:, :n], in1=snb.ap[:, :n], op=ALU.mult),
                      reads=[pr, snb], writes=[t2])
                o = outs.next()
                P.add('dve', lambda e, o=o, n=n: e.tensor_tensor(out=o.ap[:, :n], in0=t1.ap[:, :n], in1=t2.ap[:, :n], op=ALU.add),
                      reads=[t1, t2], writes=[o])
                P.dma('sp', kr_d[:, t0:t0 + n], o.ap[:, :n], o, reads=[o], accum=[qkreg])

        glreg = Buf("glreg")
        cqnreg = Buf("cqnreg")

        def odd_upproj(i, j):
            hc = h_carve([("cq", 4 * T, BF16), ("ckv", 4 * T, BF16)])
            cqb, ckvb = hc["cq"], hc["ckv"]
            cqv = cqb.ap.rearrange("p (c t) -> p c t", c=4)
            ckvv = ckvb.ap.rearrange("p (c t) -> p c t", c=4)
            for c in range(4):
                P.dma('sp', cqv[:, c, :], cqn_d[c], cqb, reads=[cqnreg], writes=[cqb])
                P.dma('sp', ckvv[:, c, :], cqn_d[4 + c], ckvb, reads=[cqnreg], writes=[ckvb])
            sc = scr_carve([("o0", 512, BF16), ("o1", 512, BF16), ("o2", 512, BF16), ("o3", 512, BF16),
                            ("cs", 512, F32), ("sn", 512, F32), ("t1", 512, F32), ("t2", 512, F32)])
            outs = Rot([sc["o0"], sc["o1"], sc["o2"], sc["o3"]])
            csb, snb, t1, t2 = sc["cs"], sc["sn"], sc["t1"], sc["t2"]
            psr = Rot([PS[0], PS[1], PS[2], PS[3], PS[4], PS[5]])
            slab = wrot.next()
            wq = od_w_uq[j].rearrange("(k p) (h d) -> p k h d", p=128, d=192)
            qn = slab.ap[:, 0:4 * 1536].rearrange("p (k h d) -> p k h d", k=4, h=12)
            qr = slab.ap[:, 6144:6144 + 4 * 768].rearrange("p (k h d) -> p k h d", k=4, h=12)
            qx = slab.ap[:, 9216:9216 + 4 * 768].rearrange("p (k h d) -> p k h d", k=4, h=12)
            for k in range(4):
                for h0 in range(0, 12, 4):
                    P.dma('pool', qn[:, k, h0:h0 + 4, :], wq[:, k, h0:h0 + 4, 0:128], slab, writes=[slab])
                    P.dma('pool', qr[:, k, h0:h0 + 4, :], wq[:, k, h0:h0 + 4, 128:192], slab, writes=[slab])
                    P.dma('pool', qx[:, k, h0:h0 + 4, 0:32], wq[:, k, h0:h0 + 4, 160:192], slab, writes=[slab])
                    P.dma('pool', qx[:, k, h0:h0 + 4, 32:64], wq[:, k, h0:h0 + 4, 128:160], slab, writes=[slab])
            qnf = slab.ap[:, 0:6144].rearrange("p (k n) -> p k n", k=4)
            qrf = slab.ap[:, 6144:9216].rearrange("p (k n) -> p k n", k=4)
            qxf = slab.ap[:, 9216:12288].rearrange("p (k n) -> p k n", k=4)
            for tb, (t0, n) in enumerate(TBS):
                for h in range(12):
                    ps = psr.next()
                    for k in range(4):
                        P.add('pe', lambda e, ps=ps, k=k, h=h, t0=t0, n=n: e.matmul(
                            ps.ap[:, :n], qnf[:, k, h * 128:(h + 1) * 128], cqv[:, k, t0:t0 + n],
                            start=(k == 0), stop=(k == 3)), reads=[slab, cqb], writes=[ps])
                    o = outs.next()
                    P.add('act', lambda e, ps=ps, o=o, n=n: e.copy(out=o.ap[:, :n], in_=ps.ap[:, :n]), reads=[ps], writes=[o])
                    P.dma('sp', qT_d[h, :, t0:t0 + n], o.ap[:, :n], o, reads=[o], accum=[qkreg])
                P.dma('sp', csb.ap[:, :n], ropeM_d[0, :, t0:t0 + n], csb, writes=[csb])
                P.dma('sp', snb.ap[:, :n], ropeM_d[1, :, t0:t0 + n], snb, writes=[snb])
                for hp in range(6):
                    pk = psr.next()
                    pr = psr.next()
                    for (pp, vv) in ((pk, qrf), (pr, qxf)):
                        for k in range(4):
                            P.add('pe', lambda e, pp=pp, vv=vv, k=k, hp=hp, t0=t0, n=n: e.matmul(
                                pp.ap[:, :n], vv[:, k, hp * 128:(hp + 1) * 128], cqv[:, k, t0:t0 + n],
                                start=(k == 0), stop=(k == 3)), reads=[slab, cqb], writes=[pp])
                    P.add('dve', lambda e, pk=pk, n=n: e.tensor_tensor(out=t1.ap[:, :n], in0=pk.ap[:, :n], in1=csb.ap[:, :n], op=ALU.mult),
                          reads=[pk, csb], writes=[t1])
                    P.add('dve', lambda e, pr=pr, n=n: e.tensor_tensor(out=t2.ap[:, :n], in0=pr.ap[:, :n], in1=snb.ap[:, :n], op=ALU.mult),
                          reads=[pr, snb], writes=[t2])
                    o = outs.next()
                    P.add('dve', lambda e, o=o, n=n: e.tensor_tensor(out=o.ap[:, :n], in0=t1.ap[:, :n], in1=t2.ap[:, :n], op=ALU.add),
                          reads=[t1, t2], writes=[o])
                    P.dma('sp', qr_d[hp, :, t0:t0 + n], o.ap[:, :n], o, reads=[o], accum=[qkreg])
            slab = wrot.next()
            wkv = od_w_ukv[j].rearrange("(k p) (h d) -> p k h d", p=128, d=256)
            kn = slab.ap[:, 0:6144].rearrange("p (k h d) -> p k h d", k=4, h=12)
            vv_ = slab.ap[:, 6144:12288].rearrange("p (k h d) -> p k h d", k=4, h=12)
            for k in range(4):
                for h0 in range(0, 12, 4):
                    P.dma('pool', kn[:, k, h0:h0 + 4, :], wkv[:, k, h0:h0 + 4, 0:128], slab, writes=[slab])
                    P.dma('pool', vv_[:, k, h0:h0 + 4, :], wkv[:, k, h0:h0 + 4, 128:256], slab, writes=[slab])
            knf = slab.ap[:, 0:6144].rearrange("p (k n) -> p k n", k=4)
            vf = slab.ap[:, 6144:12288].rearrange("p (k n) -> p k n", k=4)
            for tb, (t0, n) in enumerate(TBS):
                for h in range(12):
                    ps = psr.next()
                    for k in range(4):
                        P.add('pe', lambda e, ps=ps, k=k, h=h, t0=t0, n=n: e.matmul(
                            ps.ap[:, :n], knf[:, k, h * 128:(h + 1) * 128], ckvv[:, k, t0:t0 + n],
                            start=(k == 0), stop=(k == 3)), reads=[slab, ckvb], writes=[ps])
                    o = outs.next()
                    P.add('act', lambda e, ps=ps, o=o, n=n: e.copy(out=o.ap[:, :n], in_=ps.ap[:, :n]), reads=[ps], writes=[o])
                    P.dma('sp', kT_d[h, :, t0:t0 + n], o.ap[:, :n], o, reads=[o], accum=[qkreg])
            for tt in range(NT):
                for vg in range(3):
                    ps = psr.next()
                    for k in range(4):
                        P.add('pe', lambda e, ps=ps, k=k, vg=vg, tt=tt: e.matmul(
                            ps.ap, ckvv[:, k, tt * 128:(tt + 1) * 128], vf[:, k, vg * 512:(vg + 1) * 512],
                            start=(k == 0), stop=(k == 3)), reads=[slab, ckvb], writes=[ps])
                    o = outs.next()
                    P.add('act', lambda e, ps=ps, o=o: e.copy(out=o.ap, in_=ps.ap), reads=[ps], writes=[o])
                    P.dma('sp', v_d[tt, :, vg * 512:(vg + 1) * 512], o.ap, o, reads=[o], accum=[qkreg])

        GP = 15
        LW = L + 2 * GP
        CW = CL + 2 * GP

        def odd_conv(i, j):
            hc = h_carve([("gl", 4 * (LW + CW), BF16), ("u", 4 * T, F32)])
            glb, ub = hc["gl"], hc["u"]
            glv = glb.ap.rearrange("p (c t) -> p c t", c=4)
            uv = ub.ap.rearrange("p (c t) -> p c t", c=4)
            sc = scr_carve([("acc0", 512, F32), ("acc1", 512, F32), ("usq", 4 * 512, F32), ("mu", 512, F32), ("var", 512, F32),
                            ("t", 512, F32), ("o0", 512, BF16), ("o1", 512, BF16)])
            usq, mu, var, tt_ = sc["usq"], sc["mu"], sc["var"], sc["t"]
            outs = Rot([sc["o0"], sc["o1"]])
            usqv = usq.ap.rearrange("p (c t) -> p c t", c=4)
            ov = odvec_s.ap.rearrange("p (j v c) -> p j v c", j=2, v=5)
            cwv = odcw_s.ap.rearrange("p (j c k) -> p j c k", j=2, c=4)
            P.add('dve', lambda e: e.memset(glb.ap, 0.0), writes=[glb])
            for c in range(4):
                P.dma('sp', glv[:, c, GP:GP + L], gl_d[c, :, 0:L], glb, reads=[glreg], writes=[glb])
                P.dma('sp', glv[:, c, LW + GP:LW + GP + CL], gl_d[c, :, L:T], glb, reads=[glreg], writes=[glb])
            accs = Rot([sc["acc0"], sc["acc1"]])
            for c in range(4):
                for tb, (t0, n) in enumerate(TBS):
                    base = t0 if tb < 4 else LW
                    acc = accs.next()
                    P.add('dve', lambda e, acc=acc, c=c, base=base, n=n: e.tensor_scalar(
                        out=acc.ap[:, :n], in0=glv[:, c, base:base + n], scalar1=cwv[:, j, c, 0:1], scalar2=None, op0=ALU.mult),
                        reads=[glb, odcw_s, consts], writes=[acc])
                    for k in range(1, 31):
                        P.add('dve', lambda e, acc=acc, c=c, k=k, base=base, n=n: e.scalar_tensor_tensor(
                            out=acc.ap[:, :n], in0=glv[:, c, base + k:base + k + n], scalar=cwv[:, j, c, k:k + 1],
                            in1=acc.ap[:, :n], op0=ALU.mult, op1=ALU.add), reads=[glb, acc, odcw_s], writes=[acc])
                    P.add('act', lambda e, acc=acc, c=c, t0=t0, n=n: e.activation(
                        out=uv[:, c, t0:t0 + n], in_=acc.ap[:, :n], func=AF.Identity, bias=ov[:, j, 0, c:c + 1], scale=1.0),
                        reads=[acc, odvec_s, consts], writes=[ub])
            pm = Rot([PS[3], PS[4]])
            pv = Rot([PS[5], PS[6]])
            for tb, (t0, n) in enumerate(TBS):
                p1 = pm.next()
                p2 = pv.next()
                for c in range(4):
                    P.add('act', lambda e, c=c, t0=t0, n=n: e.activation(out=usqv[:, c, :n], in_=uv[:, c, t0:t0 + n], func=AF.Square),
                          reads=[ub], writes=[usq])
                for c in range(4):
                    P.add('pe', lambda e, p1=p1, c=c, t0=t0, n=n: e.matmul(p1.ap[:, :n], ones512f.ap, uv[:, c, t0:t0 + n],
                                                                          start=(c == 0), stop=(c == 3)),
                          reads=[ub, ones512f], writes=[p1])
                for c in range(4):
                    P.add('pe', lambda e, p2=p2, c=c, n=n: e.matmul(p2.ap[:, :n], ones512f.ap, usqv[:, c, :n],
                                                                   start=(c == 0), stop=(c == 3)),
                          reads=[usq, ones512f], writes=[p2])
                P.add('dve', lambda e, p1=p1, n=n: e.tensor_copy(out=mu.ap[:, :n], in_=p1.ap[:, :n]), reads=[p1], writes=[mu])
                P.add('dve', lambda e, n=n: e.tensor_tensor(out=var.ap[:, :n], in0=mu.ap[:, :n], in1=mu.ap[:, :n], op=ALU.mult),
                      reads=[mu], writes=[var])
                P.add('dve', lambda e, p2=p2, n=n: e.tensor_tensor(out=var.ap[:, :n], in0=p2.ap[:, :n], in1=var.ap[:, :n], op=ALU.subtract),
                      reads=[p2, var], writes=[var])
                P.add('act', lambda e, n=n: e.activation(out=var.ap[:, :n], in_=var.ap[:, :n], func=AF.Sqrt, bias=epsb.ap[:, 0:1], scale=1.0), reads=[var, epsb], writes=[var])
                P.add('dve', lambda e, n=n: e.reciprocal(out=var.ap[:, :n], in_=var.ap[:, :n]), reads=[var], writes=[var])
                for c in range(4):
                    P.add('dve', lambda e, c=c, t0=t0, n=n: e.tensor_tensor(out=tt_.ap[:, :n], in0=uv[:, c, t0:t0 + n], in1=mu.ap[:, :n], op=ALU.subtract),
                          reads=[ub, mu], writes=[tt_])
                    P.add('dve', lambda e, n=n: e.tensor_tensor(out=tt_.ap[:, :n], in0=tt_.ap[:, :n], in1=var.ap[:, :n], op=ALU.mult),
                          reads=[tt_, var], writes=[tt_])
                    o = outs.next()
                    P.add('act', lambda e, o=o, c=c, n=n: e.activation(out=o.ap[:, :n], in_=tt_.ap[:, :n], func=AF.Silu,
                                                                      bias=ov[:, j, 2, c:c + 1], scale=ov[:, j, 1, c:c + 1]),
                          reads=[tt_, odvec_s, consts], writes=[o])
                    P.dma('sp', aT_d[c, :, t0:t0 + n], o.ap[:, :n], o, reads=[o], writes=[aTreg[c][tb]])

        def final_norm():
            xsrc = state["xsrc"]
            sc = scr_carve([("x0", 512, F32), ("x1", 512, F32), ("x2", 512, F32), ("x3", 512, F32),
                            ("sq0", 512, BF16), ("sq1", 512, BF16), ("rs0", 512, F32), ("rs1", 512, F32),
                            ("t0", 512, F32), ("t1", 512, F32), ("t2", 512, F32)])
            xs = Rot([sc["x0"], sc["x1"], sc["x2"], sc["x3"]])
            sqs = Rot([sc["sq0"], sc["sq1"]])
            rss = Rot([sc["rs0"], sc["rs1"]])
            ts = Rot([sc["t0"], sc["t1"], sc["t2"]])
            for tb, (t0, n) in enumerate(TBS[:4]):
                ps = PS[tb % 2]
                for c in range(KC):
                    x = xs.next()
                    P.dma('sp', x.ap[:, :n], xsrc[c, :, t0:t0 + n], x, reads=xregs(c, t0, n), writes=[x])
                    sq = sqs.next()
                    P.add('act', lambda e, x=x, sq=sq, n=n: e.activation(out=sq.ap[:, :n], in_=x.ap[:, :n], func=AF.Square),
                          reads=[x], writes=[sq])
                    P.add('pe', lambda e, ps=ps, sq=sq, c=c, n=n: e.matmul(ps.ap[:, :n], onesD.ap, sq.ap[:, :n],
                                                                          start=(c == 0), stop=(c == KC - 1)),
                          reads=[sq, onesD], writes=[ps])
                rs = rss.next()
                P.add('act', lambda e, ps=ps, rs=rs, n=n: e.activation(out=rs.ap[:, :n], in_=ps.ap[:, :n], func=AF.Sqrt, bias=epsb.ap[:, 0:1], scale=1.0), reads=[ps, epsb], writes=[rs])
                P.add('dve', lambda e, ps=ps, rs=rs, n=n: e.reciprocal(out=rs.ap[:, :n], in_=rs.ap[:, :n]), reads=[rs], writes=[rs])
                for c in range(KC):
                    x = xs.next()
                    P.dma('sp', x.ap[:, :n], xsrc[c, :, t0:t0 + n], x, reads=xregs(c, t0, n), writes=[x])
                    t = ts.next()
                    P.add('dve', lambda e, x=x, t=t, rs=rs, c=c, n=n: e.scalar_tensor_tensor(
                        out=t.ap[:, :n], in0=x.ap[:, :n], scalar=fing_s.ap[:, c:c + 1], in1=rs.ap[:, :n],
                        op0=ALU.mult, op1=ALU.mult), reads=[x, rs, fing_s, consts], writes=[t])
                    P.dma('sp', out_d[c, :, t0:t0 + n], t.ap[:, :n], t, reads=[t], accum=[outreg])

        def copy_out():
            xsrc = state["xsrc"]
            sc = scr_carve([("x0", 512, F32), ("x1", 512, F32), ("x2", 512, F32)])
            xs = Rot([sc["x0"], sc["x1"], sc["x2"]])
            for tb, (t0, n) in enumerate(TBS[:4]):
                for c in range(KC):
                    x = xs.next()
                    P.dma('sp', x.ap[:, :n], xsrc[c, :, t0:t0 + n], x, reads=xregs(c, t0, n), writes=[x])
                    P.dma('sp', out_d[c, :, t0:t0 + n], x.ap[:, :n], x, reads=[x], accum=[outreg])

        outreg = Buf("outreg")

        for i in range(NL):
            j = i // 2
            hreg = new_hreg()
            modulate(i, 0, 1, hreg)
            if i % 2 == 0:
                even_inproj(i, j, hreg)
                P.end_phase(dummy)
                fourier(i)
                P.end_phase(dummy)
                attention(12,
                          lambda h: [(kT_d[h // 3], 128, 0)],
                          lambda h: [(qT_d[h], 128, 0)],
                          lambda h: v_d[:, :, (h // 3) * 128:(h // 3 + 1) * 128],
                          lambda h: h // 3, 128 ** -0.5, 4)
                P.end_phase(dummy)
                outproj(i, ev_w_out[j], 2)
            else:
                if stop == 'mod':
                    P.end_phase(dummy)
                    break
                odd_inproj(i, j, hreg)
                P.end_phase(dummy)
                if stop in ('inproj', 'ip_a', 'ip_b', 'ip_b1', 'ip_b2'):
                    break
                odd_upproj(i, j)
                P.end_phase(dummy)
                if stop == 'upproj':
                    break
                odd_conv(i, j)
                P.end_phase(dummy)
                if stop == 'conv':
                    break
                attention(12,
                          lambda h: [(kT_d[h], 128, 0), (kr_d[(h % 2) * 64:(h % 2) * 64 + 64, :], 64, (h % 2) * 64)],
                          lambda h: [(qT_d[h], 128, 0), (qr_d[h // 2, (h % 2) * 64:(h % 2) * 64 + 64, :], 64, (h % 2) * 64)],
                          lambda h: v_d[:, :, h * 128:(h + 1) * 128],
                          lambda h: h, 192 ** -0.5, 4)
                P.end_phase(dummy)
                if stop == 'attn':
                    break
                outproj(i, od_w_out[j], 2)
                if stop == 'outproj':
                    P.end_phase(dummy)
                    break
            P.end_phase(dummy)
            hreg = new_hreg()
            modulate(i, 3, 4, hreg)
            ffn_up(i, hreg)
            P.end_phase(dummy)
            ffn_down(i, 5)
            P.end_phase(dummy)
        if final:
            final_norm()
        else:
            copy_out()
        P.add('sp', lambda e: e.nop(), reads=[outreg])
        P.emit()
    return nc, dt_in


def _host_consts():
    bf = ml_dtypes.bfloat16
    c = {}
    c["ident_bf"] = np.eye(128, dtype=np.float32).astype(bf)
    c["ident_f"] = np.eye(128, dtype=np.float32)
    n = np.arange(128)
    a = 2 * np.pi * np.outer(n, n) / 128.0
    c["dft128"] = (np.concatenate([np.cos(a), np.sin(a)], axis=1) / np.sqrt(128.0)).astype(np.float32).astype(bf)

    def dft(N):
        k = np.arange(N, dtype=np.int64)
        ph = (np.outer(k, k) % N).astype(np.float64) * (2 * np.pi / N)
        return np.stack([np.cos(ph), -np.sin(ph)], 0) / np.sqrt(float(N))
    c["dftL"] = dft(L).astype(np.float32).astype(bf)
    c["dftC"] = dft(CL).astype(np.float32).astype(bf)

    def tables(rot_dim):
        rows = L // 64
        row = np.repeat(np.arange(rows, dtype=np.float32), 64)
        col = np.tile(np.arange(64, dtype=np.float32), rows)
        nn = rot_dim // 4
        inv = np.power(np.float32(10000.0), -np.arange(nn, dtype=np.float32) / nn).astype(np.float32)
        ang = np.concatenate([row[:, None] * inv, col[:, None] * inv], axis=-1).astype(np.float32)
        return np.cos(ang).astype(np.float32), np.sin(ang).astype(np.float32)
    ca, sa = tables(128)
    c["ropeA"] = np.concatenate([ca, sa], axis=1).reshape(16, 128, 128).astype(np.float32)
    cm, sm = tables(64)
    cosT = np.ones((64, T), np.float32)
    sinT = np.zeros((64, T), np.float32)
    cosT[:32, :L] = cm.T
    cosT[32:, :L] = cm.T
    sinT[:32, :L] = -sm.T
    sinT[32:, :L] = sm.T
    c["ropeM"] = np.stack([np.concatenate([cosT, cosT], 0), np.concatenate([sinT, sinT], 0)], 0)
    return c


def _col(v, nchunk):
    return np.ascontiguousarray(v.reshape(nchunk, 128).T)


_CACHE = {}


def make_in_maps(inputs, ncores=8):
    f = lambda a: np.ascontiguousarray(np.asarray(a, dtype=np.float32))
    x, c, ctx, c_ctx = f(inputs["x"]), f(inputs["c"]), f(inputs["ctx"]), f(inputs["c_ctx"])
    shared = dict(_host_consts())
    shared["ada_w"] = f(inputs["ada_w"])
    ab = f(inputs["ada_b"])
    shared["ada_bc"] = np.ascontiguousarray(ab.reshape(DEPTH, 96, 128).transpose(2, 0, 1).reshape(128, DEPTH * 96))
    shared["ev_w_in"] = f(inputs["ev_w_in"])
    shared["ev_w_out"] = f(inputs["ev_w_out"])
    g = np.stack([f(inputs["ev_q_gain"]), f(inputs["ev_k_gain"])], 1)
    shared["gains"] = np.ascontiguousarray(np.broadcast_to(g.reshape(1, 512), (128, 512)))
    shared["od_w_in"] = f(inputs["od_w_in"])
    shared["od_w_uq"] = f(inputs["od_w_uq"])
    shared["od_w_ukv"] = f(inputs["od_w_ukv"])
    shared["od_w_out"] = f(inputs["od_w_out"])
    cw = f(inputs["od_conv_w"])
    shared["od_cw"] = np.ascontiguousarray(cw.reshape(2, 31, 4, 128).transpose(3, 0, 2, 1).reshape(128, 2 * 4 * 31))
    vecs = np.stack([f(inputs[k]) for k in ("od_conv_b", "od_ln_g", "od_ln_b", "od_q_norm", "od_kv_norm")], 1)
    shared["od_vec"] = np.ascontiguousarray(vecs.reshape(2, 5, 4, 128).transpose(3, 0, 1, 2).reshape(128, 40))
    shared["ffn_w_up"] = f(inputs["ffn_w_up"])
    shared["ffn_w_down"] = f(inputs["ffn_w_down"])
    fcw = f(inputs["ffn_conv_w"])
    shared["ffn_cw"] = np.ascontiguousarray(fcw.reshape(DEPTH, 3, NJ, 128).transpose(3, 0, 2, 1).reshape(128, DEPTH * NJ * 3))
    fcb = f(inputs["ffn_conv_b"])
    shared["ffn_cb"] = np.ascontiguousarray(fcb.reshape(DEPTH, NJ, 128).transpose(2, 0, 1).reshape(128, DEPTH * NJ))
    shared["fin_g"] = _col(f(inputs["final_norm"]), KC)
    maps = []
    for core in range(ncores):
        b = core % 4
        m = dict(shared)
        xa = np.concatenate([x[b], ctx[b]], axis=0)
        m["xT_in"] = np.ascontiguousarray(xa.T.reshape(KC, 128, T))
        cc = np.stack([_col(c[b], KC), _col(c_ctx, KC)], -1)
        m["c2"] = np.ascontiguousarray(cc.reshape(128, KC * 2))
        maps.append(m)
    return maps


def kernel(**inputs):
    if "nc" not in _CACHE:
        _CACHE["nc"] = build_program()
    nc, _ = _CACHE["nc"]
    maps = make_in_maps(inputs)
    res = run_bass_kernel_spmd(nc, maps, core_ids=list(range(8)))
    outs = []
    for b in range(4):
        yT = np.asarray(res.results[b]["yT"], dtype=np.float32)
        outs.append(yT.reshape(D, L).T)
    return np.ascontiguousarray(np.stack(outs, 0)).astype(np.float32)
```

```python
import numpy as np
import ml_dtypes
from contextlib import ExitStack
import concourse.bass as bass
import concourse.mybir as mybir
from concourse.bass_utils import run_bass_kernel_spmd

F32 = mybir.dt.float32
BF16 = mybir.dt.bfloat16
AF = mybir.ActivationFunctionType
ALU = mybir.AluOpType
AX = mybir.AxisListType

D = 2048
KC = 16
L = 2048
CL = 256
T = L + CL
NT = T // 128
DEPTH = 4
DFF = 5504
NJ = DFF // 128
EPS = 1e-6
TBS = [(0, 512), (512, 512), (1024, 512), (1536, 512), (2048, 256)]
TB2 = [(i * 256, 256) for i in range(9)]
WS_COLS = 22016
SAME_ENG_SYNC = True


class Buf:
    __slots__ = ('name', 'ap', 'w', 'r', 'sem', 'ndma', 'ws', 'phase')

    def __init__(self, name, ap=None):
        self.name = name
        self.ap = ap
        self.w = None
        self.r = {}
        self.ws = {}
        self.phase = False
        self.sem = None
        self.ndma = 0


class Op:
    __slots__ = ('eng', 'fn', 'deps', 'tok', 'sig', 'dma', 'waits', 'ph')


class Rec:
    def __init__(self):
        self.call = None

    def __getattr__(self, name):
        def f(*a, **k):
            self.call = (name, a, k)
            return self
        return f


class Prog:
    ENGS = ('pe', 'act', 'dve', 'pool', 'sp')

    def __init__(self, nc, stack):
        self.nc = nc
        self.stack = stack
        self.ops = []
        self.barrier = None
        self.esem = {e: [stack.enter_context(nc.semaphore('s_%s%d' % (e, r))) for r in range(3)] for e in self.ENGS}
        self.phase_no = 0
        self.nsem = 0
        self.phase_bufs = []
        self.scr_used = False
        self.h_used = False
        self.dummy = None
        self.sem_pool = []
        self.pool_idx = 0
        self.semcnt = {}

    def sbuf(self, name, cols, dt):
        t = self.stack.enter_context(self.nc.sbuf_tensor(name, [128, cols], dt))
        return Buf(name, t[:])

    def psum(self, name, cols, dt):
        t = self.stack.enter_context(self.nc.psum_tensor(name, [128, cols], dt))
        return Buf(name, t[:])

    def region(self, name, ap=None, phase=True):
        b = Buf(name, ap)
        b.w = self.barrier
        if phase:
            b.phase = True
            self.phase_bufs.append(b)
        return b

    def end_phase(self, dummy=None):
        dummy = dummy or self.dummy
        bufs = self.phase_bufs
        self.phase_bufs = []
        self.scr_used = False
        self.h_used = False
        i = self.add('dve', lambda e: e.memset(dummy.ap[:, 0:1], 0.0), writes=[dummy] + bufs)
        self.barrier = i
        self.phase_no += 1
        self.pool_idx = (self.phase_no % 4) * 12

    def add(self, eng, fn, reads=(), writes=(), dma=None, accum=()):
        i = len(self.ops)
        deps = set()
        for b in reads:
            if b.w is not None:
                deps.add(b.w)
            deps.update(b.ws.values())
        for b in writes:
            if b.w is not None:
                deps.add(b.w)
            deps.update(b.r.values())
            deps.update(b.ws.values())
        for b in accum:
            if b.w is not None:
                deps.add(b.w)
            deps.update(b.r.values())
        op = Op()
        op.eng = eng
        rec = Rec()
        fn(rec)
        op.fn = rec.call
        op.dma = dma
        op.ph = self.phase_no % 3
        op.sig = False
        op.tok = None
        op.waits = None
        if dma is not None:
            if dma.sem is None:
                if dma.phase:
                    while self.pool_idx >= len(self.sem_pool):
                        self.sem_pool.append(self.stack.enter_context(self.nc.semaphore('q%d' % len(self.sem_pool))))
                    assert self.pool_idx < (self.phase_no % 4) * 12 + 12
                    dma.sem = self.sem_pool[self.pool_idx]
                    self.pool_idx += 1
                else:
                    dma.sem = self.stack.enter_context(self.nc.semaphore('d%d' % self.nsem))
                    self.nsem += 1
            k_ = id(dma.sem)
            self.semcnt[k_] = self.semcnt.get(k_, 0) + 1
            op.tok = (dma.sem, 16 * self.semcnt[k_])
            key = ('dma', id(dma))
        else:
            key = eng
        for b in writes:
            b.w = i
            b.r = {}
            b.ws = {}
        for b in accum:
            b.ws[key] = i
        for b in reads:
            if b.w != i:
                b.r[key] = i
        deps.discard(i)
        op.deps = deps
        self.ops.append(op)
        return i

    def dma(self, eng, out_ap, in_ap, tile, reads=(), writes=(), accum=(), **kw):
        return self.add(eng, lambda e: e.dma_start(out=out_ap, in_=in_ap, **kw),
                        reads=reads, writes=writes, dma=tile, accum=accum)

    def finalize(self):
        ops = self.ops
        for op in ops:
            for d in op.deps:
                dop = ops[d]
                if dop.dma is None:
                    if dop.eng == op.eng and op.dma is None and (op.eng == 'pe' or not SAME_ENG_SYNC):
                        continue
                    dop.sig = True
        cnt = {(e, r): 0 for e in self.ENGS for r in range(3)}
        for op in ops:
            if op.dma is None and op.sig:
                cnt[(op.eng, op.ph)] += 1
                op.tok = (self.esem[op.eng][op.ph], cnt[(op.eng, op.ph)])
        self.max_cnt = max(cnt.values())
        known = {e: {} for e in self.ENGS}
        for op in ops:
            need = {}
            for d in op.deps:
                dop = ops[d]
                if dop.tok is None:
                    continue
                if dop.dma is None and dop.eng == op.eng and op.dma is None and (op.eng == 'pe' or not SAME_ENG_SYNC):
                    continue
                sem, val = dop.tok
                k = id(sem)
                if known[op.eng].get(k, 0) >= val:
                    continue
                if k not in need or need[k][1] < val:
                    need[k] = (sem, val)
            for k, (sem, val) in need.items():
                known[op.eng][k] = val
            op.waits = list(need.values())

    def emit(self):
        self.finalize()
        mx = {}
        for op in self.ops:
            if op.tok is not None:
                mx[op.tok[0].name] = max(mx.get(op.tok[0].name, 0), op.tok[1])
        self.max_tok = mx
        print("[kernel] ops=%d max_sem_value=%d nsems=%d" % (len(self.ops), max(mx.values()), len(mx)))
        nc = self.nc
        ops = self.ops

        def run(eng, e):
            for op in ops:
                if op.eng != eng:
                    continue
                for sem, val in op.waits:
                    e.wait_ge(sem, val)
                name, a, k = op.fn
                ins = getattr(e, name)(*a, **k)
                if op.dma is not None:
                    ins.then_inc(op.tok[0], 16)
                elif op.sig:
                    ins.then_inc(op.tok[0], 1)

        with nc.Block() as block:
            @block.tensor
            def _(e):
                run('pe', e)

            @block.scalar
            def _(e):
                run('act', e)

            @block.vector
            def _(e):
                run('dve', e)

            @block.gpsimd
            def _(e):
                run('pool', e)

            @block.sync
            def _(e):
                run('sp', e)


class Rot:
    def __init__(self, items):
        self.items = items
        self.i = 0

    def next(self):
        b = self.items[self.i % len(self.items)]
        self.i += 1
        return b


def build_program(NL=DEPTH, final=True, stop=None):
    nc = bass.Bass("TRN2", target_bir_lowering=False)
    dt_in = {}

    def ein(name, shape, dt=F32):
        dt_in[name] = dt
        return nc.dram_tensor(name, list(shape), dt, kind="ExternalInput").ap()

    xT_in = ein("xT_in", [KC, 128, T])
    c2 = ein("c2", [128, KC * 2])
    ada_w = ein("ada_w", [DEPTH, D, 6 * D])
    ada_bc = ein("ada_bc", [128, DEPTH * 96])
    ev_w_in = ein("ev_w_in", [2, D, 3072])
    ev_w_out = ein("ev_w_out", [2, D, D])
    gains = ein("gains", [128, 2 * 2 * 128])
    od_w_in = ein("od_w_in", [2, D, 2112])
    od_w_uq = ein("od_w_uq", [2, 512, 2304])
    od_w_ukv = ein("od_w_ukv", [2, 512, 3072])
    od_w_out = ein("od_w_out", [2, D, D])
    od_cw = ein("od_cw", [128, 2 * 4 * 31])
    od_vec = ein("od_vec", [128, 2 * 5 * 4])
    ffn_w_up = ein("ffn_w_up", [DEPTH, D, 2 * DFF])
    ffn_w_down = ein("ffn_w_down", [DEPTH, DFF, D])
    ffn_cw = ein("ffn_cw", [128, DEPTH * NJ * 3])
    ffn_cb = ein("ffn_cb", [128, DEPTH * NJ])
    fin_g = ein("fin_g", [128, KC])
    ident_bf_d = ein("ident_bf", [128, 128], BF16)
    ident_f_d = ein("ident_f", [128, 128])
    dft128_d = ein("dft128", [128, 256], BF16)
    dftL_d = ein("dftL", [2, L, L], BF16)
    dftC_d = ein("dftC", [2, CL, CL], BF16)
    ropeA_d = ein("ropeA", [16, 128, 128])
    ropeM_d = ein("ropeM", [2, 128, T])
    out_d = nc.dram_tensor("yT", [KC, 128, L], F32, kind="ExternalOutput").ap()

    xT_d = nc.dram_tensor("xT_s", [KC, 128, T], F32).ap()
    aT_d = nc.dram_tensor("aT_s", [KC, 128, T], BF16).ap()
    qT_d = nc.dram_tensor("qT_s", [12, 128, T], BF16).ap()
    kT_d = nc.dram_tensor("kT_s", [12, 128, T], BF16).ap()
    v_d = nc.dram_tensor("v_s", [NT, 128, 1536], BF16).ap()
    fT_d = nc.dram_tensor("fT_s", [4, 128, T], BF16).ap()
    ffa_d = nc.dram_tensor("ffa_s", [NJ, 128, T], BF16).ap()
    cqn_d = nc.dram_tensor("cqn_s", [8, 128, T], BF16).ap()
    gl_d = nc.dram_tensor("gl_s", [4, 128, T], BF16).ap()
    kr_d = nc.dram_tensor("kr_s", [128, T], BF16).ap()
    qr_d = nc.dram_tensor("qr_s", [6, 128, T], BF16).ap()

    with ExitStack() as stack:
        P = Prog(nc, stack)

        H = P.sbuf("H", 36864, BF16)
        WS = [P.sbuf("WS0", WS_COLS, BF16), P.sbuf("WS1", WS_COLS, BF16)]
        SCR = P.sbuf("SCR", 13312, BF16)
        wrot = Rot(WS)
        modc = P.sbuf("modc", DEPTH * 96 * 2, F32)
        modv = modc.ap.rearrange("p (i f s) -> p i f s", i=DEPTH, f=96)
        c2s = P.sbuf("c2s", KC * 2, F32)
        adab = P.sbuf("adab", DEPTH * 96, F32)
        gains_s = P.sbuf("gains_s", 512, F32)
        odcw_s = P.sbuf("odcw_s", 2 * 4 * 31, F32)
        odvec_s = P.sbuf("odvec_s", 40, F32)
        ffcw_s = P.sbuf("ffcw_s", DEPTH * NJ * 3, F32)
        ffcb_s = P.sbuf("ffcb_s", DEPTH * NJ, F32)
        fing_s = P.sbuf("fing_s", KC, F32)
        ident_bf = P.sbuf("ident_bf_s", 128, BF16)
        ident_f = P.sbuf("ident_f_s", 128, F32)
        dft128 = P.sbuf("dft128_s", 256, BF16)
        onesD = P.sbuf("onesD", 128, BF16)
        ones1 = P.sbuf("ones1", 128, BF16)
        ones512 = P.sbuf("ones512", 128, BF16)
        ones512f = P.sbuf("ones512f", 128, F32)
        epsb = P.sbuf("epsb", 4, F32)
        dummy = P.sbuf("dummy_t", 4, F32)
        consts = Buf("consts")
        P.dummy = dummy

        PS = [P.psum("ps%d" % i, 512, F32) for i in range(8)]

        for dst, src in ((c2s, c2), (adab, ada_bc), (gains_s, gains), (odcw_s, od_cw), (odvec_s, od_vec),
                         (ffcw_s, ffn_cw), (ffcb_s, ffn_cb), (fing_s, fin_g), (ident_bf, ident_bf_d),
                         (ident_f, ident_f_d), (dft128, dft128_d)):
            P.dma('sp', dst.ap, src, consts, writes=[consts, dst])
        P.add('dve', lambda e: e.memset(onesD.ap, 1.0 / D), writes=[onesD])
        P.add('dve', lambda e: e.memset(ones1.ap, 1.0), writes=[ones1])
        P.add('dve', lambda e: e.memset(ones512.ap, 1.0 / 512), writes=[ones512])
        P.add('dve', lambda e: e.memset(ones512f.ap, 1.0 / 512), writes=[ones512f])
        P.add('dve', lambda e: e.memset(epsb.ap, EPS), writes=[epsb])

        def scr_carve(specs):
            if P.scr_used:
                P.end_phase()
            P.scr_used = True
            out = {}
            off = 0
            for name, cols, dt in specs:
                n16 = cols * (2 if dt == F32 else 1)
                ap = SCR.ap[:, off:off + n16]
                if dt == F32:
                    ap = ap.bitcast(F32)
                out[name] = P.region(name, ap)
                off += n16
            assert off <= 13312, off
            return out

        def h_carve(specs):
            if P.h_used:
                P.end_phase()
            P.h_used = True
            out = {}
            off = 0
            for name, cols, dt in specs:
                n16 = cols * (2 if dt == F32 else 1)
                ap = H.ap[:, off:off + n16]
                if dt == F32:
                    ap = ap.bitcast(F32)
                out[name] = P.region(name, ap)
                off += n16
            assert off <= 36864, off
            return out

        def w_load(parts, f32=False):
            slab = wrot.next()
            for off, n, src, shape in parts:
                base = slab.ap.bitcast(F32) if f32 else slab.ap
                dst = base[:, off:off + n]
                if shape is not None:
                    dst = dst.rearrange(shape[0], **shape[1])
                P.dma('sp' if f32 else 'pool', dst, src, slab, writes=[slab])
            return slab

        sc = scr_carve([("silu", KC * 2, F32)])
        silu = sc["silu"]
        P.add('act', lambda e: e.activation(out=silu.ap, in_=c2s.ap, func=AF.Silu), reads=[c2s], writes=[silu])
        siluv = silu.ap.rearrange("p (k s) -> p k s", k=KC)
        adabv = adab.ap.rearrange("p (i f) -> p i f", i=DEPTH)
        for i in range(NL):
            for nb in range(24):
                src = ada_w[i, :, nb * 512:(nb + 1) * 512].rearrange("(k p) n -> p k n", p=128)
                slab = wrot.next()
                sv = slab.ap.bitcast(F32)[:, 0:8192].rearrange("p (k n) -> p k n", k=KC)
                for q in range(4):
                    P.dma('sp', sv[:, q * 4:(q + 1) * 4, :], src[:, q * 4:(q + 1) * 4, :], slab, writes=[slab])
                pt = PS[nb % 2]
                for q in range(4):
                    for k in range(KC):
                        P.add('pe', lambda e, pt=pt, k=k, q=q, sv=sv: e.matmul(pt.ap[:, q * 2:q * 2 + 2], sv[:, k, q * 128:(q + 1) * 128],
                                                                          siluv[:, k, :], start=(k == 0), stop=(k == KC - 1)),
                              reads=[silu, slab], writes=[pt])
                P.add('dve', lambda e, pt=pt, i=i, nb=nb: e.tensor_tensor(
                    out=modv[:, i, nb * 4:nb * 4 + 4, :],
                    in0=pt.ap[:, 0:8].rearrange("p (f s) -> p f s", f=4),
                    in1=adabv[:, i, nb * 4:nb * 4 + 4].unsqueeze(2).broadcast_to([128, 4, 2]), op=ALU.add),
                    reads=[pt, adab, consts], writes=[modc])
            for part in (1, 4):
                P.add('dve', lambda e, i=i, part=part: e.tensor_scalar_add(
                    out=modv[:, i, part * 16:(part + 1) * 16, :], in0=modv[:, i, part * 16:(part + 1) * 16, :],
                    scalar1=1.0), reads=[modc], writes=[modc])
        P.end_phase(dummy)

        xreg = [[Buf("x%d_%d" % (c, b)) for b in range(9)] for c in range(KC)]

        def xregs(c, t0, n):
            return [xreg[c][b] for b in range(t0 // 256, (t0 + n) // 256)]

        aTreg = [[Buf("a%d_%d" % (c, b)) for b in range(5)] for c in range(KC)]
        state = {"xsrc": xT_in}

        hT = H.ap.rearrange("p (k t) -> p k t", k=KC)

        def modulate(i, part_sh, part_sc, hreg):
            xsrc = state["xsrc"]
            sc = scr_carve([("x0", 512, F32), ("x1", 512, F32), ("x2", 512, F32), ("x3", 512, F32),
                            ("sq0", 512, BF16), ("sq1", 512, BF16),
                            ("rs0", 512, F32), ("rs1", 512, F32), ("t0", 512, F32), ("t1", 512, F32)])
            xs = Rot([sc["x0"], sc["x1"], sc["x2"], sc["x3"]])
            sqs = Rot([sc["sq0"], sc["sq1"]])
            rss = Rot([sc["rs0"], sc["rs1"]])
            ts = Rot([sc["t0"], sc["t1"]])
            for tb, (t0, n) in enumerate(TBS):
                s = 1 if tb == 4 else 0
                ps = PS[tb % 2]
                for c in range(KC):
                    x = xs.next()
                    P.dma('sp', x.ap[:, :n], xsrc[c, :, t0:t0 + n], x, reads=xregs(c, t0, n), writes=[x])
                    sq = sqs.next()
                    P.add('act', lambda e, x=x, sq=sq: e.activation(out=sq.ap[:, :n], in_=x.ap[:, :n], func=AF.Square),
                          reads=[x], writes=[sq])
                    P.add('pe', lambda e, ps=ps, sq=sq, c=c: e.matmul(ps.ap[:, :n], onesD.ap, sq.ap[:, :n],
                                                                     start=(c == 0), stop=(c == KC - 1)),
                          reads=[sq, onesD], writes=[ps])
                rs = rss.next()
                P.add('act', lambda e, ps=ps, rs=rs: e.activation(out=rs.ap[:, :n], in_=ps.ap[:, :n], func=AF.Sqrt, bias=epsb.ap[:, 0:1], scale=1.0), reads=[ps, epsb], writes=[rs])
                P.add('dve', lambda e, ps=ps, rs=rs: e.reciprocal(out=rs.ap[:, :n], in_=rs.ap[:, :n]), reads=[rs], writes=[rs])
                for c in range(KC):
                    x = xs.next()
                    P.dma('sp', x.ap[:, :n], xsrc[c, :, t0:t0 + n], x, reads=xregs(c, t0, n), writes=[x])
                    t = ts.next()
                    P.add('dve', lambda e, x=x, t=t, rs=rs, c=c, s=s: e.scalar_tensor_tensor(
                        out=t.ap[:, :n], in0=x.ap[:, :n], scalar=modv[:, i, part_sc * 16 + c, s:s + 1],
                        in1=rs.ap[:, :n], op0=ALU.mult, op1=ALU.mult), reads=[x, rs, modc], writes=[t])
                    P.add('act', lambda e, t=t, c=c, s=s: e.activation(
                        out=hT[:, c, t0:t0 + n], in_=t.ap[:, :n], func=AF.Identity,
                        bias=modv[:, i, part_sh * 16 + c, s:s + 1], scale=1.0),
                        reads=[t, modc], writes=[hreg[c][tb]])

        def new_hreg():
            if P.h_used:
                P.end_phase()
            P.h_used = True
            return [[P.region("h%d_%d" % (c, tb)) for tb in range(5)] for c in range(KC)]

        def wslab_cols(wsrc, c0, ncols, k_chunks=KC, dst_off=0, slab=None):
            pass

        def load_cols(slab, off, wsrc, c0, ncols, kc=KC, step=4):
            dstv = slab.ap[:, off:off + kc * ncols].rearrange("p (k n) -> p k n", k=kc)
            srcv = wsrc[:, c0:c0 + ncols].rearrange("(k p) n -> p k n", p=128)
            for q in range(0, kc, step):
                P.dma('pool', dstv[:, q:q + step, :], srcv[:, q:q + step, :], slab, writes=[slab])
            return dstv

        def even_inproj(i, j, hreg):
            w = ev_w_in[j]
            sc = scr_carve([("st0", 512, BF16), ("st1", 512, BF16), ("st2", 512, BF16),
                            ("sq", 512, F32), ("zn", 512, F32), ("ss", 8, F32), ("rstd", 8, F32),
                            ("ta", 256, F32), ("tb", 256, F32), ("zr0", 512, BF16), ("zr1", 512, BF16),
                            ("cs0", 128, F32), ("cs1", 128, F32)])
            sts = Rot([sc["st0"], sc["st1"], sc["st2"]])
            zrs = Rot([sc["zr0"], sc["zr1"]])
            css = Rot([sc["cs0"], sc["cs1"]])
            sq, zn, ss, rstd, ta, tbb = sc["sq"], sc["zn"], sc["ss"], sc["rstd"], sc["ta"], sc["tb"]
            gv = gains_s.ap.rearrange("p (j q d) -> p j q d", j=2, q=2)
            psr = Rot([PS[0], PS[1], PS[2]])
            ptr = Rot([PS[3], PS[4]])
            allh = [hreg[c][tb] for c in range(KC) for tb in range(5)]
            slab = wrot.next()
            wv = load_cols(slab, 0, w, 0, 512)
            for tb, (t0, n) in enumerate(TBS):
                for m in range(4):
                    ps = psr.next()
                    for k in range(KC):
                        P.add('pe', lambda e, ps=ps, k=k, m=m, wv=wv: e.matmul(
                            ps.ap[:, :n], wv[:, k, m * 128:(m + 1) * 128], hT[:, k, t0:t0 + n],
                            start=(k == 0), stop=(k == KC - 1)), reads=[slab, hreg[k][tb]], writes=[ps])
                    st = sts.next()
                    P.add('act', lambda e, ps=ps, st=st: e.copy(out=st.ap[:, :n], in_=ps.ap[:, :n]),
                          reads=[ps], writes=[st])
                    P.dma('sp', fT_d[m, :, t0:t0 + n], st.ap[:, :n], st, reads=[st], accum=[fTreg])
            for blk in range(1, 5):
                slab = wrot.next()
                wv = load_cols(slab, 0, w, blk * 512, 512)
                isk = (blk == 4)
                for tt in range(NT):
                    ps = psr.next()
                    tbi = min(tt // 4, 4)
                    for k in range(KC):
                        P.add('pe', lambda e, ps=ps, k=k, wv=wv, tt=tt: e.matmul(
                            ps.ap, hT[:, k, tt * 128:(tt + 1) * 128], wv[:, k, :],
                            start=(k == 0), stop=(k == KC - 1)), reads=[slab, hreg[k][tbi]], writes=[ps])
                    P.add('act', lambda e, ps=ps: e.activation(out=sq.ap, in_=ps.ap, func=AF.Square),
                          reads=[ps], writes=[sq])
                    P.add('dve', lambda e: e.reduce_sum(out=ss.ap[:, 0:4], in_=sq.ap.rearrange("p (h d) -> p h d", h=4),
                                                        axis=AX.X), reads=[sq], writes=[ss])
                    P.add('act', lambda e: e.activation(out=rstd.ap[:, 0:4], in_=ss.ap[:, 0:4], func=AF.Sqrt, bias=epsb.ap[:, 0:1], scale=1.0 / 128),
                          reads=[ss, epsb], writes=[rstd])
                    P.add('dve', lambda e: e.reciprocal(out=rstd.ap[:, 0:4], in_=rstd.ap[:, 0:4]), reads=[rstd], writes=[rstd])
                    zn3 = zn.ap.rearrange("p (h d) -> p h d", h=4)
                    P.add('dve', lambda e, ps=ps, zn3=zn3: e.tensor_tensor(
                        out=zn3, in0=ps.ap.rearrange("p (h d) -> p h d", h=4),
                        in1=rstd.ap[:, 0:4].unsqueeze(2).broadcast_to([128, 4, 128]), op=ALU.mult),
                        reads=[ps, rstd], writes=[zn])
                    zr = zrs.next()
                    zr3 = zr.ap.rearrange("p (h d) -> p h d", h=4)
                    gsel = gv[:, j, 1 if isk else 0, :].unsqueeze(1).broadcast_to([128, 4, 128])
                    if tt < 16:
                        P.add('dve', lambda e, zn3=zn3, gsel=gsel: e.tensor_tensor(out=zn3, in0=zn3, in1=gsel, op=ALU.mult),
                              reads=[zn, gains_s, consts], writes=[zn])
                        cs = css.next()
                        P.dma('sp', cs.ap, ropeA_d[tt], cs, writes=[cs])
                        cosb = cs.ap[:, 0:64].unsqueeze(1).broadcast_to([128, 4, 64])
                        sinb = cs.ap[:, 64:128].unsqueeze(1).broadcast_to([128, 4, 64])
                        ta3 = ta.ap.rearrange("p (h d) -> p h d", h=4)
                        tb3 = tbb.ap.rearrange("p (h d) -> p h d", h=4)
                        x1 = zn3[:, :, 0:64]
                        x2 = zn3[:, :, 64:128]
                        P.add('dve', lambda e, x1=x1, cosb=cosb, ta3=ta3: e.tensor_tensor(out=ta3, in0=x1, in1=cosb, op=ALU.mult),
                              reads=[zn, cs], writes=[ta])
                        P.add('dve', lambda e, x2=x2, sinb=sinb, tb3=tb3: e.tensor_tensor(out=tb3, in0=x2, in1=sinb, op=ALU.mult),
                              reads=[zn, cs], writes=[tbb])
                        P.add('dve', lambda e, zr3=zr3, ta3=ta3, tb3=tb3: e.tensor_tensor(out=zr3[:, :, 0:64], in0=ta3, in1=tb3, op=ALU.subtract),
                              reads=[ta, tbb], writes=[zr])
                        P.add('dve', lambda e, x2=x2, cosb=cosb, ta3=ta3: e.tensor_tensor(out=ta3, in0=x2, in1=cosb, op=ALU.mult),
                              reads=[zn, cs], writes=[ta])
                        P.add('dve', lambda e, x1=x1, sinb=sinb, tb3=tb3: e.tensor_tensor(out=tb3, in0=x1, in1=sinb, op=ALU.mult),
                              reads=[zn, cs], writes=[tbb])
                        P.add('dve', lambda e, zr3=zr3, ta3=ta3, tb3=tb3: e.tensor_tensor(out=zr3[:, :, 64:128], in0=ta3, in1=tb3, op=ALU.add),
                              reads=[ta, tbb], writes=[zr])
                    else:
                        P.add('dve', lambda e, zn3=zn3, gsel=gsel, zr3=zr3: e.tensor_tensor(out=zr3, in0=zn3, in1=gsel, op=ALU.mult),
                              reads=[zn, gains_s, consts], writes=[zr])
                    pt = ptr.next()
                    ptb = pt.ap.bitcast(BF16)
                    for h in range(4):
                        P.add('pe', lambda e, ptb=ptb, zr=zr, h=h: e.transpose(ptb[:, h * 128:(h + 1) * 128],
                                                                             zr.ap[:, h * 128:(h + 1) * 128], ident_bf.ap),
                              reads=[zr, ident_bf, consts], writes=[pt])
                    st = sts.next()
                    P.add('act', lambda e, ptb=ptb, st=st: e.copy(out=st.ap, in_=ptb[:, 0:512]), reads=[pt], writes=[st])
                    dstT = kT_d if isk else qT_d
                    h0 = 0 if isk else (blk - 1) * 4
                    P.dma('sp', dstT[h0:h0 + 4, :, tt * 128:(tt + 1) * 128].rearrange("h p t -> p h t"),
                          st.ap.rearrange("p (h t) -> p h t", h=4), st, reads=[st], accum=[qkreg])
            slab = wrot.next()
            wv = load_cols(slab, 0, w, 2560, 512)
            for tt in range(NT):
                ps = psr.next()
                tbi = min(tt // 4, 4)
                for k in range(KC):
                    P.add('pe', lambda e, ps=ps, k=k, wv=wv, tt=tt: e.matmul(
                        ps.ap, hT[:, k, tt * 128:(tt + 1) * 128], wv[:, k, :],
                        start=(k == 0), stop=(k == KC - 1)), reads=[slab, hreg[k][tbi]], writes=[ps])
                st = sts.next()
                P.add('act', lambda e, ps=ps, st=st: e.copy(out=st.ap, in_=ps.ap), reads=[ps], writes=[st])
                P.dma('sp', v_d[tt, :, 0:512], st.ap, st, reads=[st], accum=[qkreg])

        fTreg = Buf("fTreg")
        qkreg = Buf("qkreg")
        aTall = Buf("aTall")

        def fourier(i):
            hc = h_carve([("f", 4 * T, BF16), ("A", 4 * NT * 256, BF16)])
            fb, Ab = hc["f"], hc["A"]
            fv = fb.ap.rearrange("p (g t) -> p g t", g=4)
            Av = Ab.ap.rearrange("p (g n c) -> p g n c", g=4, n=NT)
            sc = scr_carve([("st0", 512, BF16), ("st1", 512, BF16)])
            sts = Rot([sc["st0"], sc["st1"]])
            for g in range(4):
                P.dma('sp', fv[:, g, :], fT_d[g], fb, reads=[fTreg], writes=[fb])
            psr = Rot([PS[0], PS[1], PS[2]])
            for g in range(4):
                for n in range(NT):
                    ps = psr.next()
                    P.add('pe', lambda e, ps=ps, g=g, n=n: e.matmul(ps.ap[:, 0:256], fv[:, g, n * 128:(n + 1) * 128], dft128.ap,
                                                                   start=True, stop=True),
                          reads=[fb, dft128, consts], writes=[ps])
                    P.add('act', lambda e, ps=ps, g=g, n=n: e.copy(out=Av[:, g, n, :], in_=ps.ap[:, 0:256]),
                          reads=[ps], writes=[Ab])
            for kb in range(5):
                slab = wrot.next()
                if kb < 4:
                    nn, ncol, n0 = 16, 512, 0
                    for cs_ in range(2):
                        dst = slab.ap[:, cs_ * 8192:(cs_ + 1) * 8192].rearrange("p (n k) -> p n k", n=16)
                        src = dftL_d[cs_, :, kb * 512:(kb + 1) * 512].rearrange("(n p) k -> p n k", p=128)
                        for q in range(0, 16, 4):
                            P.dma('sp', dst[:, q:q + 4, :], src[:, q:q + 4, :], slab, writes=[slab])
                else:
                    nn, ncol, n0 = 2, 256, 16
                    for cs_ in range(2):
                        dst = slab.ap[:, cs_ * 8192:cs_ * 8192 + 512].rearrange("p (n k) -> p n k", n=2)
                        src = dftC_d[cs_].rearrange("(n p) k -> p n k", p=128)
                        P.dma('sp', dst, src, slab, writes=[slab])
                t0 = kb * 512
                for g in range(4):
                    ps = psr.next()
                    cnt = 0
                    for cs_ in range(2):
                        sv = slab.ap[:, cs_ * 8192:cs_ * 8192 + nn * ncol].rearrange("p (n k) -> p n k", n=nn)
                        for n in range(nn):
                            P.add('pe', lambda e, ps=ps, g=g, n=n, cs_=cs_, sv=sv, cnt=cnt, nn=nn, ncol=ncol, n0=n0: e.matmul(
                                ps.ap[:, :ncol], Av[:, g, n0 + n, cs_ * 128:(cs_ + 1) * 128], sv[:, n, :],
                                start=(cnt == 0), stop=(cnt == 2 * nn - 1)), reads=[Ab, slab], writes=[ps])
                            cnt += 1
                    st = sts.next()
                    P.add('act', lambda e, ps=ps, st=st, ncol=ncol: e.copy(out=st.ap[:, :ncol], in_=ps.ap[:, :ncol]),
                          reads=[ps], writes=[st])
                    P.dma('sp', aT_d[g, :, t0:t0 + ncol], st.ap[:, :ncol], st, reads=[st], writes=[aTreg[g][kb]])

        def attention(nheads, kparts_of, qparts_of, v_of, kvkey_of, scale, out_chunk0):
            hc = h_carve([("k0", 2 * T, BF16), ("k1", 2 * T, BF16), ("v0", NT * 128, BF16), ("v1", NT * 128, BF16)])
            kbufs = Rot([hc["k0"], hc["k1"]])
            vbufs = Rot([hc["v0"], hc["v1"]])
            sc = scr_carve([("q0", 1024, BF16), ("q1", 1024, BF16), ("q2", 1024, BF16),
                            ("p0", 512, BF16), ("p1", 512, BF16), ("p2", 512, BF16),
                            ("o0", 512, BF16), ("o1", 512, BF16), ("r0", 512, F32), ("r1", 512, F32)])
            qbufs = Rot([sc["q0"], sc["q1"], sc["q2"]])
            pbufs = Rot([sc["p0"], sc["p1"], sc["p2"]])
            obufs = Rot([sc["o0"], sc["o1"]])
            rbufs = Rot([sc["r0"], sc["r1"]])
            pss = Rot([PS[0], PS[1], PS[2]])
            pso = Rot([PS[3], PS[4]])
            psm = Rot([PS[5], PS[6]])
            blocks = []
            cur_kv = None
            for h in range(nheads):
                for tb, (t0, n) in enumerate(TBS):
                    blocks.append((h, tb, t0, n))
            st = {"kb": None, "vb": None, "cur": None}
            loaded = {}

            def do_loads(bi):
                h, tb, t0, n = blocks[bi]
                key = kvkey_of(h)
                if key != st["cur"]:
                    st["cur"] = key
                    kb = kbufs.next()
                    vb = vbufs.next()
                    st["kb"], st["vb"] = kb, vb
                    for pi, (src, rows, base) in enumerate(kparts_of(h)):
                        P.dma('sp', kb.ap[base:base + rows, pi * T:(pi + 1) * T], src, kb, reads=[qkreg], writes=[kb])
                    P.dma('sp', vb.ap.rearrange("p (t d) -> p t d", t=NT), v_of(h).rearrange("t p d -> p t d"), vb,
                          reads=[qkreg], writes=[vb])
                qb = qbufs.next()
                for pi, (src, rows, base) in enumerate(qparts_of(h)):
                    P.dma('sp', qb.ap[base:base + rows, pi * 512:pi * 512 + n], src[:, t0:t0 + n], qb,
                          reads=[qkreg], writes=[qb])
                loaded[bi] = (st["kb"], st["vb"], qb)

            def block_steps(bi):
                h, tb, t0, n = blocks[bi]
                kb, vb, qb = loaded[bi]
                kts = list(range(NT)) if tb < 4 else [16, 17]
                po = pso.next()
                pm = psm.next()
                return [(h, tb, t0, n, kt, ki == 0, ki == len(kts) - 1, kb, vb, qb, po, pm) for ki, kt in enumerate(kts)]
            nparts = len(kparts_of(0))
            pend = None

            def emit_qk(s):
                h, tb, t0, n, kt, first, last, kb, vb, qb, po, pm = s
                ps = pss.next()
                parts = kparts_of(h)
                for pi, (src, rows, base) in enumerate(parts):
                    P.add('pe', lambda e, ps=ps, kb=kb, qb=qb, pi=pi, rows=rows, base=base, kt=kt, n=n: e.matmul(
                        ps.ap[:, :n], kb.ap[base:base + rows, pi * T + kt * 128:pi * T + (kt + 1) * 128],
                        qb.ap[base:base + rows, pi * 512:pi * 512 + n],
                        start=(pi == 0), stop=(pi == len(parts) - 1)), reads=[kb, qb], writes=[ps])
                pb = pbufs.next()
                P.add('act', lambda e, ps=ps, pb=pb, n=n: e.activation(out=pb.ap[:, :n], in_=ps.ap[:, :n], func=AF.Exp, scale=scale),
                      reads=[ps], writes=[pb])
                return pb

            def emit_pv(s, pb):
                h, tb, t0, n, kt, first, last, kb, vb, qb, po, pm = s
                P.add('pe', lambda e, po=po, vb=vb, pb=pb, kt=kt, n=n, first=first, last=last: e.matmul(
                    po.ap[:, :n], vb.ap[:, kt * 128:(kt + 1) * 128], pb.ap[:, :n], start=first, stop=last),
                    reads=[vb, pb], writes=[po])
                P.add('pe', lambda e, pm=pm, pb=pb, n=n, first=first, last=last: e.matmul(
                    pm.ap[:, :n], ones1.ap, pb.ap[:, :n], start=first, stop=last), reads=[ones1, pb], writes=[pm])
                if last:
                    rb = rbufs.next()
                    ob = obufs.next()
                    P.add('dve', lambda e, pm=pm, rb=rb, n=n: e.reciprocal(out=rb.ap[:, :n], in_=pm.ap[:, :n]),
                          reads=[pm], writes=[rb])
                    P.add('dve', lambda e, po=po, rb=rb, ob=ob, n=n: e.tensor_tensor(out=ob.ap[:, :n], in0=po.ap[:, :n],
                                                                                    in1=rb.ap[:, :n], op=ALU.mult),
                          reads=[po, rb], writes=[ob])
                    P.dma('sp', aT_d[out_chunk0 + h, :, t0:t0 + n], ob.ap[:, :n], ob, reads=[ob],
                          writes=[aTreg[out_chunk0 + h][tb]])

            do_loads(0)
            for bi in range(len(blocks)):
                if bi + 1 < len(blocks):
                    do_loads(bi + 1)
                for s in block_steps(bi):
                    pb = emit_qk(s)
                    if pend is not None:
                        emit_pv(*pend)
                    pend = (s, pb)
            emit_pv(*pend)

        def outproj(i, w, gpart):
            xsrc = state["xsrc"]
            hc = h_carve([("a0", KC * 512, BF16), ("a1", KC * 512, BF16)])
            abufs = Rot([hc["a0"], hc["a1"]])
            sc = scr_carve([("x0", 512, F32), ("x1", 512, F32), ("x2", 512, F32),
                            ("o0", 512, F32), ("o1", 512, F32), ("o2", 512, F32)])
            xs = Rot([sc["x0"], sc["x1"], sc["x2"]])
            os_ = Rot([sc["o0"], sc["o1"], sc["o2"]])
            psr = Rot([PS[0], PS[1], PS[2], PS[3]])
            for mb in range(4):
                slab = wrot.next()
                wv = load_cols(slab, 0, w, mb * 512, 512)
                for tb, (t0, n) in enumerate(TBS):
                    s = 1 if tb == 4 else 0
                    ab = abufs.next()
                    abv = ab.ap.rearrange("p (k t) -> p k t", k=KC)
                    P.dma('sp', abv[:, :, :n], aT_d[:, :, t0:t0 + n].rearrange("k p t -> p k t"), ab,
                          reads=[aTreg[c][tb] for c in range(KC)], writes=[ab])
                    for m4 in range(4):
                        m = mb * 4 + m4
                        ps = psr.next()
                        for k in range(KC):
                            P.add('pe', lambda e, ps=ps, k=k, m4=m4, wv=wv, abv=abv, n=n: e.matmul(
                                ps.ap[:, :n], wv[:, k, m4 * 128:(m4 + 1) * 128], abv[:, k, :n],
                                start=(k == 0), stop=(k == KC - 1)), reads=[slab, ab], writes=[ps])
                        x = xs.next()
                        P.dma('sp', x.ap[:, :n], xsrc[m, :, t0:t0 + n], x, reads=xregs(m, t0, n), writes=[x])
                        o = os_.next()
                        P.add('dve', lambda e, ps=ps, x=x, o=o, m=m, s=s, n=n: e.scalar_tensor_tensor(
                            out=o.ap[:, :n], in0=ps.ap[:, :n], scalar=modv[:, i, gpart * 16 + m, s:s + 1],
                            in1=x.ap[:, :n], op0=ALU.mult, op1=ALU.add), reads=[ps, x, modc], writes=[o])
                        P.dma('sp', xT_d[m, :, t0:t0 + n], o.ap[:, :n], o, reads=[o], writes=xregs(m, t0, n))
            state["xsrc"] = xT_d

        GW = 2308

        def ffn_up(i, hreg):
            w = ffn_w_up[i]
            sc = scr_carve([("g0", GW, BF16), ("g1", GW, BF16), ("sg0", T, BF16), ("sg1", T, BF16),
                            ("acc0", 512, F32), ("acc1", 512, F32), ("a0", 512, BF16), ("a1", 512, BF16), ("a2", 512, BF16)])
            gbufs = Rot([sc["g0"], sc["g1"]])
            sgbufs = Rot([sc["sg0"], sc["sg1"]])
            accs = Rot([sc["acc0"], sc["acc1"]])
            asts = Rot([sc["a0"], sc["a1"], sc["a2"]])
            for gb in (sc["g0"], sc["g1"]):
                P.add('dve', lambda e, gb=gb: e.memset(gb.ap, 0.0), writes=[gb])
            cwv = ffcw_s.ap.rearrange("p (i j k) -> p i j k", i=DEPTH, j=NJ)
            cbv = ffcb_s.ap.rearrange("p (i j) -> p i j", i=DEPTH)
            psg = Rot([PS[0], PS[1], PS[2]])
            psu = Rot([PS[3], PS[4], PS[5]])

            def goff(tb, t0):
                return 1 + t0 if tb < 4 else 2051

            slabs = {}

            def get_slab(jg):
                if jg not in slabs:
                    nj = min(4, NJ - jg * 4)
                    slab = wrot.next()
                    gv_ = load_cols(slab, 0, w, jg * 512, nj * 128)
                    uv_ = load_cols(slab, 8192, w, DFF + jg * 512, nj * 128)
                    slabs[jg] = (slab, gv_, uv_)
                return slabs[jg]

            def emit_g(j):
                slab, gv_, uv_ = get_slab(j // 4)
                jj = j % 4
                gb = gbufs.next()
                sg = sgbufs.next()
                for tb, (t0, n) in enumerate(TBS):
                    ps = psg.next()
                    for k in range(KC):
                        P.add('pe', lambda e, ps=ps, k=k, jj=jj, gv_=gv_, t0=t0, n=n: e.matmul(
                            ps.ap[:, :n], gv_[:, k, jj * 128:(jj + 1) * 128], hT[:, k, t0:t0 + n],
                            start=(k == 0), stop=(k == KC - 1)), reads=[slab, hreg[k][tb]], writes=[ps])
                    off = goff(tb, t0)
                    P.add('dve', lambda e, ps=ps, gb=gb, off=off, n=n: e.tensor_copy(out=gb.ap[:, off:off + n], in_=ps.ap[:, :n]),
                          reads=[ps], writes=[gb])
                for tb, (t0, n) in enumerate(TBS):
                    off = goff(tb, t0)
                    acc = accs.next()
                    P.add('dve', lambda e, gb=gb, acc=acc, off=off, n=n, j=j: e.tensor_scalar(
                        out=acc.ap[:, :n], in0=gb.ap[:, off - 1:off - 1 + n], scalar1=cwv[:, i, j, 0:1], scalar2=None,
                        op0=ALU.mult), reads=[gb, ffcw_s, consts], writes=[acc])
                    for kk in (1, 2):
                        P.add('dve', lambda e, gb=gb, acc=acc, off=off, n=n, j=j, kk=kk: e.scalar_tensor_tensor(
                            out=acc.ap[:, :n], in0=gb.ap[:, off - 1 + kk:off - 1 + kk + n], scalar=cwv[:, i, j, kk:kk + 1],
                            in1=acc.ap[:, :n], op0=ALU.mult, op1=ALU.add), reads=[gb, acc, ffcw_s], writes=[acc])
                    P.add('act', lambda e, acc=acc, sg=sg, t0=t0, n=n, j=j: e.activation(
                        out=sg.ap[:, t0:t0 + n], in_=acc.ap[:, :n], func=AF.Silu, bias=cbv[:, i, j:j + 1], scale=1.0),
                        reads=[acc, ffcb_s, consts], writes=[sg])
                return sg

            def emit_u(j, sg):
                slab, gv_, uv_ = get_slab(j // 4)
                jj = j % 4
                for tb, (t0, n) in enumerate(TBS):
                    ps = psu.next()
                    for k in range(KC):
                        P.add('pe', lambda e, ps=ps, k=k, jj=jj, uv_=uv_, t0=t0, n=n: e.matmul(
                            ps.ap[:, :n], uv_[:, k, jj * 128:(jj + 1) * 128], hT[:, k, t0:t0 + n],
                            start=(k == 0), stop=(k == KC - 1)), reads=[slab, hreg[k][tb]], writes=[ps])
                    a = asts.next()
                    P.add('dve', lambda e, ps=ps, sg=sg, a=a, t0=t0, n=n: e.tensor_tensor(
                        out=a.ap[:, :n], in0=ps.ap[:, :n], in1=sg.ap[:, t0:t0 + n], op=ALU.mult),
                        reads=[ps, sg], writes=[a])
                    P.dma('sp', ffa_d[j, :, t0:t0 + n], a.ap[:, :n], a, reads=[a], accum=[ffareg])

            pend = None
            for j in range(NJ):
                sg = emit_g(j)
                if pend is not None:
                    emit_u(*pend)
                pend = (j, sg)
            emit_u(*pend)

        ffareg = Buf("ffareg")

        def ffn_down(i, gpart):
            w = ffn_w_down[i]
            xsrc = state["xsrc"]
            JS = 29
            hc = h_carve([("a0", NJ * 512, BF16), ("a1h", JS * 512, BF16)])
            sc = scr_carve([("a1s", (NJ - JS) * 512, BF16),
                            ("x0", 512, F32), ("x1", 512, F32), ("x2", 512, F32),
                            ("o0", 512, F32), ("o1", 512, F32), ("o2", 512, F32)])
            a0v = hc["a0"].ap.rearrange("p (j t) -> p j t", j=NJ)
            a1hv = hc["a1h"].ap.rearrange("p (j t) -> p j t", j=JS)
            a1sv = sc["a1s"].ap.rearrange("p (j t) -> p j t", j=NJ - JS)
            abufs = Rot([[(hc["a0"], a0v, 0, NJ)], [(hc["a1h"], a1hv, 0, JS), (sc["a1s"], a1sv, JS, NJ)]])
            xs = Rot([sc["x0"], sc["x1"], sc["x2"]])
            os_ = Rot([sc["o0"], sc["o1"], sc["o2"]])
            psr = Rot([PS[0], PS[1], PS[2], PS[3]])
            for mb in range(4):
                slab = wrot.next()
                wv = slab.ap[:, 0:NJ * 512].rearrange("p (j n) -> p j n", j=NJ)
                srcv = w[:, mb * 512:(mb + 1) * 512].rearrange("(j p) n -> p j n", p=128)
                for q in range(0, NJ, 4):
                    q1 = min(q + 4, NJ)
                    P.dma('pool', wv[:, q:q1, :], srcv[:, q:q1, :], slab, writes=[slab])
                for tb, (t0, n) in enumerate(TBS):
                    s = 1 if tb == 4 else 0
                    parts = abufs.next()
                    for (buf, view, j0, j1) in parts:
                        for q in range(j0, j1, 11):
                            q1 = min(q + 11, j1)
                            P.dma('sp', view[:, q - j0:q1 - j0, :n], ffa_d[q:q1, :, t0:t0 + n].rearrange("j p t -> p j t"), buf,
                                  reads=[ffareg], writes=[buf])
                    for m4 in range(4):
                        m = mb * 4 + m4
                        ps = psr.next()
                        for (buf, view, j0, j1) in parts:
                            for jx in range(j0, j1):
                                P.add('pe', lambda e, ps=ps, jx=jx, j0=j0, m4=m4, wv=wv, view=view, n=n: e.matmul(
                                    ps.ap[:, :n], wv[:, jx, m4 * 128:(m4 + 1) * 128], view[:, jx - j0, :n],
                                    start=(jx == 0), stop=(jx == NJ - 1)), reads=[slab, buf], writes=[ps])
                        x = xs.next()
                        P.dma('sp', x.ap[:, :n], xsrc[m, :, t0:t0 + n], x, reads=xregs(m, t0, n), writes=[x])
                        o = os_.next()
                        P.add('dve', lambda e, ps=ps, x=x, o=o, m=m, s=s, n=n: e.scalar_tensor_tensor(
                            out=o.ap[:, :n], in0=ps.ap[:, :n], scalar=modv[:, i, gpart * 16 + m, s:s + 1],
                            in1=x.ap[:, :n], op0=ALU.mult, op1=ALU.add), reads=[ps, x, modc], writes=[o])
                        P.dma('sp', xT_d[m, :, t0:t0 + n], o.ap[:, :n], o, reads=[o], writes=xregs(m, t0, n))

        def odd_inproj(i, j, hreg):
            w = od_w_in[j]
            ov = odvec_s.ap.rearrange("p (j v c) -> p j v c", j=2, v=5)
            sc = scr_carve([("stg", 4 * 512, F32), ("sq", 4 * 512, BF16), ("rs", 512, F32),
                            ("o0", 512, BF16), ("o1", 512, BF16), ("o2", 512, BF16),
                            ("cs", 512, F32), ("sn", 512, F32), ("t1", 512, F32), ("t2", 512, F32)])
            stg, sqb, rs = sc["stg"], sc["sq"], sc["rs"]
            outs = Rot([sc["o0"], sc["o1"], sc["o2"]])
            sgm, csb, snb, t1, t2 = sc["t1"], sc["cs"], sc["sn"], sc["t1"], sc["t2"]
            stgv = stg.ap.rearrange("p (c t) -> p c t", c=4)
            sqv = sqb.ap.rearrange("p (c t) -> p c t", c=4)
            psr = Rot([PS[0], PS[1], PS[2], PS[3], PS[4], PS[5]])
            slab = wrot.next()
            av_ = load_cols(slab, 0, w, 0, 512)
            gv_ = load_cols(slab, 8192, w, 512, 512)
            for tb, (t0, n) in enumerate(TBS):
                for c in range(4):
                    pa = psr.next()
                    pg = psr.next()
                    for (pp, vv) in ((pa, av_), (pg, gv_)):
                        for k in range(KC):
                            P.add('pe', lambda e, pp=pp, vv=vv, k=k, c=c, t0=t0, n=n: e.matmul(
                                pp.ap[:, :n], vv[:, k, c * 128:(c + 1) * 128], hT[:, k, t0:t0 + n],
                                start=(k == 0), stop=(k == KC - 1)), reads=[slab, hreg[k][tb]], writes=[pp])
                    P.add('act', lambda e, pg=pg, n=n: e.activation(out=sgm.ap[:, :n], in_=pg.ap[:, :n], func=AF.Sigmoid),
                          reads=[pg], writes=[sgm])
                    o = outs.next()
                    P.add('dve', lambda e, pa=pa, o=o, n=n: e.tensor_tensor(out=o.ap[:, :n], in0=pa.ap[:, :n], in1=sgm.ap[:, :n], op=ALU.mult),
                          reads=[pa, sgm], writes=[o])
                    P.dma('sp', gl_d[c, :, t0:t0 + n], o.ap[:, :n], o, reads=[o], accum=[glreg])
            if stop == 'ip_a':
                return
            for which in range(2):
                if stop == 'ip_b1' and which == 1:
                    return
                slab = wrot.next()
                wv = load_cols(slab, 0, w, 1024 + which * 512, 512)
                for tb, (t0, n) in enumerate(TBS):
                    pss_ = psr.next()
                    for c in range(4):
                        ps = psr.next()
                        for k in range(KC):
                            P.add('pe', lambda e, ps=ps, wv=wv, k=k, c=c, t0=t0, n=n: e.matmul(
                                ps.ap[:, :n], wv[:, k, c * 128:(c + 1) * 128], hT[:, k, t0:t0 + n],
                                start=(k == 0), stop=(k == KC - 1)), reads=[slab, hreg[k][tb]], writes=[ps])
                        P.add('dve', lambda e, ps=ps, c=c, n=n: e.tensor_copy(out=stgv[:, c, :n], in_=ps.ap[:, :n]),
                              reads=[ps], writes=[stg])
                        P.add('act', lambda e, c=c, n=n: e.activation(out=sqv[:, c, :n], in_=stgv[:, c, :n], func=AF.Square),
                              reads=[stg], writes=[sqb])
                    if stop == 'ip_b2':
                        continue
                    for c in range(4):
                        P.add('pe', lambda e, pss_=pss_, c=c, n=n: e.matmul(pss_.ap[:, :n], ones512.ap, sqv[:, c, :n],
                                                                           start=(c == 0), stop=(c == 3)),
                              reads=[sqb, ones512], writes=[pss_])
                    P.add('act', lambda e, pss_=pss_, n=n: e.activation(out=rs.ap[:, :n], in_=pss_.ap[:, :n], func=AF.Sqrt, bias=epsb.ap[:, 0:1], scale=1.0), reads=[pss_, epsb], writes=[rs])
                    P.add('dve', lambda e, pss_=pss_, n=n: e.reciprocal(out=rs.ap[:, :n], in_=rs.ap[:, :n]), reads=[rs], writes=[rs])
                    for c in range(4):
                        o = outs.next()
                        P.add('dve', lambda e, o=o, c=c, n=n, which=which: e.scalar_tensor_tensor(
                            out=o.ap[:, :n], in0=stgv[:, c, :n], scalar=ov[:, j, 3 + which, c:c + 1], in1=rs.ap[:, :n],
                            op0=ALU.mult, op1=ALU.mult), reads=[stg, rs, odvec_s, consts], writes=[o])
                        P.dma('sp', cqn_d[which * 4 + c, :, t0:t0 + n], o.ap[:, :n], o, reads=[o], accum=[cqnreg])
            if stop in ('ip_b', 'ip_b2'):
                return
            slab = wrot.next()
            krv = slab.ap[:, 0:KC * 256].rearrange("p (k n) -> p k n", k=KC)
            wsrc = w.rearrange("(k p) n -> p k n", p=128)
            for (d0, s0, nn_) in ((0, 2048, 64), (64, 2048, 64), (128, 2080, 32), (160, 2048, 32), (192, 2080, 32), (224, 2048, 32)):
                for q in range(0, KC, 4):
                    P.dma('pool', krv[:, q:q + 4, d0:d0 + nn_], wsrc[:, q:q + 4, s0:s0 + nn_], slab, writes=[slab])
            for tb, (t0, n) in enumerate(TBS):
                pk = psr.next()
                pr = psr.next()
                for (pp, c0) in ((pk, 0), (pr, 128)):
                    for k in range(KC):
                        P.add('pe', lambda e, pp=pp, c0=c0, k=k, t0=t0, n=n: e.matmul(
                            pp.ap[:, :n], krv[:, k, c0:c0 + 128], hT[:, k, t0:t0 + n],
                            start=(k == 0), stop=(k == KC - 1)), reads=[slab, hreg[k][tb]], writes=[pp])
                P.dma('sp', csb.ap[:, :n], ropeM_d[0, :, t0:t0 + n], csb, writes=[csb])
                P.dma('sp', snb.ap[:, :n], ropeM_d[1, :, t0:t0 + n], snb, writes=[snb])
                P.add('dve', lambda e, pk=pk, n=n: e.tensor_tensor(out=t1.ap[:, :n], in0=pk.ap[:, :n], in1=csb.ap[:, :n], op=ALU.mult),
                      reads=[pk, csb], writes=[t1])
                P.add('dve', lambda e, pr=pr, n=n: e.tensor_tensor(out=t2.ap[:, :n], in0=pr.ap[:, :n], in1=snb.ap[:, :n], op=ALU.mult),
                      reads=[pr, snb], writes=[t2])
                o = outs.next()
                P.add('dve', lambda e, o=o, n=n: e.tensor_tensor(out=o.ap[:, :n], in0=t1.ap[:, :n], in1=t2.ap[:, :n], op=ALU.add),
                      reads=[t1, t2], writes=[o])
                P.dma('sp', kr_d[:, t0:t0 + n], o.ap[:, :n], o, reads=[o], accum=[qkreg])

        glreg = Buf("glreg")
        cqnreg = Buf("cqnreg")

        def odd_upproj(i, j):
            hc = h_carve([("cq", 4 * T, BF16), ("ckv", 4 * T, BF16)])
            cqb, ckvb = hc["cq"], hc["ckv"]
            cqv = cqb.ap.rearrange("p (c t) -> p c t", c=4)
            ckvv = ckvb.ap.rearrange("p (c t) -> p c t", c=4)
            for c in range(4):
                P.dma('sp', cqv[:, c, :], cqn_d[c], cqb, reads=[cqnreg], writes=[cqb])
                P.dma('sp', ckvv[:, c, :], cqn_d[4 + c], ckvb, reads=[cqnreg], writes=[ckvb])
            sc = scr_carve([("o0", 512, BF16), ("o1", 512, BF16), ("o2", 512, BF16), ("o3", 512, BF16),
                            ("cs", 512, F32), ("sn", 512, F32), ("t1", 512, F32), ("t2", 512, F32)])
            outs = Rot([sc["o0"], sc["o1"], sc["o2"], sc["o3"]])
            csb, snb, t1, t2 = sc["cs"], sc["sn"], sc["t1"], sc["t2"]
            psr = Rot([PS[0], PS[1], PS[2], PS[3], PS[4], PS[5]])
            slab = wrot.next()
            wq = od_w_uq[j].rearrange("(k p) (h d) -> p k h d", p=128, d=192)
            qn = slab.ap[:, 0:4 * 1536].rearrange("p (k h d) -> p k h d", k=4, h=12)
            qr = slab.ap[:, 6144:6144 + 4 * 768].rearrange("p (k h d) -> p k h d", k=4, h=12)
            qx = slab.ap[:, 9216:9216 + 4 * 768].rearrange("p (k h d) -> p k h d", k=4, h=12)
            for k in range(4):
                for h0 in range(0, 12, 4):
                    P.dma('pool', qn[:, k, h0:h0 + 4, :], wq[:, k, h0:h0 + 4, 0:128], slab, writes=[slab])
                    P.dma('pool', qr[:, k, h0:h0 + 4, :], wq[:, k, h0:h0 + 4, 128:192], slab, writes=[slab])
                    P.dma('pool', qx[:, k, h0:h0 + 4, 0:32], wq[:, k, h0:h0 + 4, 160:192], slab, writes=[slab])
                    P.dma('pool', qx[:, k, h0:h0 + 4, 32:64], wq[:, k, h0:h0 + 4, 128:160], slab, writes=[slab])
            qnf = slab.ap[:, 0:6144].rearrange("p (k n) -> p k n", k=4)
            qrf = slab.ap[:, 6144:9216].rearrange("p (k n) -> p k n", k=4)
            qxf = slab.ap[:, 9216:12288].rearrange("p (k n) -> p k n", k=4)
            for tb, (t0, n) in enumerate(TBS):
                for h in range(12):
                    ps = psr.next()
                    for k in range(4):
                        P.add('pe', lambda e, ps=ps, k=k, h=h, t0=t0, n=n: e.matmul(
                            ps.ap[:, :n], qnf[:, k, h * 128:(h + 1) * 128], cqv[:, k, t0:t0 + n],
                            start=(k == 0), stop=(k == 3)), reads=[slab, cqb], writes=[ps])
                    o = outs.next()
                    P.add('act', lambda e, ps=ps, o=o, n=n: e.copy(out=o.ap[:, :n], in_=ps.ap[:, :n]), reads=[ps], writes=[o])
                    P.dma('sp', qT_d[h, :, t0:t0 + n], o.ap[:, :n], o, reads=[o], accum=[qkreg])
                P.dma('sp', csb.ap[:, :n], ropeM_d[0, :, t0:t0 + n], csb, writes=[csb])
                P.dma('sp', snb.ap[:, :n], ropeM_d[1, :, t0:t0 + n], snb, writes=[snb])
                for hp in range(6):
                    pk = psr.next()
                    pr = psr.next()
                    for (pp, vv) in ((pk, qrf), (pr, qxf)):
                        for k in range(4):
                            P.add('pe', lambda e, pp=pp, vv=vv, k=k, hp=hp, t0=t0, n=n: e.matmul(
                                pp.ap[:, :n], vv[:, k, hp * 128:(hp + 1) * 128], cqv[:, k, t0:t0 + n],
                                start=(k == 0), stop=(k == 3)), reads=[slab, cqb], writes=[pp])
                    P.add('dve', lambda e, pk=pk, n=n: e.tensor_tensor(out=t1.ap[:, :n], in0=pk.ap[:, :n], in1=csb.ap[:, :n], op=ALU.mult),
                          reads=[pk, csb], writes=[t1])
                    P.add('dve', lambda e, pr=pr, n=n: e.tensor_tensor(out=t2.ap[:, :n], in0=pr.ap[:, :n], in1=snb.ap[:, :n], op=ALU.mult),
                          reads=[pr, snb], writes=[t2])
                    o = outs.next()
                    P.add('dve', lambda e, o=o, n=n: e.tensor_tensor(out=o.ap[:, :n], in0=t1.ap[:, :n], in1=t2.ap[:, :n], op=ALU.add),
                          reads=[t1, t2], writes=[o])
                    P.dma('sp', qr_d[hp, :, t0:t0 + n], o.ap[:, :n], o, reads=[o], accum=[qkreg])
            slab = wrot.next()
            wkv = od_w_ukv[j].rearrange("(k p) (h d) -> p k h d", p=128, d=256)
            kn = slab.ap[:, 0:6144].rearrange("p (k h d) -> p k h d", k=4, h=12)
            vv_ = slab.ap[:, 6144:12288].rearrange("p (k h d) -> p k h d", k=4, h=12)
            for k in range(4):
                for h0 in range(0, 12, 4):
                    P.dma('pool', kn[:, k, h0:h0 + 4, :], wkv[:, k, h0:h0 + 4, 0:128], slab, writes=[slab])
                    P.dma('pool', vv_[:, k, h0:h0 + 4, :], wkv[:, k, h0:h0 + 4, 128:256], slab, writes=[slab])
            knf = slab.ap[:, 0:6144].rearrange("p (k n) -> p k n", k=4)
            vf = slab.ap[:, 6144:12288].rearrange("p (k n) -> p k n", k=4)
            for tb, (t0, n) in enumerate(TBS):
                for h in range(12):
                    ps = psr.next()
                    for k in range(4):
                        P.add('pe', lambda e, ps=ps, k=k, h=h, t0=t0, n=n: e.matmul(
                            ps.ap[:, :n], knf[:, k, h * 128:(h + 1) * 128], ckvv[:, k, t0:t0 + n],
                            start=(k == 0), stop=(k == 3)), reads=[slab, ckvb], writes=[ps])
                    o = outs.next()
                    P.add('act', lambda e, ps=ps, o=o, n=n: e.copy(out=o.ap[:, :n], in_=ps.ap[:, :n]), reads=[ps], writes=[o])
                    P.dma('sp', kT_d[h, :, t0:t0 + n], o.ap[:, :n], o, reads=[o], accum=[qkreg])
            for tt in range(NT):
                for vg in range(3):
                    ps = psr.next()
                    for k in range(4):
                        P.add('pe', lambda e, ps=ps, k=k, vg=vg, tt=tt: e.matmul(
                            ps.ap, ckvv[:, k, tt * 128:(tt + 1) * 128], vf[:, k, vg * 512:(vg + 1) * 512],
                            start=(k == 0), stop=(k == 3)), reads=[slab, ckvb], writes=[ps])
                    o = outs.next()
                    P.add('act', lambda e, ps=ps, o=o: e.copy(out=o.ap, in_=ps.ap), reads=[ps], writes=[o])
                    P.dma('sp', v_d[tt, :, vg * 512:(vg + 1) * 512], o.ap, o, reads=[o], accum=[qkreg])

        GP = 15
        LW = L + 2 * GP
        CW = CL + 2 * GP

        def odd_conv(i, j):
            hc = h_carve([("gl", 4 * (LW + CW), BF16), ("u", 4 * T, F32)])
            glb, ub = hc["gl"], hc["u"]
            glv = glb.ap.rearrange("p (c t) -> p c t", c=4)
            uv = ub.ap.rearrange("p (c t) -> p c t", c=4)
            sc = scr_carve([("acc0", 512, F32), ("acc1", 512, F32), ("usq", 4 * 512, F32), ("mu", 512, F32), ("var", 512, F32),
                            ("t", 512, F32), ("o0", 512, BF16), ("o1", 512, BF16)])
            usq, mu, var, tt_ = sc["usq"], sc["mu"], sc["var"], sc["t"]
            outs = Rot([sc["o0"], sc["o1"]])
            usqv = usq.ap.rearrange("p (c t) -> p c t", c=4)
            ov = odvec_s.ap.rearrange("p (j v c) -> p j v c", j=2, v=5)
            cwv = odcw_s.ap.rearrange("p (j c k) -> p j c k", j=2, c=4)
            P.add('dve', lambda e: e.memset(glb.ap, 0.0), writes=[glb])
            for c in range(4):
                P.dma('sp', glv[:, c, GP:GP + L], gl_d[c, :, 0:L], glb, reads=[glreg], writes=[glb])
                P.dma('sp', glv[:, c, LW + GP:LW + GP + CL], gl_d[c, :, L:T], glb, reads=[glreg], writes=[glb])
            accs = Rot([sc["acc0"], sc["acc1"]])
            for c in range(4):
                for tb, (t0, n) in enumerate(TBS):
                    base = t0 if tb < 4 else LW
                    acc = accs.next()
                    P.add('dve', lambda e, acc=acc, c=c, base=base, n=n: e.tensor_scalar(
                        out=acc.ap[:, :n], in0=glv[:, c, base:base + n], scalar1=cwv[:, j, c, 0:1], scalar2=None, op0=ALU.mult),
                        reads=[glb, odcw_s, consts], writes=[acc])
                    for k in range(1, 31):
                        P.add('dve', lambda e, acc=acc, c=c, k=k, base=base, n=n: e.scalar_tensor_tensor(
                            out=acc.ap[:, :n], in0=glv[:, c, base + k:base + k + n], scalar=cwv[:, j, c, k:k + 1],
                            in1=acc.ap[:, :n], op0=ALU.mult, op1=ALU.add), reads=[glb, acc, odcw_s], writes=[acc])
                    P.add('act', lambda e, acc=acc, c=c, t0=t0, n=n: e.activation(
                        out=uv[:, c, t0:t0 + n], in_=acc.ap[:, :n], func=AF.Identity, bias=ov[:, j, 0, c:c + 1], scale=1.0),
                        reads=[acc, odvec_s, consts], writes=[ub])
            pm = Rot([PS[3], PS[4]])
            pv = Rot([PS[5], PS[6]])
            for tb, (t0, n) in enumerate(TBS):
                p1 = pm.next()
                p2 = pv.next()
                for c in range(4):
                    P.add('act', lambda e, c=c, t0=t0, n=n: e.activation(out=usqv[:, c, :n], in_=uv[:, c, t0:t0 + n], func=AF.Square),
                          reads=[ub], writes=[usq])
                for c in range(4):
                    P.add('pe', lambda e, p1=p1, c=c, t0=t0, n=n: e.matmul(p1.ap[:, :n], ones512f.ap, uv[:, c, t0:t0 + n],
                                                                          start=(c == 0), stop=(c == 3)),
                          reads=[ub, ones512f], writes=[p1])
                for c in range(4):
                    P.add('pe', lambda e, p2=p2, c=c, n=n: e.matmul(p2.ap[:, :n], ones512f.ap, usqv[:, c, :n],
                                                                   start=(c == 0), stop=(c == 3)),
                          reads=[usq, ones512f], writes=[p2])
                P.add('dve', lambda e, p1=p1, n=n: e.tensor_copy(out=mu.ap[:, :n], in_=p1.ap[:, :n]), reads=[p1], writes=[mu])
                P.add('dve', lambda e, n=n: e.tensor_tensor(out=var.ap[:, :n], in0=mu.ap[:, :n], in1=mu.ap[:, :n], op=ALU.mult),
                      reads=[mu], writes=[var])
                P.add('dve', lambda e, p2=p2, n=n: e.tensor_tensor(out=var.ap[:, :n], in0=p2.ap[:, :n], in1=var.ap[:, :n], op=ALU.subtract),
                      reads=[p2, var], writes=[var])
                P.add('act', lambda e, n=n: e.activation(out=var.ap[:, :n], in_=var.ap[:, :n], func=AF.Sqrt, bias=epsb.ap[:, 0:1], scale=1.0), reads=[var, epsb], writes=[var])
                P.add('dve', lambda e, n=n: e.reciprocal(out=var.ap[:, :n], in_=var.ap[:, :n]), reads=[var], writes=[var])
                for c in range(4):
                    P.add('dve', lambda e, c=c, t0=t0, n=n: e.tensor_tensor(out=tt_.ap[:, :n], in0=uv[:, c, t0:t0 + n], in1=mu.ap[:, :n], op=ALU.subtract),
                          reads=[ub, mu], writes=[tt_])
                    P.add('dve', lambda e, n=n: e.tensor_tensor(out=tt_.ap[:, :n], in0=tt_.ap[:, :n], in1=var.ap[:, :n], op=ALU.mult),
                          reads=[tt_, var], writes=[tt_])
                    o = outs.next()
                    P.add('act', lambda e, o=o, c=c, n=n: e.activation(out=o.ap[:, :n], in_=tt_.ap[:, :n], func=AF.Silu,
                                                                      bias=ov[:, j, 2, c:c + 1], scale=ov[:, j, 1, c:c + 1]),
                          reads=[tt_, odvec_s, consts], writes=[o])
                    P.dma('sp', aT_d[c, :, t0:t0 + n], o.ap[:, :n], o, reads=[o], writes=[aTreg[c][tb]])

        def final_norm():
            xsrc = state["xsrc"]
            sc = scr_carve([("x0", 512, F32), ("x1", 512, F32), ("x2", 512, F32), ("x3", 512, F32),
                            ("sq0", 512, BF16), ("sq1", 512, BF16), ("rs0", 512, F32), ("rs1", 512, F32),
                            ("t0", 512, F32), ("t1", 512, F32), ("t2", 512, F32)])
            xs = Rot([sc["x0"], sc["x1"], sc["x2"], sc["x3"]])
            sqs = Rot([sc["sq0"], sc["sq1"]])
            rss = Rot([sc["rs0"], sc["rs1"]])
            ts = Rot([sc["t0"], sc["t1"], sc["t2"]])
            for tb, (t0, n) in enumerate(TBS[:4]):
                ps = PS[tb % 2]
                for c in range(KC):
                    x = xs.next()
                    P.dma('sp', x.ap[:, :n], xsrc[c, :, t0:t0 + n], x, reads=xregs(c, t0, n), writes=[x])
                    sq = sqs.next()
                    P.add('act', lambda e, x=x, sq=sq, n=n: e.activation(out=sq.ap[:, :n], in_=x.ap[:, :n], func=AF.Square),
                          reads=[x], writes=[sq])
                    P.add('pe', lambda e, ps=ps, sq=sq, c=c, n=n: e.matmul(ps.ap[:, :n], onesD.ap, sq.ap[:, :n],
                                                                          start=(c == 0), stop=(c == KC - 1)),
                          reads=[sq, onesD], writes=[ps])
                rs = rss.next()
                P.add('act', lambda e, ps=ps, rs=rs, n=n: e.activation(out=rs.ap[:, :n], in_=ps.ap[:, :n], func=AF.Sqrt, bias=epsb.ap[:, 0:1], scale=1.0), reads=[ps, epsb], writes=[rs])
                P.add('dve', lambda e, ps=ps, rs=rs, n=n: e.reciprocal(out=rs.ap[:, :n], in_=rs.ap[:, :n]), reads=[rs], writes=[rs])
                for c in range(KC):
                    x = xs.next()
                    P.dma('sp', x.ap[:, :n], xsrc[c, :, t0:t0 + n], x, reads=xregs(c, t0, n), writes=[x])
                    t = ts.next()
                    P.add('dve', lambda e, x=x, t=t, rs=rs, c=c, n=n: e.scalar_tensor_tensor(
                        out=t.ap[:, :n], in0=x.ap[:, :n], scalar=fing_s.ap[:, c:c + 1], in1=rs.ap[:, :n],
                        op0=ALU.mult, op1=ALU.mult), reads=[x, rs, fing_s, consts], writes=[t])
                    P.dma('sp', out_d[c, :, t0:t0 + n], t.ap[:, :n], t, reads=[t], accum=[outreg])

        def copy_out():
            xsrc = state["xsrc"]
            sc = scr_carve([("x0", 512, F32), ("x1", 512, F32), ("x2", 512, F32)])
            xs = Rot([sc["x0"], sc["x1"], sc["x2"]])
            for tb, (t0, n) in enumerate(TBS[:4]):
                for c in range(KC):
                    x = xs.next()
                    P.dma('sp', x.ap[:, :n], xsrc[c, :, t0:t0 + n], x, reads=xregs(c, t0, n), writes=[x])
                    P.dma('sp', out_d[c, :, t0:t0 + n], x.ap[:, :n], x, reads=[x], accum=[outreg])

        outreg = Buf("outreg")

        for i in range(NL):
            j = i // 2
            hreg = new_hreg()
            modulate(i, 0, 1, hreg)
            if i % 2 == 0:
                even_inproj(i, j, hreg)
                P.end_phase(dummy)
                fourier(i)
                P.end_phase(dummy)
                attention(12,
                          lambda h: [(kT_d[h // 3], 128, 0)],
                          lambda h: [(qT_d[h], 128, 0)],
                          lambda h: v_d[:, :, (h // 3) * 128:(h // 3 + 1) * 128],
                          lambda h: h // 3, 128 ** -0.5, 4)
                P.end_phase(dummy)
                outproj(i, ev_w_out[j], 2)
            else:
                if stop == 'mod':
                    P.end_phase(dummy)
                    break
                odd_inproj(i, j, hreg)
                P.end_phase(dummy)
                if stop in ('inproj', 'ip_a', 'ip_b', 'ip_b1', 'ip_b2'):
                    break
                odd_upproj(i, j)
                P.end_phase(dummy)
                if stop == 'upproj':
                    break
                odd_conv(i, j)
                P.end_phase(dummy)
                if stop == 'conv':
                    break
                attention(12,
                          lambda h: [(kT_d[h], 128, 0), (kr_d[(h % 2) * 64:(h % 2) * 64 + 64, :], 64, (h % 2) * 64)],
                          lambda h: [(qT_d[h], 128, 0), (qr_d[h // 2, (h % 2) * 64:(h % 2) * 64 + 64, :], 64, (h % 2) * 64)],
                          lambda h: v_d[:, :, h * 128:(h + 1) * 128],
                          lambda h: h, 192 ** -0.5, 4)
                P.end_phase(dummy)
                if stop == 'attn':
                    break
                outproj(i, od_w_out[j], 2)
                if stop == 'outproj':
                    P.end_phase(dummy)
                    break
            P.end_phase(dummy)
            hreg = new_hreg()
            modulate(i, 3, 4, hreg)
            ffn_up(i, hreg)
            P.end_phase(dummy)
            ffn_down(i, 5)
            P.end_phase(dummy)
        if final:
            final_norm()
        else:
            copy_out()
        P.add('sp', lambda e: e.nop(), reads=[outreg])
        P.emit()
    return nc, dt_in


def _host_consts():
    bf = ml_dtypes.bfloat16
    c = {}
    c["ident_bf"] = np.eye(128, dtype=np.float32).astype(bf)
    c["ident_f"] = np.eye(128, dtype=np.float32)
    n = np.arange(128)
    a = 2 * np.pi * np.outer(n, n) / 128.0
    c["dft128"] = (np.concatenate([np.cos(a), np.sin(a)], axis=1) / np.sqrt(128.0)).astype(np.float32).astype(bf)

    def dft(N):
        k = np.arange(N, dtype=np.int64)
        ph = (np.outer(k, k) % N).astype(np.float64) * (2 * np.pi / N)
        return np.stack([np.cos(ph), -np.sin(ph)], 0) / np.sqrt(float(N))
    c["dftL"] = dft(L).astype(np.float32).astype(bf)
    c["dftC"] = dft(CL).astype(np.float32).astype(bf)

    def tables(rot_dim):
        rows = L // 64
        row = np.repeat(np.arange(rows, dtype=np.float32), 64)
        col = np.tile(np.arange(64, dtype=np.float32), rows)
        nn = rot_dim // 4
        inv = np.power(np.float32(10000.0), -np.arange(nn, dtype=np.float32) / nn).astype(np.float32)
        ang = np.concatenate([row[:, None] * inv, col[:, None] * inv], axis=-1).astype(np.float32)
        return np.cos(ang).astype(np.float32), np.sin(ang).astype(np.float32)
    ca, sa = tables(128)
    c["ropeA"] = np.concatenate([ca, sa], axis=1).reshape(16, 128, 128).astype(np.float32)
    cm, sm = tables(64)
    cosT = np.ones((64, T), np.float32)
    sinT = np.zeros((64, T), np.float32)
    cosT[:32, :L] = cm.T
    cosT[32:, :L] = cm.T
    sinT[:32, :L] = -sm.T
    sinT[32:, :L] = sm.T
    c["ropeM"] = np.stack([np.concatenate([cosT, cosT], 0), np.concatenate([sinT, sinT], 0)], 0)
    return c


def _col(v, nchunk):
    return np.ascontiguousarray(v.reshape(nchunk, 128).T)


_CACHE = {}
ACTIVE = [0, 1, 4, 5]


def make_in_maps(inputs, ncores=8):
    f = lambda a: np.ascontiguousarray(np.asarray(a, dtype=np.float32))
    x, c, ctx, c_ctx = f(inputs["x"]), f(inputs["c"]), f(inputs["ctx"]), f(inputs["c_ctx"])
    shared = dict(_host_consts())
    shared["ada_w"] = f(inputs["ada_w"])
    ab = f(inputs["ada_b"])
    shared["ada_bc"] = np.ascontiguousarray(ab.reshape(DEPTH, 96, 128).transpose(2, 0, 1).reshape(128, DEPTH * 96))
    shared["ev_w_in"] = f(inputs["ev_w_in"])
    shared["ev_w_out"] = f(inputs["ev_w_out"])
    g = np.stack([f(inputs["ev_q_gain"]), f(inputs["ev_k_gain"])], 1)
    shared["gains"] = np.ascontiguousarray(np.broadcast_to(g.reshape(1, 512), (128, 512)))
    shared["od_w_in"] = f(inputs["od_w_in"])
    shared["od_w_uq"] = f(inputs["od_w_uq"])
    shared["od_w_ukv"] = f(inputs["od_w_ukv"])
    shared["od_w_out"] = f(inputs["od_w_out"])
    cw = f(inputs["od_conv_w"])
    shared["od_cw"] = np.ascontiguousarray(cw.reshape(2, 31, 4, 128).transpose(3, 0, 2, 1).reshape(128, 2 * 4 * 31))
    vecs = np.stack([f(inputs[k]) for k in ("od_conv_b", "od_ln_g", "od_ln_b", "od_q_norm", "od_kv_norm")], 1)
    shared["od_vec"] = np.ascontiguousarray(vecs.reshape(2, 5, 4, 128).transpose(3, 0, 1, 2).reshape(128, 40))
    shared["ffn_w_up"] = f(inputs["ffn_w_up"])
    shared["ffn_w_down"] = f(inputs["ffn_w_down"])
    fcw = f(inputs["ffn_conv_w"])
    shared["ffn_cw"] = np.ascontiguousarray(fcw.reshape(DEPTH, 3, NJ, 128).transpose(3, 0, 2, 1).reshape(128, DEPTH * NJ * 3))
    fcb = f(inputs["ffn_conv_b"])
    shared["ffn_cb"] = np.ascontiguousarray(fcb.reshape(DEPTH, NJ, 128).transpose(2, 0, 1).reshape(128, DEPTH * NJ))
    shared["fin_g"] = _col(f(inputs["final_norm"]), KC)
    maps = []
    zeros = {k: np.zeros_like(v) for k, v in shared.items()}
    for core in range(ncores):
        if core not in ACTIVE:
            m = dict(zeros)
            m["xT_in"] = np.zeros((KC, 128, T), np.float32)
            m["c2"] = np.zeros((128, KC * 2), np.float32)
            maps.append(m)
            continue
        b = ACTIVE.index(core)
        m = dict(shared)
        xa = np.concatenate([x[b], ctx[b]], axis=0)
        m["xT_in"] = np.ascontiguousarray(xa.T.reshape(KC, 128, T))
        cc = np.stack([_col(c[b], KC), _col(c_ctx, KC)], -1)
        m["c2"] = np.ascontiguousarray(cc.reshape(128, KC * 2))
        maps.append(m)
    return maps


def kernel(**inputs):
    if "nc" not in _CACHE:
        _CACHE["nc"] = build_program()
    nc, _ = _CACHE["nc"]
    maps = make_in_maps(inputs)
    res = run_bass_kernel_spmd(nc, maps, core_ids=list(range(8)))
    outs = []
    for b in range(4):
        yT = np.asarray(res.results[ACTIVE[b]]["yT"], dtype=np.float32)
        outs.append(yT.reshape(D, L).T)
    return np.ascontiguousarray(np.stack(outs, 0)).astype(np.float32)
```

```python
import numpy as np
import ml_dtypes
from contextlib import ExitStack
import concourse.bass as bass
import concourse.mybir as mybir
from concourse.bass_utils import run_bass_kernel_spmd

F32 = mybir.dt.float32
BF16 = mybir.dt.bfloat16
AF = mybir.ActivationFunctionType
ALU = mybir.AluOpType
AX = mybir.AxisListType

D = 2048
KC = 16
L = 2048
CL = 256
T = L + CL
NT = T // 128
DEPTH = 4
DFF = 5504
NJ = DFF // 128
EPS = 1e-6
TBS = [(0, 512), (512, 512), (1024, 512), (1536, 512), (2048, 256)]
TB2 = [(i * 256, 256) for i in range(9)]
WS_COLS = 22016
SAME_ENG_SYNC = True


class Buf:
    __slots__ = ('name', 'ap', 'w', 'r', 'sem', 'ndma', 'ws', 'phase')

    def __init__(self, name, ap=None):
        self.name = name
        self.ap = ap
        self.w = None
        self.r = {}
        self.ws = {}
        self.phase = False
        self.sem = None
        self.ndma = 0


class Op:
    __slots__ = ('eng', 'fn', 'deps', 'tok', 'sig', 'dma', 'waits', 'ph')


class Rec:
    def __init__(self):
        self.call = None

    def __getattr__(self, name):
        def f(*a, **k):
            self.call = (name, a, k)
            return self
        return f


class Prog:
    ENGS = ('pe', 'act', 'dve', 'pool', 'sp')

    def __init__(self, nc, stack):
        self.nc = nc
        self.stack = stack
        self.ops = []
        self.barrier = None
        self.esem = {e: [stack.enter_context(nc.semaphore('s_%s%d' % (e, r))) for r in range(3)] for e in self.ENGS}
        self.phase_no = 0
        self.nsem = 0
        self.phase_bufs = []
        self.scr_used = False
        self.h_used = False
        self.dummy = None
        self.sem_pool = []
        self.pool_idx = 0
        self.semcnt = {}

    def sbuf(self, name, cols, dt):
        t = self.stack.enter_context(self.nc.sbuf_tensor(name, [128, cols], dt))
        return Buf(name, t[:])

    def psum(self, name, cols, dt):
        t = self.stack.enter_context(self.nc.psum_tensor(name, [128, cols], dt))
        return Buf(name, t[:])

    def region(self, name, ap=None, phase=True):
        b = Buf(name, ap)
        b.w = self.barrier
        if phase:
            b.phase = True
            self.phase_bufs.append(b)
        return b

    def end_phase(self, dummy=None):
        dummy = dummy or self.dummy
        bufs = self.phase_bufs
        self.phase_bufs = []
        self.scr_used = False
        self.h_used = False
        i = self.add('dve', lambda e: e.memset(dummy.ap[:, 0:1], 0.0), writes=[dummy] + bufs)
        self.barrier = i
        self.phase_no += 1
        self.pool_idx = (self.phase_no % 4) * 12

    def add(self, eng, fn, reads=(), writes=(), dma=None, accum=()):
        i = len(self.ops)
        deps = set()
        for b in reads:
            if b.w is not None:
                deps.add(b.w)
            deps.update(b.ws.values())
        for b in writes:
            if b.w is not None:
                deps.add(b.w)
            deps.update(b.r.values())
            deps.update(b.ws.values())
        for b in accum:
            if b.w is not None:
                deps.add(b.w)
            deps.update(b.r.values())
        op = Op()
        op.eng = eng
        rec = Rec()
        fn(rec)
        op.fn = rec.call
        op.dma = dma
        op.ph = self.phase_no % 3
        op.sig = False
        op.tok = None
        op.waits = None
        if dma is not None:
            if dma.sem is None:
                if dma.phase:
                    while self.pool_idx >= len(self.sem_pool):
                        self.sem_pool.append(self.stack.enter_context(self.nc.semaphore('q%d' % len(self.sem_pool))))
                    assert self.pool_idx < (self.phase_no % 4) * 12 + 12
                    dma.sem = self.sem_pool[self.pool_idx]
                    self.pool_idx += 1
                else:
                    dma.sem = self.stack.enter_context(self.nc.semaphore('d%d' % self.nsem))
                    self.nsem += 1
            k_ = id(dma.sem)
            self.semcnt[k_] = self.semcnt.get(k_, 0) + 1
            op.tok = (dma.sem, 16 * self.semcnt[k_])
            key = ('dma', id(dma))
        else:
            key = eng
        for b in writes:
            b.w = i
            b.r = {}
            b.ws = {}
        for b in accum:
            b.ws[key] = i
        for b in reads:
            if b.w != i:
                b.r[key] = i
        deps.discard(i)
        op.deps = deps
        self.ops.append(op)
        return i

    def dma(self, eng, out_ap, in_ap, tile, reads=(), writes=(), accum=(), **kw):
        return self.add(eng, lambda e: e.dma_start(out=out_ap, in_=in_ap, **kw),
                        reads=reads, writes=writes, dma=tile, accum=accum)

    def finalize(self):
        ops = self.ops
        for op in ops:
            for d in op.deps:
                dop = ops[d]
                if dop.dma is None:
                    if dop.eng == op.eng and op.dma is None and (op.eng == 'pe' or not SAME_ENG_SYNC):
                        continue
                    dop.sig = True
        cnt = {(e, r): 0 for e in self.ENGS for r in range(3)}
        for op in ops:
            if op.dma is None and op.sig:
                cnt[(op.eng, op.ph)] += 1
                op.tok = (self.esem[op.eng][op.ph], cnt[(op.eng, op.ph)])
        self.max_cnt = max(cnt.values())
        known = {e: {} for e in self.ENGS}
        for op in ops:
            need = {}
            for d in op.deps:
                dop = ops[d]
                if dop.tok is None:
                    continue
                if dop.dma is None and dop.eng == op.eng and op.dma is None and (op.eng == 'pe' or not SAME_ENG_SYNC):
                    continue
                sem, val = dop.tok
                k = id(sem)
                if known[op.eng].get(k, 0) >= val:
                    continue
                if k not in need or need[k][1] < val:
                    need[k] = (sem, val)
            for k, (sem, val) in need.items():
                known[op.eng][k] = val
            op.waits = list(need.values())

    def emit(self):
        self.finalize()
        mx = {}
        for op in self.ops:
            if op.tok is not None:
                mx[op.tok[0].name] = max(mx.get(op.tok[0].name, 0), op.tok[1])
        self.max_tok = mx
        print("[kernel] ops=%d max_sem_value=%d nsems=%d" % (len(self.ops), max(mx.values()), len(mx)))
        nc = self.nc
        ops = self.ops

        def run(eng, e):
            for op in ops:
                if op.eng != eng:
                    continue
                for sem, val in op.waits:
                    e.wait_ge(sem, val)
                name, a, k = op.fn
                ins = getattr(e, name)(*a, **k)
                if op.dma is not None:
                    ins.then_inc(op.tok[0], 16)
                elif op.sig:
                    ins.then_inc(op.tok[0], 1)

        with nc.Block() as block:
            @block.tensor
            def _(e):
                run('pe', e)

            @block.scalar
            def _(e):
                run('act', e)

            @block.vector
            def _(e):
                run('dve', e)

            @block.gpsimd
            def _(e):
                run('pool', e)

            @block.sync
            def _(e):
                run('sp', e)


class Rot:
    def __init__(self, items):
        self.items = items
        self.i = 0

    def next(self):
        b = self.items[self.i % len(self.items)]
        self.i += 1
        return b


def build_program(NL=DEPTH, final=True, stop=None):
    nc = bass.Bass("TRN2", target_bir_lowering=False)
    dt_in = {}

    def ein(name, shape, dt=F32):
        dt_in[name] = dt
        return nc.dram_tensor(name, list(shape), dt, kind="ExternalInput").ap()

    xT_in = ein("xT_in", [KC, 128, T])
    c2 = ein("c2", [128, KC * 2])
    ada_w = ein("ada_w", [DEPTH, D, 6 * D])
    ada_bc = ein("ada_bc", [128, DEPTH * 96])
    ev_w_in = ein("ev_w_in", [2, D, 3072])
    ev_w_out = ein("ev_w_out", [2, D, D])
    gains = ein("gains", [128, 2 * 2 * 128])
    od_w_in = ein("od_w_in", [2, D, 2112])
    od_w_uq = ein("od_w_uq", [2, 512, 2304])
    od_w_ukv = ein("od_w_ukv", [2, 512, 3072])
    od_w_out = ein("od_w_out", [2, D, D])
    od_cw = ein("od_cw", [128, 2 * 4 * 31])
    od_vec = ein("od_vec", [128, 2 * 5 * 4])
    ffn_w_up = ein("ffn_w_up", [DEPTH, D, 2 * DFF])
    ffn_w_down = ein("ffn_w_down", [DEPTH, DFF, D])
    ffn_cw = ein("ffn_cw", [128, DEPTH * NJ * 3])
    ffn_cb = ein("ffn_cb", [128, DEPTH * NJ])
    fin_g = ein("fin_g", [128, KC])
    ident_bf_d = ein("ident_bf", [128, 128], BF16)
    ident_f_d = ein("ident_f", [128, 128])
    dft128_d = ein("dft128", [128, 256], BF16)
    dftL_d = ein("dftL", [2, L, L], BF16)
    dftC_d = ein("dftC", [2, CL, CL], BF16)
    ropeA_d = ein("ropeA", [16, 128, 128])
    ropeM_d = ein("ropeM", [2, 128, T])
    out_d = nc.dram_tensor("yT", [KC, 128, L], F32, kind="ExternalOutput").ap()

    xT_d = nc.dram_tensor("xT_s", [KC, 128, T], F32).ap()
    aT_d = nc.dram_tensor("aT_s", [KC, 128, T], BF16).ap()
    qT_d = nc.dram_tensor("qT_s", [12, 128, T], BF16).ap()
    kT_d = nc.dram_tensor("kT_s", [12, 128, T], BF16).ap()
    v_d = nc.dram_tensor("v_s", [NT, 128, 1536], BF16).ap()
    fT_d = nc.dram_tensor("fT_s", [4, 128, T], BF16).ap()
    ffa_d = nc.dram_tensor("ffa_s", [NJ, 128, T], BF16).ap()
    cqn_d = nc.dram_tensor("cqn_s", [8, 128, T], BF16).ap()
    gl_d = nc.dram_tensor("gl_s", [4, 128, T], BF16).ap()
    kr_d = nc.dram_tensor("kr_s", [128, T], BF16).ap()
    qr_d = nc.dram_tensor("qr_s", [6, 128, T], BF16).ap()

    with ExitStack() as stack:
        P = Prog(nc, stack)

        H = P.sbuf("H", 36864, BF16)
        WS = [P.sbuf("WS0", WS_COLS, BF16), P.sbuf("WS1", WS_COLS, BF16)]
        SCR = P.sbuf("SCR", 13312, BF16)
        wrot = Rot(WS)
        modc = P.sbuf("modc", DEPTH * 96 * 2, F32)
        modv = modc.ap.rearrange("p (i f s) -> p i f s", i=DEPTH, f=96)
        c2s = P.sbuf("c2s", KC * 2, F32)
        adab = P.sbuf("adab", DEPTH * 96, F32)
        gains_s = P.sbuf("gains_s", 512, F32)
        odcw_s = P.sbuf("odcw_s", 2 * 4 * 31, F32)
        odvec_s = P.sbuf("odvec_s", 40, F32)
        ffcw_s = P.sbuf("ffcw_s", DEPTH * NJ * 3, F32)
        ffcb_s = P.sbuf("ffcb_s", DEPTH * NJ, F32)
        fing_s = P.sbuf("fing_s", KC, F32)
        ident_bf = P.sbuf("ident_bf_s", 128, BF16)
        ident_f = P.sbuf("ident_f_s", 128, F32)
        dft128 = P.sbuf("dft128_s", 256, BF16)
        onesD = P.sbuf("onesD", 128, BF16)
        ones1 = P.sbuf("ones1", 128, BF16)
        ones512 = P.sbuf("ones512", 128, BF16)
        ones512f = P.sbuf("ones512f", 128, F32)
        epsb = P.sbuf("epsb", 4, F32)
        dummy = P.sbuf("dummy_t", 4, F32)
        consts = Buf("consts")
        P.dummy = dummy

        PS = [P.psum("ps%d" % i, 512, F32) for i in range(8)]

        for dst, src in ((c2s, c2), (adab, ada_bc), (gains_s, gains), (odcw_s, od_cw), (odvec_s, od_vec),
                         (ffcw_s, ffn_cw), (ffcb_s, ffn_cb), (fing_s, fin_g), (ident_bf, ident_bf_d),
                         (ident_f, ident_f_d), (dft128, dft128_d)):
            P.dma('sp', dst.ap, src, consts, writes=[consts, dst])
        P.add('dve', lambda e: e.memset(onesD.ap, 1.0 / D), writes=[onesD])
        P.add('dve', lambda e: e.memset(ones1.ap, 1.0), writes=[ones1])
        P.add('dve', lambda e: e.memset(ones512.ap, 1.0 / 512), writes=[ones512])
        P.add('dve', lambda e: e.memset(ones512f.ap, 1.0 / 512), writes=[ones512f])
        P.add('dve', lambda e: e.memset(epsb.ap, EPS), writes=[epsb])

        def scr_carve(specs):
            if P.scr_used:
                P.end_phase()
            P.scr_used = True
            out = {}
            off = 0
            for name, cols, dt in specs:
                n16 = cols * (2 if dt == F32 else 1)
                ap = SCR.ap[:, off:off + n16]
                if dt == F32:
                    ap = ap.bitcast(F32)
                out[name] = P.region(name, ap)
                off += n16
            assert off <= 13312, off
            return out

        def h_carve(specs):
            if P.h_used:
                P.end_phase()
            P.h_used = True
            out = {}
            off = 0
            for name, cols, dt in specs:
                n16 = cols * (2 if dt == F32 else 1)
                ap = H.ap[:, off:off + n16]
                if dt == F32:
                    ap = ap.bitcast(F32)
                out[name] = P.region(name, ap)
                off += n16
            assert off <= 36864, off
            return out

        def w_load(parts, f32=False):
            slab = wrot.next()
            for off, n, src, shape in parts:
                base = slab.ap.bitcast(F32) if f32 else slab.ap
                dst = base[:, off:off + n]
                if shape is not None:
                    dst = dst.rearrange(shape[0], **shape[1])
                P.dma('sp' if f32 else 'pool', dst, src, slab, writes=[slab])
            return slab

        sc = scr_carve([("silu", KC * 2, F32)])
        silu = sc["silu"]
        P.add('act', lambda e: e.activation(out=silu.ap, in_=c2s.ap, func=AF.Silu), reads=[c2s], writes=[silu])
        siluv = silu.ap.rearrange("p (k s) -> p k s", k=KC)
        adabv = adab.ap.rearrange("p (i f) -> p i f", i=DEPTH)
        for i in range(NL):
            for nb in range(24):
                src = ada_w[i, :, nb * 512:(nb + 1) * 512].rearrange("(k p) n -> p k n", p=128)
                slab = wrot.next()
                sv = slab.ap.bitcast(F32)[:, 0:8192].rearrange("p (k n) -> p k n", k=KC)
                for q in range(4):
                    P.dma('sp', sv[:, q * 4:(q + 1) * 4, :], src[:, q * 4:(q + 1) * 4, :], slab, writes=[slab])
                pt = PS[nb % 2]
                for q in range(4):
                    for k in range(KC):
                        P.add('pe', lambda e, pt=pt, k=k, q=q, sv=sv: e.matmul(pt.ap[:, q * 2:q * 2 + 2], sv[:, k, q * 128:(q + 1) * 128],
                                                                          siluv[:, k, :], start=(k == 0), stop=(k == KC - 1)),
                              reads=[silu, slab], writes=[pt])
                P.add('dve', lambda e, pt=pt, i=i, nb=nb: e.tensor_tensor(
                    out=modv[:, i, nb * 4:nb * 4 + 4, :],
                    in0=pt.ap[:, 0:8].rearrange("p (f s) -> p f s", f=4),
                    in1=adabv[:, i, nb * 4:nb * 4 + 4].unsqueeze(2).broadcast_to([128, 4, 2]), op=ALU.add),
                    reads=[pt, adab, consts], writes=[modc])
            for part in (1, 4):
                P.add('dve', lambda e, i=i, part=part: e.tensor_scalar_add(
                    out=modv[:, i, part * 16:(part + 1) * 16, :], in0=modv[:, i, part * 16:(part + 1) * 16, :],
                    scalar1=1.0), reads=[modc], writes=[modc])
        P.end_phase(dummy)

        xreg = [[Buf("x%d_%d" % (c, b)) for b in range(9)] for c in range(KC)]

        def xregs(c, t0, n):
            return [xreg[c][b] for b in range(t0 // 256, (t0 + n) // 256)]

        aTreg = [[Buf("a%d_%d" % (c, b)) for b in range(5)] for c in range(KC)]
        state = {"xsrc": xT_in}

        hT = H.ap.rearrange("p (k t) -> p k t", k=KC)

        def modulate(i, part_sh, part_sc, hreg):
            xsrc = state["xsrc"]
            sc = scr_carve([("x0", 512, F32), ("x1", 512, F32), ("x2", 512, F32), ("x3", 512, F32),
                            ("sq0", 512, BF16), ("sq1", 512, BF16),
                            ("rs0", 512, F32), ("rs1", 512, F32), ("t0", 512, F32), ("t1", 512, F32)])
            xs = Rot([sc["x0"], sc["x1"], sc["x2"], sc["x3"]])
            sqs = Rot([sc["sq0"], sc["sq1"]])
            rss = Rot([sc["rs0"], sc["rs1"]])
            ts = Rot([sc["t0"], sc["t1"]])
            for tb, (t0, n) in enumerate(TBS):
                s = 1 if tb == 4 else 0
                ps = PS[tb % 2]
                for c in range(KC):
                    x = xs.next()
                    P.dma('sp', x.ap[:, :n], xsrc[c, :, t0:t0 + n], x, reads=xregs(c, t0, n), writes=[x])
                    sq = sqs.next()
                    P.add('act', lambda e, x=x, sq=sq: e.activation(out=sq.ap[:, :n], in_=x.ap[:, :n], func=AF.Square),
                          reads=[x], writes=[sq])
                    P.add('pe', lambda e, ps=ps, sq=sq, c=c: e.matmul(ps.ap[:, :n], onesD.ap, sq.ap[:, :n],
                                                                     start=(c == 0), stop=(c == KC - 1)),
                          reads=[sq, onesD], writes=[ps])
                rs = rss.next()
                P.add('act', lambda e, ps=ps, rs=rs: e.activation(out=rs.ap[:, :n], in_=ps.ap[:, :n], func=AF.Sqrt, bias=epsb.ap[:, 0:1], scale=1.0), reads=[ps, epsb], writes=[rs])
                P.add('dve', lambda e, ps=ps, rs=rs: e.reciprocal(out=rs.ap[:, :n], in_=rs.ap[:, :n]), reads=[rs], writes=[rs])
                for c in range(KC):
                    x = xs.next()
                    P.dma('sp', x.ap[:, :n], xsrc[c, :, t0:t0 + n], x, reads=xregs(c, t0, n), writes=[x])
                    t = ts.next()
                    P.add('dve', lambda e, x=x, t=t, rs=rs, c=c, s=s: e.scalar_tensor_tensor(
                        out=t.ap[:, :n], in0=x.ap[:, :n], scalar=modv[:, i, part_sc * 16 + c, s:s + 1],
                        in1=rs.ap[:, :n], op0=ALU.mult, op1=ALU.mult), reads=[x, rs, modc], writes=[t])
                    P.add('act', lambda e, t=t, c=c, s=s: e.activation(
                        out=hT[:, c, t0:t0 + n], in_=t.ap[:, :n], func=AF.Identity,
                        bias=modv[:, i, part_sh * 16 + c, s:s + 1], scale=1.0),
                        reads=[t, modc], writes=[hreg[c][tb]])

        def new_hreg():
            if P.h_used:
                P.end_phase()
            P.h_used = True
            return [[P.region("h%d_%d" % (c, tb)) for tb in range(5)] for c in range(KC)]

        def wslab_cols(wsrc, c0, ncols, k_chunks=KC, dst_off=0, slab=None):
            pass

        def load_cols(slab, off, wsrc, c0, ncols, kc=KC, step=4):
            dstv = slab.ap[:, off:off + kc * ncols].rearrange("p (k n) -> p k n", k=kc)
            srcv = wsrc[:, c0:c0 + ncols].rearrange("(k p) n -> p k n", p=128)
            for q in range(0, kc, step):
                P.dma('pool', dstv[:, q:q + step, :], srcv[:, q:q + step, :], slab, writes=[slab])
            return dstv

        def even_inproj(i, j, hreg):
            w = ev_w_in[j]
            sc = scr_carve([("st0", 512, BF16), ("st1", 512, BF16), ("st2", 512, BF16),
                            ("sq", 512, F32), ("zn", 512, F32), ("ss", 8, F32), ("rstd", 8, F32),
                            ("ta", 256, F32), ("tb", 256, F32), ("zr0", 512, BF16), ("zr1", 512, BF16),
                            ("rope", 16 * 128, F32)])
            sts = Rot([sc["st0"], sc["st1"], sc["st2"]])
            zrs = Rot([sc["zr0"], sc["zr1"]])
            ropeb = sc["rope"]
            ropev = ropeb.ap.rearrange("p (t d) -> p t d", t=16)
            P.dma('sp', ropev, ropeA_d.rearrange("t p d -> p t d"), ropeb, writes=[ropeb])
            sq, zn, ss, rstd, ta, tbb = sc["sq"], sc["zn"], sc["ss"], sc["rstd"], sc["ta"], sc["tb"]
            gv = gains_s.ap.rearrange("p (j q d) -> p j q d", j=2, q=2)
            psr = Rot([PS[0], PS[1], PS[2]])
            ptr = Rot([PS[3], PS[4]])
            allh = [hreg[c][tb] for c in range(KC) for tb in range(5)]
            slab = wrot.next()
            wv = load_cols(slab, 0, w, 0, 512)
            for tb, (t0, n) in enumerate(TBS):
                for m in range(4):
                    ps = psr.next()
                    for k in range(KC):
                        P.add('pe', lambda e, ps=ps, k=k, m=m, wv=wv: e.matmul(
                            ps.ap[:, :n], wv[:, k, m * 128:(m + 1) * 128], hT[:, k, t0:t0 + n],
                            start=(k == 0), stop=(k == KC - 1)), reads=[slab, hreg[k][tb]], writes=[ps])
                    st = sts.next()
                    P.add('act', lambda e, ps=ps, st=st: e.copy(out=st.ap[:, :n], in_=ps.ap[:, :n]),
                          reads=[ps], writes=[st])
                    P.dma('sp', fT_d[m, :, t0:t0 + n], st.ap[:, :n], st, reads=[st], accum=[fTreg])
            for blk in range(1, 5):
                slab = wrot.next()
                wv = load_cols(slab, 0, w, blk * 512, 512)
                isk = (blk == 4)
                for tt in range(NT):
                    ps = psr.next()
                    tbi = min(tt // 4, 4)
                    for k in range(KC):
                        P.add('pe', lambda e, ps=ps, k=k, wv=wv, tt=tt: e.matmul(
                            ps.ap, hT[:, k, tt * 128:(tt + 1) * 128], wv[:, k, :],
                            start=(k == 0), stop=(k == KC - 1)), reads=[slab, hreg[k][tbi]], writes=[ps])
                    P.add('act', lambda e, ps=ps: e.activation(out=sq.ap, in_=ps.ap, func=AF.Square),
                          reads=[ps], writes=[sq])
                    P.add('dve', lambda e: e.reduce_sum(out=ss.ap[:, 0:4], in_=sq.ap.rearrange("p (h d) -> p h d", h=4),
                                                        axis=AX.X), reads=[sq], writes=[ss])
                    P.add('act', lambda e: e.activation(out=rstd.ap[:, 0:4], in_=ss.ap[:, 0:4], func=AF.Sqrt, bias=epsb.ap[:, 0:1], scale=1.0 / 128),
                          reads=[ss, epsb], writes=[rstd])
                    P.add('dve', lambda e: e.reciprocal(out=rstd.ap[:, 0:4], in_=rstd.ap[:, 0:4]), reads=[rstd], writes=[rstd])
                    zn3 = zn.ap.rearrange("p (h d) -> p h d", h=4)
                    P.add('dve', lambda e, ps=ps, zn3=zn3: e.tensor_tensor(
                        out=zn3, in0=ps.ap.rearrange("p (h d) -> p h d", h=4),
                        in1=rstd.ap[:, 0:4].unsqueeze(2).broadcast_to([128, 4, 128]), op=ALU.mult),
                        reads=[ps, rstd], writes=[zn])
                    zr = zrs.next()
                    zr3 = zr.ap.rearrange("p (h d) -> p h d", h=4)
                    gsel = gv[:, j, 1 if isk else 0, :].unsqueeze(1).broadcast_to([128, 4, 128])
                    if tt < 16:
                        P.add('dve', lambda e, zn3=zn3, gsel=gsel: e.tensor_tensor(out=zn3, in0=zn3, in1=gsel, op=ALU.mult),
                              reads=[zn, gains_s, consts], writes=[zn])
                        cs = ropeb
                        cosb = ropev[:, tt, 0:64].unsqueeze(1).broadcast_to([128, 4, 64])
                        sinb = ropev[:, tt, 64:128].unsqueeze(1).broadcast_to([128, 4, 64])
                        ta3 = ta.ap.rearrange("p (h d) -> p h d", h=4)
                        tb3 = tbb.ap.rearrange("p (h d) -> p h d", h=4)
                        x1 = zn3[:, :, 0:64]
                        x2 = zn3[:, :, 64:128]
                        P.add('dve', lambda e, x1=x1, cosb=cosb, ta3=ta3: e.tensor_tensor(out=ta3, in0=x1, in1=cosb, op=ALU.mult),
                              reads=[zn, cs], writes=[ta])
                        P.add('dve', lambda e, x2=x2, sinb=sinb, tb3=tb3: e.tensor_tensor(out=tb3, in0=x2, in1=sinb, op=ALU.mult),
                              reads=[zn, cs], writes=[tbb])
                        P.add('dve', lambda e, zr3=zr3, ta3=ta3, tb3=tb3: e.tensor_tensor(out=zr3[:, :, 0:64], in0=ta3, in1=tb3, op=ALU.subtract),
                              reads=[ta, tbb], writes=[zr])
                        P.add('dve', lambda e, x2=x2, cosb=cosb, ta3=ta3: e.tensor_tensor(out=ta3, in0=x2, in1=cosb, op=ALU.mult),
                              reads=[zn, cs], writes=[ta])
                        P.add('dve', lambda e, x1=x1, sinb=sinb, tb3=tb3: e.tensor_tensor(out=tb3, in0=x1, in1=sinb, op=ALU.mult),
                              reads=[zn, cs], writes=[tbb])
                        P.add('dve', lambda e, zr3=zr3, ta3=ta3, tb3=tb3: e.tensor_tensor(out=zr3[:, :, 64:128], in0=ta3, in1=tb3, op=ALU.add),
                              reads=[ta, tbb], writes=[zr])
                    else:
                        P.add('dve', lambda e, zn3=zn3, gsel=gsel, zr3=zr3: e.tensor_tensor(out=zr3, in0=zn3, in1=gsel, op=ALU.mult),
                              reads=[zn, gains_s, consts], writes=[zr])
                    pt = ptr.next()
                    ptb = pt.ap.bitcast(BF16)
                    for h in range(4):
                        P.add('pe', lambda e, ptb=ptb, zr=zr, h=h: e.transpose(ptb[:, h * 128:(h + 1) * 128],
                                                                             zr.ap[:, h * 128:(h + 1) * 128], ident_bf.ap),
                              reads=[zr, ident_bf, consts], writes=[pt])
                    st = sts.next()
                    P.add('act', lambda e, ptb=ptb, st=st: e.copy(out=st.ap, in_=ptb[:, 0:512]), reads=[pt], writes=[st])
                    dstT = kT_d if isk else qT_d
                    h0 = 0 if isk else (blk - 1) * 4
                    P.dma('sp', dstT[h0:h0 + 4, :, tt * 128:(tt + 1) * 128].rearrange("h p t -> p h t"),
                          st.ap.rearrange("p (h t) -> p h t", h=4), st, reads=[st], accum=[qkreg])
            slab = wrot.next()
            wv = load_cols(slab, 0, w, 2560, 512)
            for tt in range(NT):
                ps = psr.next()
                tbi = min(tt // 4, 4)
                for k in range(KC):
                    P.add('pe', lambda e, ps=ps, k=k, wv=wv, tt=tt: e.matmul(
                        ps.ap, hT[:, k, tt * 128:(tt + 1) * 128], wv[:, k, :],
                        start=(k == 0), stop=(k == KC - 1)), reads=[slab, hreg[k][tbi]], writes=[ps])
                st = sts.next()
                P.add('act', lambda e, ps=ps, st=st: e.copy(out=st.ap, in_=ps.ap), reads=[ps], writes=[st])
                P.dma('sp', v_d[tt, :, 0:512], st.ap, st, reads=[st], accum=[qkreg])

        fTreg = Buf("fTreg")
        qkreg = Buf("qkreg")
        aTall = Buf("aTall")

        def fourier(i):
            hc = h_carve([("f", 4 * T, BF16), ("A", 4 * NT * 256, BF16)])
            fb, Ab = hc["f"], hc["A"]
            fv = fb.ap.rearrange("p (g t) -> p g t", g=4)
            Av = Ab.ap.rearrange("p (g n c) -> p g n c", g=4, n=NT)
            sc = scr_carve([("st0", 512, BF16), ("st1", 512, BF16)])
            sts = Rot([sc["st0"], sc["st1"]])
            for g in range(4):
                P.dma('sp', fv[:, g, :], fT_d[g], fb, reads=[fTreg], writes=[fb])
            psr = Rot([PS[0], PS[1], PS[2]])
            for g in range(4):
                for n in range(NT):
                    ps = psr.next()
                    P.add('pe', lambda e, ps=ps, g=g, n=n: e.matmul(ps.ap[:, 0:256], fv[:, g, n * 128:(n + 1) * 128], dft128.ap,
                                                                   start=True, stop=True),
                          reads=[fb, dft128, consts], writes=[ps])
                    P.add('act', lambda e, ps=ps, g=g, n=n: e.copy(out=Av[:, g, n, :], in_=ps.ap[:, 0:256]),
                          reads=[ps], writes=[Ab])
            for kb in range(5):
                slab = wrot.next()
                if kb < 4:
                    nn, ncol, n0 = 16, 512, 0
                    for cs_ in range(2):
                        dst = slab.ap[:, cs_ * 8192:(cs_ + 1) * 8192].rearrange("p (n k) -> p n k", n=16)
                        src = dftL_d[cs_, :, kb * 512:(kb + 1) * 512].rearrange("(n p) k -> p n k", p=128)
                        for q in range(0, 16, 4):
                            P.dma('sp', dst[:, q:q + 4, :], src[:, q:q + 4, :], slab, writes=[slab])
                else:
                    nn, ncol, n0 = 2, 256, 16
                    for cs_ in range(2):
                        dst = slab.ap[:, cs_ * 8192:cs_ * 8192 + 512].rearrange("p (n k) -> p n k", n=2)
                        src = dftC_d[cs_].rearrange("(n p) k -> p n k", p=128)
                        P.dma('sp', dst, src, slab, writes=[slab])
                t0 = kb * 512
                for g in range(4):
                    ps = psr.next()
                    cnt = 0
                    for cs_ in range(2):
                        sv = slab.ap[:, cs_ * 8192:cs_ * 8192 + nn * ncol].rearrange("p (n k) -> p n k", n=nn)
                        for n in range(nn):
                            P.add('pe', lambda e, ps=ps, g=g, n=n, cs_=cs_, sv=sv, cnt=cnt, nn=nn, ncol=ncol, n0=n0: e.matmul(
                                ps.ap[:, :ncol], Av[:, g, n0 + n, cs_ * 128:(cs_ + 1) * 128], sv[:, n, :],
                                start=(cnt == 0), stop=(cnt == 2 * nn - 1)), reads=[Ab, slab], writes=[ps])
                            cnt += 1
                    st = sts.next()
                    P.add('act', lambda e, ps=ps, st=st, ncol=ncol: e.copy(out=st.ap[:, :ncol], in_=ps.ap[:, :ncol]),
                          reads=[ps], writes=[st])
                    P.dma('sp', aT_d[g, :, t0:t0 + ncol], st.ap[:, :ncol], st, reads=[st], writes=[aTreg[g][kb]])

        def attention(nheads, kparts_of, qparts_of, v_of, kvkey_of, scale, out_chunk0):
            hc = h_carve([("k0", 2 * T, BF16), ("k1", 2 * T, BF16), ("v0", NT * 128, BF16), ("v1", NT * 128, BF16)])
            kbufs = Rot([hc["k0"], hc["k1"]])
            vbufs = Rot([hc["v0"], hc["v1"]])
            sc = scr_carve([("q0", 1024, BF16), ("q1", 1024, BF16), ("q2", 1024, BF16),
                            ("p0", 512, BF16), ("p1", 512, BF16), ("p2", 512, BF16),
                            ("o0", 512, BF16), ("o1", 512, BF16), ("r0", 512, F32), ("r1", 512, F32)])
            qbufs = Rot([sc["q0"], sc["q1"], sc["q2"]])
            pbufs = Rot([sc["p0"], sc["p1"], sc["p2"]])
            obufs = Rot([sc["o0"], sc["o1"]])
            rbufs = Rot([sc["r0"], sc["r1"]])
            pss = Rot([PS[0], PS[1], PS[2]])
            pso = Rot([PS[3], PS[4]])
            psm = Rot([PS[5], PS[6]])
            blocks = []
            cur_kv = None
            for h in range(nheads):
                for tb, (t0, n) in enumerate(TBS):
                    blocks.append((h, tb, t0, n))
            st = {"kb": None, "vb": None, "cur": None}
            loaded = {}

            def do_loads(bi):
                h, tb, t0, n = blocks[bi]
                key = kvkey_of(h)
                if key != st["cur"]:
                    st["cur"] = key
                    kb = kbufs.next()
                    vb = vbufs.next()
                    st["kb"], st["vb"] = kb, vb
                    for pi, (src, rows, base) in enumerate(kparts_of(h)):
                        P.dma('sp', kb.ap[base:base + rows, pi * T:(pi + 1) * T], src, kb, reads=[qkreg], writes=[kb])
                    P.dma('sp', vb.ap.rearrange("p (t d) -> p t d", t=NT), v_of(h).rearrange("t p d -> p t d"), vb,
                          reads=[qkreg], writes=[vb])
                qb = qbufs.next()
                for pi, (src, rows, base) in enumerate(qparts_of(h)):
                    P.dma('sp', qb.ap[base:base + rows, pi * 512:pi * 512 + n], src[:, t0:t0 + n], qb,
                          reads=[qkreg], writes=[qb])
                loaded[bi] = (st["kb"], st["vb"], qb)

            def block_steps(bi):
                h, tb, t0, n = blocks[bi]
                kb, vb, qb = loaded[bi]
                kts = list(range(NT)) if tb < 4 else [16, 17]
                po = pso.next()
                pm = psm.next()
                return [(h, tb, t0, n, kt, ki == 0, ki == len(kts) - 1, kb, vb, qb, po, pm) for ki, kt in enumerate(kts)]
            nparts = len(kparts_of(0))
            pend = None

            def emit_qk(s):
                h, tb, t0, n, kt, first, last, kb, vb, qb, po, pm = s
                ps = pss.next()
                parts = kparts_of(h)
                for pi, (src, rows, base) in enumerate(parts):
                    P.add('pe', lambda e, ps=ps, kb=kb, qb=qb, pi=pi, rows=rows, base=base, kt=kt, n=n: e.matmul(
                        ps.ap[:, :n], kb.ap[base:base + rows, pi * T + kt * 128:pi * T + (kt + 1) * 128],
                        qb.ap[base:base + rows, pi * 512:pi * 512 + n],
                        start=(pi == 0), stop=(pi == len(parts) - 1)), reads=[kb, qb], writes=[ps])
                pb = pbufs.next()
                P.add('act', lambda e, ps=ps, pb=pb, n=n: e.activation(out=pb.ap[:, :n], in_=ps.ap[:, :n], func=AF.Exp, scale=scale),
                      reads=[ps], writes=[pb])
                return pb

            def emit_pv(s, pb):
                h, tb, t0, n, kt, first, last, kb, vb, qb, po, pm = s
                P.add('pe', lambda e, po=po, vb=vb, pb=pb, kt=kt, n=n, first=first, last=last: e.matmul(
                    po.ap[:, :n], vb.ap[:, kt * 128:(kt + 1) * 128], pb.ap[:, :n], start=first, stop=last),
                    reads=[vb, pb], writes=[po])
                P.add('pe', lambda e, pm=pm, pb=pb, n=n, first=first, last=last: e.matmul(
                    pm.ap[:, :n], ones1.ap, pb.ap[:, :n], start=first, stop=last), reads=[ones1, pb], writes=[pm])
                if last:
                    rb = rbufs.next()
                    ob = obufs.next()
                    P.add('dve', lambda e, pm=pm, rb=rb, n=n: e.reciprocal(out=rb.ap[:, :n], in_=pm.ap[:, :n]),
                          reads=[pm], writes=[rb])
                    P.add('dve', lambda e, po=po, rb=rb, ob=ob, n=n: e.tensor_tensor(out=ob.ap[:, :n], in0=po.ap[:, :n],
                                                                                    in1=rb.ap[:, :n], op=ALU.mult),
                          reads=[po, rb], writes=[ob])
                    P.dma('sp', aT_d[out_chunk0 + h, :, t0:t0 + n], ob.ap[:, :n], ob, reads=[ob],
                          writes=[aTreg[out_chunk0 + h][tb]])

            do_loads(0)
            for bi in range(len(blocks)):
                if bi + 1 < len(blocks):
                    do_loads(bi + 1)
                for s in block_steps(bi):
                    pb = emit_qk(s)
                    if pend is not None:
                        emit_pv(*pend)
                    pend = (s, pb)
            emit_pv(*pend)

        def outproj(i, w, gpart):
            xsrc = state["xsrc"]
            hc = h_carve([("a0", KC * 512, BF16), ("a1", KC * 512, BF16)])
            abufs = Rot([hc["a0"], hc["a1"]])
            sc = scr_carve([("x0", 512, F32), ("x1", 512, F32), ("x2", 512, F32),
                            ("o0", 512, F32), ("o1", 512, F32), ("o2", 512, F32)])
            xs = Rot([sc["x0"], sc["x1"], sc["x2"]])
            os_ = Rot([sc["o0"], sc["o1"], sc["o2"]])
            psr = Rot([PS[0], PS[1], PS[2], PS[3]])
            op_loaded = {}

            def op_load(bi):
                tb_ = bi % 5
                t0_, n_ = TBS[tb_]
                ab_ = abufs.next()
                abv_ = ab_.ap.rearrange("p (k t) -> p k t", k=KC)
                P.dma('sp', abv_[:, :, :n_], aT_d[:, :, t0_:t0_ + n_].rearrange("k p t -> p k t"), ab_,
                      reads=[aTreg[c][tb_] for c in range(KC)], writes=[ab_])
                op_loaded[bi] = ab_

            for mb in range(4):
                slab = wrot.next()
                wv = load_cols(slab, 0, w, mb * 512, 512)
                for tb, (t0, n) in enumerate(TBS):
                    s = 1 if tb == 4 else 0
                    if mb == 0 and tb == 0:
                        op_load(0)
                    ab = op_loaded.pop(mb * 5 + tb)
                    abv = ab.ap.rearrange("p (k t) -> p k t", k=KC)
                    if mb * 5 + tb + 1 < 20:
                        op_load(mb * 5 + tb + 1)
                    for m4 in range(4):
                        m = mb * 4 + m4
                        ps = psr.next()
                        for k in range(KC):
                            P.add('pe', lambda e, ps=ps, k=k, m4=m4, wv=wv, abv=abv, n=n: e.matmul(
                                ps.ap[:, :n], wv[:, k, m4 * 128:(m4 + 1) * 128], abv[:, k, :n],
                                start=(k == 0), stop=(k == KC - 1)), reads=[slab, ab], writes=[ps])
                        x = xs.next()
                        P.dma('sp', x.ap[:, :n], xsrc[m, :, t0:t0 + n], x, reads=xregs(m, t0, n), writes=[x])
                        o = os_.next()
                        P.add('dve', lambda e, ps=ps, x=x, o=o, m=m, s=s, n=n: e.scalar_tensor_tensor(
                            out=o.ap[:, :n], in0=ps.ap[:, :n], scalar=modv[:, i, gpart * 16 + m, s:s + 1],
                            in1=x.ap[:, :n], op0=ALU.mult, op1=ALU.add), reads=[ps, x, modc], writes=[o])
                        P.dma('sp', xT_d[m, :, t0:t0 + n], o.ap[:, :n], o, reads=[o], writes=xregs(m, t0, n))
            state["xsrc"] = xT_d

        GW = 2308

        def ffn_up(i, hreg):
            w = ffn_w_up[i]
            sc = scr_carve([("g0", GW, BF16), ("g1", GW, BF16), ("sg0", T, BF16), ("sg1", T, BF16),
                            ("acc0", 512, F32), ("acc1", 512, F32), ("a0", 512, BF16), ("a1", 512, BF16), ("a2", 512, BF16)])
            gbufs = Rot([sc["g0"], sc["g1"]])
            sgbufs = Rot([sc["sg0"], sc["sg1"]])
            accs = Rot([sc["acc0"], sc["acc1"]])
            asts = Rot([sc["a0"], sc["a1"], sc["a2"]])
            for gb in (sc["g0"], sc["g1"]):
                P.add('dve', lambda e, gb=gb: e.memset(gb.ap, 0.0), writes=[gb])
            cwv = ffcw_s.ap.rearrange("p (i j k) -> p i j k", i=DEPTH, j=NJ)
            cbv = ffcb_s.ap.rearrange("p (i j) -> p i j", i=DEPTH)
            psg = Rot([PS[0], PS[1], PS[2]])
            psu = Rot([PS[3], PS[4], PS[5]])

            def goff(tb, t0):
                return 1 + t0 if tb < 4 else 2051

            slabs = {}

            def get_slab(jg):
                if jg not in slabs:
                    nj = min(4, NJ - jg * 4)
                    slab = wrot.next()
                    gv_ = load_cols(slab, 0, w, jg * 512, nj * 128)
                    uv_ = load_cols(slab, 8192, w, DFF + jg * 512, nj * 128)
                    slabs[jg] = (slab, gv_, uv_)
                return slabs[jg]

            def emit_g(j):
                slab, gv_, uv_ = get_slab(j // 4)
                jj = j % 4
                gb = gbufs.next()
                sg = sgbufs.next()
                for tb, (t0, n) in enumerate(TBS):
                    ps = psg.next()
                    for k in range(KC):
                        P.add('pe', lambda e, ps=ps, k=k, jj=jj, gv_=gv_, t0=t0, n=n: e.matmul(
                            ps.ap[:, :n], gv_[:, k, jj * 128:(jj + 1) * 128], hT[:, k, t0:t0 + n],
                            start=(k == 0), stop=(k == KC - 1)), reads=[slab, hreg[k][tb]], writes=[ps])
                    off = goff(tb, t0)
                    P.add('dve', lambda e, ps=ps, gb=gb, off=off, n=n: e.tensor_copy(out=gb.ap[:, off:off + n], in_=ps.ap[:, :n]),
                          reads=[ps], writes=[gb])
                for tb, (t0, n) in enumerate(TBS):
                    off = goff(tb, t0)
                    acc = accs.next()
                    P.add('dve', lambda e, gb=gb, acc=acc, off=off, n=n, j=j: e.tensor_scalar(
                        out=acc.ap[:, :n], in0=gb.ap[:, off - 1:off - 1 + n], scalar1=cwv[:, i, j, 0:1], scalar2=None,
                        op0=ALU.mult), reads=[gb, ffcw_s, consts], writes=[acc])
                    for kk in (1, 2):
                        P.add('dve', lambda e, gb=gb, acc=acc, off=off, n=n, j=j, kk=kk: e.scalar_tensor_tensor(
                            out=acc.ap[:, :n], in0=gb.ap[:, off - 1 + kk:off - 1 + kk + n], scalar=cwv[:, i, j, kk:kk + 1],
                            in1=acc.ap[:, :n], op0=ALU.mult, op1=ALU.add), reads=[gb, acc, ffcw_s], writes=[acc])
                    P.add('act', lambda e, acc=acc, sg=sg, t0=t0, n=n, j=j: e.activation(
                        out=sg.ap[:, t0:t0 + n], in_=acc.ap[:, :n], func=AF.Silu, bias=cbv[:, i, j:j + 1], scale=1.0),
                        reads=[acc, ffcb_s, consts], writes=[sg])
                return sg

            def emit_u(j, sg):
                slab, gv_, uv_ = get_slab(j // 4)
                jj = j % 4
                for tb, (t0, n) in enumerate(TBS):
                    ps = psu.next()
                    for k in range(KC):
                        P.add('pe', lambda e, ps=ps, k=k, jj=jj, uv_=uv_, t0=t0, n=n: e.matmul(
                            ps.ap[:, :n], uv_[:, k, jj * 128:(jj + 1) * 128], hT[:, k, t0:t0 + n],
                            start=(k == 0), stop=(k == KC - 1)), reads=[slab, hreg[k][tb]], writes=[ps])
                    a = asts.next()
                    P.add('dve', lambda e, ps=ps, sg=sg, a=a, t0=t0, n=n: e.tensor_tensor(
                        out=a.ap[:, :n], in0=ps.ap[:, :n], in1=sg.ap[:, t0:t0 + n], op=ALU.mult),
                        reads=[ps, sg], writes=[a])
                    P.dma('sp', ffa_d[j, :, t0:t0 + n], a.ap[:, :n], a, reads=[a], accum=[ffareg])

            pend = None
            for j in range(NJ):
                sg = emit_g(j)
                if pend is not None:
                    emit_u(*pend)
                pend = (j, sg)
            emit_u(*pend)

        ffareg = Buf("ffareg")

        def ffn_down(i, gpart):
            w = ffn_w_down[i]
            xsrc = state["xsrc"]
            JS = 29
            hc = h_carve([("a0", NJ * 512, BF16), ("a1h", JS * 512, BF16)])
            sc = scr_carve([("a1s", (NJ - JS) * 512, BF16),
                            ("x0", 512, F32), ("x1", 512, F32), ("x2", 512, F32),
                            ("o0", 512, F32), ("o1", 512, F32), ("o2", 512, F32)])
            a0v = hc["a0"].ap.rearrange("p (j t) -> p j t", j=NJ)
            a1hv = hc["a1h"].ap.rearrange("p (j t) -> p j t", j=JS)
            a1sv = sc["a1s"].ap.rearrange("p (j t) -> p j t", j=NJ - JS)
            abufs = Rot([[(hc["a0"], a0v, 0, NJ)], [(hc["a1h"], a1hv, 0, JS), (sc["a1s"], a1sv, JS, NJ)]])
            xs = Rot([sc["x0"], sc["x1"], sc["x2"]])
            os_ = Rot([sc["o0"], sc["o1"], sc["o2"]])
            psr = Rot([PS[0], PS[1], PS[2], PS[3]])
            fd_loaded = {}

            def fd_load(bi):
                t0_, n_ = TBS[bi % 5]
                parts_ = abufs.next()
                for (buf, view, j0, j1) in parts_:
                    for q in range(j0, j1, 11):
                        q1 = min(q + 11, j1)
                        P.dma('sp', view[:, q - j0:q1 - j0, :n_], ffa_d[q:q1, :, t0_:t0_ + n_].rearrange("j p t -> p j t"), buf,
                              reads=[ffareg], writes=[buf])
                fd_loaded[bi] = parts_

            for mb in range(4):
                slab = wrot.next()
                wv = slab.ap[:, 0:NJ * 512].rearrange("p (j n) -> p j n", j=NJ)
                srcv = w[:, mb * 512:(mb + 1) * 512].rearrange("(j p) n -> p j n", p=128)
                for q in range(0, NJ, 4):
                    q1 = min(q + 4, NJ)
                    P.dma('pool', wv[:, q:q1, :], srcv[:, q:q1, :], slab, writes=[slab])
                for tb, (t0, n) in enumerate(TBS):
                    s = 1 if tb == 4 else 0
                    if mb == 0 and tb == 0:
                        fd_load(0)
                    parts = fd_loaded.pop(mb * 5 + tb)
                    if mb * 5 + tb + 1 < 20:
                        fd_load(mb * 5 + tb + 1)
                    for m4 in range(4):
                        m = mb * 4 + m4
                        ps = psr.next()
                        for (buf, view, j0, j1) in parts:
                            for jx in range(j0, j1):
                                P.add('pe', lambda e, ps=ps, jx=jx, j0=j0, m4=m4, wv=wv, view=view, n=n: e.matmul(
                                    ps.ap[:, :n], wv[:, jx, m4 * 128:(m4 + 1) * 128], view[:, jx - j0, :n],
                                    start=(jx == 0), stop=(jx == NJ - 1)), reads=[slab, buf], writes=[ps])
                        x = xs.next()
                        P.dma('sp', x.ap[:, :n], xsrc[m, :, t0:t0 + n], x, reads=xregs(m, t0, n), writes=[x])
                        o = os_.next()
                        P.add('dve', lambda e, ps=ps, x=x, o=o, m=m, s=s, n=n: e.scalar_tensor_tensor(
                            out=o.ap[:, :n], in0=ps.ap[:, :n], scalar=modv[:, i, gpart * 16 + m, s:s + 1],
                            in1=x.ap[:, :n], op0=ALU.mult, op1=ALU.add), reads=[ps, x, modc], writes=[o])
                        P.dma('sp', xT_d[m, :, t0:t0 + n], o.ap[:, :n], o, reads=[o], writes=xregs(m, t0, n))

        def odd_inproj(i, j, hreg):
            w = od_w_in[j]
            ov = odvec_s.ap.rearrange("p (j v c) -> p j v c", j=2, v=5)
            sc = scr_carve([("stg", 4 * 512, F32), ("sq", 4 * 512, BF16), ("rs", 512, F32),
                            ("o0", 512, BF16), ("o1", 512, BF16), ("o2", 512, BF16),
                            ("cs", 512, F32), ("sn", 512, F32), ("t1", 512, F32), ("t2", 512, F32)])
            stg, sqb, rs = sc["stg"], sc["sq"], sc["rs"]
            outs = Rot([sc["o0"], sc["o1"], sc["o2"]])
            sgm, csb, snb, t1, t2 = sc["t1"], sc["cs"], sc["sn"], sc["t1"], sc["t2"]
            stgv = stg.ap.rearrange("p (c t) -> p c t", c=4)
            sqv = sqb.ap.rearrange("p (c t) -> p c t", c=4)
            psr = Rot([PS[0], PS[1], PS[2], PS[3], PS[4], PS[5]])
            slab = wrot.next()
            av_ = load_cols(slab, 0, w, 0, 512)
            gv_ = load_cols(slab, 8192, w, 512, 512)
            for tb, (t0, n) in enumerate(TBS):
                for c in range(4):
                    pa = psr.next()
                    pg = psr.next()
                    for (pp, vv) in ((pa, av_), (pg, gv_)):
                        for k in range(KC):
                            P.add('pe', lambda e, pp=pp, vv=vv, k=k, c=c, t0=t0, n=n: e.matmul(
                                pp.ap[:, :n], vv[:, k, c * 128:(c + 1) * 128], hT[:, k, t0:t0 + n],
                                start=(k == 0), stop=(k == KC - 1)), reads=[slab, hreg[k][tb]], writes=[pp])
                    P.add('act', lambda e, pg=pg, n=n: e.activation(out=sgm.ap[:, :n], in_=pg.ap[:, :n], func=AF.Sigmoid),
                          reads=[pg], writes=[sgm])
                    o = outs.next()
                    P.add('dve', lambda e, pa=pa, o=o, n=n: e.tensor_tensor(out=o.ap[:, :n], in0=pa.ap[:, :n], in1=sgm.ap[:, :n], op=ALU.mult),
                          reads=[pa, sgm], writes=[o])
                    P.dma('sp', gl_d[c, :, t0:t0 + n], o.ap[:, :n], o, reads=[o], accum=[glreg])
            if stop == 'ip_a':
                return
            for which in range(2):
                if stop == 'ip_b1' and which == 1:
                    return
                slab = wrot.next()
                wv = load_cols(slab, 0, w, 1024 + which * 512, 512)
                for tb, (t0, n) in enumerate(TBS):
                    pss_ = psr.next()
                    for c in range(4):
                        ps = psr.next()
                        for k in range(KC):
                            P.add('pe', lambda e, ps=ps, wv=wv, k=k, c=c, t0=t0, n=n: e.matmul(
                                ps.ap[:, :n], wv[:, k, c * 128:(c + 1) * 128], hT[:, k, t0:t0 + n],
                                start=(k == 0), stop=(k == KC - 1)), reads=[slab, hreg[k][tb]], writes=[ps])
                        P.add('dve', lambda e, ps=ps, c=c, n=n: e.tensor_copy(out=stgv[:, c, :n], in_=ps.ap[:, :n]),
                              reads=[ps], writes=[stg])
                        P.add('act', lambda e, c=c, n=n: e.activation(out=sqv[:, c, :n], in_=stgv[:, c, :n], func=AF.Square),
                              reads=[stg], writes=[sqb])
                    if stop == 'ip_b2':
                        continue
                    for c in range(4):
                        P.add('pe', lambda e, pss_=pss_, c=c, n=n: e.matmul(pss_.ap[:, :n], ones512.ap, sqv[:, c, :n],
                                                                           start=(c == 0), stop=(c == 3)),
                              reads=[sqb, ones512], writes=[pss_])
                    P.add('act', lambda e, pss_=pss_, n=n: e.activation(out=rs.ap[:, :n], in_=pss_.ap[:, :n], func=AF.Sqrt, bias=epsb.ap[:, 0:1], scale=1.0), reads=[pss_, epsb], writes=[rs])
                    P.add('dve', lambda e, pss_=pss_, n=n: e.reciprocal(out=rs.ap[:, :n], in_=rs.ap[:, :n]), reads=[rs], writes=[rs])
                    for c in range(4):
                        o = outs.next()
                        P.add('dve', lambda e, o=o, c=c, n=n, which=which: e.scalar_tensor_tensor(
                            out=o.ap[:, :n], in0=stgv[:, c, :n], scalar=ov[:, j, 3 + which, c:c + 1], in1=rs.ap[:, :n],
                            op0=ALU.mult, op1=ALU.mult), reads=[stg, rs, odvec_s, consts], writes=[o])
                        P.dma('sp', cqn_d[which * 4 + c, :, t0:t0 + n], o.ap[:, :n], o, reads=[o], accum=[cqnreg])
            if stop in ('ip_b', 'ip_b2'):
                return
            slab = wrot.next()
            krv = slab.ap[:, 0:KC * 256].rearrange("p (k n) -> p k n", k=KC)
            wsrc = w.rearrange("(k p) n -> p k n", p=128)
            for (d0, s0, nn_) in ((0, 2048, 64), (64, 2048, 64), (128, 2080, 32), (160, 2048, 32), (192, 2080, 32), (224, 2048, 32)):
                for q in range(0, KC, 4):
                    P.dma('pool', krv[:, q:q + 4, d0:d0 + nn_], wsrc[:, q:q + 4, s0:s0 + nn_], slab, writes=[slab])
            for tb, (t0, n) in enumerate(TBS):
                pk = psr.next()
                pr = psr.next()
                for (pp, c0) in ((pk, 0), (pr, 128)):
                    for k in range(KC):
                        P.add('pe', lambda e, pp=pp, c0=c0, k=k, t0=t0, n=n: e.matmul(
                            pp.ap[:, :n], krv[:, k, c0:c0 + 128], hT[:, k, t0:t0 + n],
                            start=(k == 0), stop=(k == KC - 1)), reads=[slab, hreg[k][tb]], writes=[pp])
                P.dma('sp', csb.ap[:, :n], ropeM_d[0, :, t0:t0 + n], csb, writes=[csb])
                P.dma('sp', snb.ap[:, :n], ropeM_d[1, :, t0:t0 + n], snb, writes=[snb])
                P.add('dve', lambda e, pk=pk, n=n: e.tensor_tensor(out=t1.ap[:, :n], in0=pk.ap[:, :n], in1=csb.ap[:, :n], op=ALU.mult),
                      reads=[pk, csb], writes=[t1])
                P.add('dve', lambda e, pr=pr, n=n: e.tensor_tensor(out=t2.ap[:, :n], in0=pr.ap[:, :n], in1=snb.ap[:, :n], op=ALU.mult),
                      reads=[pr, snb], writes=[t2])
                o = outs.next()
                P.add('dve', lambda e, o=o, n=n: e.tensor_tensor(out=o.ap[:, :n], in0=t1.ap[:, :n], in1=t2.ap[:, :n], op=ALU.add),
                      reads=[t1, t2], writes=[o])
                P.dma('sp', kr_d[:, t0:t0 + n], o.ap[:, :n], o, reads=[o], accum=[qkreg])

        glreg = Buf("glreg")
        cqnreg = Buf("cqnreg")

        def odd_upproj(i, j):
            hc = h_carve([("cq", 4 * T, BF16), ("ckv", 4 * T, BF16)])
            cqb, ckvb = hc["cq"], hc["ckv"]
            cqv = cqb.ap.rearrange("p (c t) -> p c t", c=4)
            ckvv = ckvb.ap.rearrange("p (c t) -> p c t", c=4)
            for c in range(4):
                P.dma('sp', cqv[:, c, :], cqn_d[c], cqb, reads=[cqnreg], writes=[cqb])
                P.dma('sp', ckvv[:, c, :], cqn_d[4 + c], ckvb, reads=[cqnreg], writes=[ckvb])
            sc = scr_carve([("o0", 512, BF16), ("o1", 512, BF16), ("o2", 512, BF16), ("o3", 512, BF16),
                            ("cs", 512, F32), ("sn", 512, F32), ("t1", 512, F32), ("t2", 512, F32)])
            outs = Rot([sc["o0"], sc["o1"], sc["o2"], sc["o3"]])
            csb, snb, t1, t2 = sc["cs"], sc["sn"], sc["t1"], sc["t2"]
            psr = Rot([PS[0], PS[1], PS[2], PS[3], PS[4], PS[5]])
            slab = wrot.next()
            wq = od_w_uq[j].rearrange("(k p) (h d) -> p k h d", p=128, d=192)
            qn = slab.ap[:, 0:4 * 1536].rearrange("p (k h d) -> p k h d", k=4, h=12)
            qr = slab.ap[:, 6144:6144 + 4 * 768].rearrange("p (k h d) -> p k h d", k=4, h=12)
            qx = slab.ap[:, 9216:9216 + 4 * 768].rearrange("p (k h d) -> p k h d", k=4, h=12)
            for k in range(4):
                for h0 in range(0, 12, 4):
                    P.dma('pool', qn[:, k, h0:h0 + 4, :], wq[:, k, h0:h0 + 4, 0:128], slab, writes=[slab])
                    P.dma('pool', qr[:, k, h0:h0 + 4, :], wq[:, k, h0:h0 + 4, 128:192], slab, writes=[slab])
                    P.dma('pool', qx[:, k, h0:h0 + 4, 0:32], wq[:, k, h0:h0 + 4, 160:192], slab, writes=[slab])
                    P.dma('pool', qx[:, k, h0:h0 + 4, 32:64], wq[:, k, h0:h0 + 4, 128:160], slab, writes=[slab])
            qnf = slab.ap[:, 0:6144].rearrange("p (k n) -> p k n", k=4)
            qrf = slab.ap[:, 6144:9216].rearrange("p (k n) -> p k n", k=4)
            qxf = slab.ap[:, 9216:12288].rearrange("p (k n) -> p k n", k=4)
            for tb, (t0, n) in enumerate(TBS):
                for h in range(12):
                    ps = psr.next()
                    for k in range(4):
                        P.add('pe', lambda e, ps=ps, k=k, h=h, t0=t0, n=n: e.matmul(
                            ps.ap[:, :n], qnf[:, k, h * 128:(h + 1) * 128], cqv[:, k, t0:t0 + n],
                            start=(k == 0), stop=(k == 3)), reads=[slab, cqb], writes=[ps])
                    o = outs.next()
                    P.add('act', lambda e, ps=ps, o=o, n=n: e.copy(out=o.ap[:, :n], in_=ps.ap[:, :n]), reads=[ps], writes=[o])
                    P.dma('sp', qT_d[h, :, t0:t0 + n], o.ap[:, :n], o, reads=[o], accum=[qkreg])
                P.dma('sp', csb.ap[:, :n], ropeM_d[0, :, t0:t0 + n], csb, writes=[csb])
                P.dma('sp', snb.ap[:, :n], ropeM_d[1, :, t0:t0 + n], snb, writes=[snb])
                for hp in range(6):
                    pk = psr.next()
                    pr = psr.next()
                    for (pp, vv) in ((pk, qrf), (pr, qxf)):
                        for k in range(4):
                            P.add('pe', lambda e, pp=pp, vv=vv, k=k, hp=hp, t0=t0, n=n: e.matmul(
                                pp.ap[:, :n], vv[:, k, hp * 128:(hp + 1) * 128], cqv[:, k, t0:t0 + n],
                                start=(k == 0), stop=(k == 3)), reads=[slab, cqb], writes=[pp])
                    P.add('dve', lambda e, pk=pk, n=n: e.tensor_tensor(out=t1.ap[:, :n], in0=pk.ap[:, :n], in1=csb.ap[:, :n], op=ALU.mult),
                          reads=[pk, csb], writes=[t1])
                    P.add('dve', lambda e, pr=pr, n=n: e.tensor_tensor(out=t2.ap[:, :n], in0=pr.ap[:, :n], in1=snb.ap[:, :n], op=ALU.mult),
                          reads=[pr, snb], writes=[t2])
                    o = outs.next()
                    P.add('dve', lambda e, o=o, n=n: e.tensor_tensor(out=o.ap[:, :n], in0=t1.ap[:, :n], in1=t2.ap[:, :n], op=ALU.add),
                          reads=[t1, t2], writes=[o])
                    P.dma('sp', qr_d[hp, :, t0:t0 + n], o.ap[:, :n], o, reads=[o], accum=[qkreg])
            slab = wrot.next()
            wkv = od_w_ukv[j].rearrange("(k p) (h d) -> p k h d", p=128, d=256)
            kn = slab.ap[:, 0:6144].rearrange("p (k h d) -> p k h d", k=4, h=12)
            vv_ = slab.ap[:, 6144:12288].rearrange("p (k h d) -> p k h d", k=4, h=12)
            for k in range(4):
                for h0 in range(0, 12, 4):
                    P.dma('pool', kn[:, k, h0:h0 + 4, :], wkv[:, k, h0:h0 + 4, 0:128], slab, writes=[slab])
                    P.dma('pool', vv_[:, k, h0:h0 + 4, :], wkv[:, k, h0:h0 + 4, 128:256], slab, writes=[slab])
            knf = slab.ap[:, 0:6144].rearrange("p (k n) -> p k n", k=4)
            vf = slab.ap[:, 6144:12288].rearrange("p (k n) -> p k n", k=4)
            for tb, (t0, n) in enumerate(TBS):
                for h in range(12):
                    ps = psr.next()
                    for k in range(4):
                        P.add('pe', lambda e, ps=ps, k=k, h=h, t0=t0, n=n: e.matmul(
                            ps.ap[:, :n], knf[:, k, h * 128:(h + 1) * 128], ckvv[:, k, t0:t0 + n],
                            start=(k == 0), stop=(k == 3)), reads=[slab, ckvb], writes=[ps])
                    o = outs.next()
                    P.add('act', lambda e, ps=ps, o=o, n=n: e.copy(out=o.ap[:, :n], in_=ps.ap[:, :n]), reads=[ps], writes=[o])
                    P.dma('sp', kT_d[h, :, t0:t0 + n], o.ap[:, :n], o, reads=[o], accum=[qkreg])
            for tt in range(NT):
                for vg in range(3):
                    ps = psr.next()
                    for k in range(4):
                        P.add('pe', lambda e, ps=ps, k=k, vg=vg, tt=tt: e.matmul(
                            ps.ap, ckvv[:, k, tt * 128:(tt + 1) * 128], vf[:, k, vg * 512:(vg + 1) * 512],
                            start=(k == 0), stop=(k == 3)), reads=[slab, ckvb], writes=[ps])
                    o = outs.next()
                    P.add('act', lambda e, ps=ps, o=o: e.copy(out=o.ap, in_=ps.ap), reads=[ps], writes=[o])
                    P.dma('sp', v_d[tt, :, vg * 512:(vg + 1) * 512], o.ap, o, reads=[o], accum=[qkreg])

        GP = 15
        LW = L + 2 * GP
        CW = CL + 2 * GP

        def odd_conv(i, j):
            hc = h_carve([("gl", 4 * (LW + CW), BF16), ("u", 4 * T, F32)])
            glb, ub = hc["gl"], hc["u"]
            glv = glb.ap.rearrange("p (c t) -> p c t", c=4)
            uv = ub.ap.rearrange("p (c t) -> p c t", c=4)
            sc = scr_carve([("acc0", 512, F32), ("acc1", 512, F32), ("usq", 4 * 512, F32), ("mu", 512, F32), ("var", 512, F32),
                            ("t", 512, F32), ("o0", 512, BF16), ("o1", 512, BF16)])
            usq, mu, var, tt_ = sc["usq"], sc["mu"], sc["var"], sc["t"]
            outs = Rot([sc["o0"], sc["o1"]])
            usqv = usq.ap.rearrange("p (c t) -> p c t", c=4)
            ov = odvec_s.ap.rearrange("p (j v c) -> p j v c", j=2, v=5)
            cwv = odcw_s.ap.rearrange("p (j c k) -> p j c k", j=2, c=4)
            P.add('dve', lambda e: e.memset(glb.ap, 0.0), writes=[glb])
            for c in range(4):
                P.dma('sp', glv[:, c, GP:GP + L], gl_d[c, :, 0:L], glb, reads=[glreg], writes=[glb])
                P.dma('sp', glv[:, c, LW + GP:LW + GP + CL], gl_d[c, :, L:T], glb, reads=[glreg], writes=[glb])
            accs = Rot([sc["acc0"], sc["acc1"]])
            for c in range(4):
                for tb, (t0, n) in enumerate(TBS):
                    base = t0 if tb < 4 else LW
                    acc = accs.next()
                    P.add('dve', lambda e, acc=acc, c=c, base=base, n=n: e.tensor_scalar(
                        out=acc.ap[:, :n], in0=glv[:, c, base:base + n], scalar1=cwv[:, j, c, 0:1], scalar2=None, op0=ALU.mult),
                        reads=[glb, odcw_s, consts], writes=[acc])
                    for k in range(1, 31):
                        P.add('dve', lambda e, acc=acc, c=c, k=k, base=base, n=n: e.scalar_tensor_tensor(
                            out=acc.ap[:, :n], in0=glv[:, c, base + k:base + k + n], scalar=cwv[:, j, c, k:k + 1],
                            in1=acc.ap[:, :n], op0=ALU.mult, op1=ALU.add), reads=[glb, acc, odcw_s], writes=[acc])
                    P.add('act', lambda e, acc=acc, c=c, t0=t0, n=n: e.activation(
                        out=uv[:, c, t0:t0 + n], in_=acc.ap[:, :n], func=AF.Identity, bias=ov[:, j, 0, c:c + 1], scale=1.0),
                        reads=[acc, odvec_s, consts], writes=[ub])
            pm = Rot([PS[3], PS[4]])
            pv = Rot([PS[5], PS[6]])
            for tb, (t0, n) in enumerate(TBS):
                p1 = pm.next()
                p2 = pv.next()
                for c in range(4):
                    P.add('act', lambda e, c=c, t0=t0, n=n: e.activation(out=usqv[:, c, :n], in_=uv[:, c, t0:t0 + n], func=AF.Square),
                          reads=[ub], writes=[usq])
                for c in range(4):
                    P.add('pe', lambda e, p1=p1, c=c, t0=t0, n=n: e.matmul(p1.ap[:, :n], ones512f.ap, uv[:, c, t0:t0 + n],
                                                                          start=(c == 0), stop=(c == 3)),
                          reads=[ub, ones512f], writes=[p1])
                for c in range(4):
                    P.add('pe', lambda e, p2=p2, c=c, n=n: e.matmul(p2.ap[:, :n], ones512f.ap, usqv[:, c, :n],
                                                                   start=(c == 0), stop=(c == 3)),
                          reads=[usq, ones512f], writes=[p2])
                P.add('dve', lambda e, p1=p1, n=n: e.tensor_copy(out=mu.ap[:, :n], in_=p1.ap[:, :n]), reads=[p1], writes=[mu])
                P.add('dve', lambda e, n=n: e.tensor_tensor(out=var.ap[:, :n], in0=mu.ap[:, :n], in1=mu.ap[:, :n], op=ALU.mult),
                      reads=[mu], writes=[var])
                P.add('dve', lambda e, p2=p2, n=n: e.tensor_tensor(out=var.ap[:, :n], in0=p2.ap[:, :n], in1=var.ap[:, :n], op=ALU.subtract),
                      reads=[p2, var], writes=[var])
                P.add('act', lambda e, n=n: e.activation(out=var.ap[:, :n], in_=var.ap[:, :n], func=AF.Sqrt, bias=epsb.ap[:, 0:1], scale=1.0), reads=[var, epsb], writes=[var])
                P.add('dve', lambda e, n=n: e.reciprocal(out=var.ap[:, :n], in_=var.ap[:, :n]), reads=[var], writes=[var])
                for c in range(4):
                    P.add('dve', lambda e, c=c, t0=t0, n=n: e.tensor_tensor(out=tt_.ap[:, :n], in0=uv[:, c, t0:t0 + n], in1=mu.ap[:, :n], op=ALU.subtract),
                          reads=[ub, mu], writes=[tt_])
                    P.add('dve', lambda e, n=n: e.tensor_tensor(out=tt_.ap[:, :n], in0=tt_.ap[:, :n], in1=var.ap[:, :n], op=ALU.mult),
                          reads=[tt_, var], writes=[tt_])
                    o = outs.next()
                    P.add('act', lambda e, o=o, c=c, n=n: e.activation(out=o.ap[:, :n], in_=tt_.ap[:, :n], func=AF.Silu,
                                                                      bias=ov[:, j, 2, c:c + 1], scale=ov[:, j, 1, c:c + 1]),
                          reads=[tt_, odvec_s, consts], writes=[o])
                    P.dma('sp', aT_d[c, :, t0:t0 + n], o.ap[:, :n], o, reads=[o], writes=[aTreg[c][tb]])

        def final_norm():
            xsrc = state["xsrc"]
            sc = scr_carve([("x0", 512, F32), ("x1", 512, F32), ("x2", 512, F32), ("x3", 512, F32),
                            ("sq0", 512, BF16), ("sq1", 512, BF16), ("rs0", 512, F32), ("rs1", 512, F32),
                            ("t0", 512, F32), ("t1", 512, F32), ("t2", 512, F32)])
            xs = Rot([sc["x0"], sc["x1"], sc["x2"], sc["x3"]])
            sqs = Rot([sc["sq0"], sc["sq1"]])
            rss = Rot([sc["rs0"], sc["rs1"]])
            ts = Rot([sc["t0"], sc["t1"], sc["t2"]])
            for tb, (t0, n) in enumerate(TBS[:4]):
                ps = PS[tb % 2]
                for c in range(KC):
                    x = xs.next()
                    P.dma('sp', x.ap[:, :n], xsrc[c, :, t0:t0 + n], x, reads=xregs(c, t0, n), writes=[x])
                    sq = sqs.next()
                    P.add('act', lambda e, x=x, sq=sq, n=n: e.activation(out=sq.ap[:, :n], in_=x.ap[:, :n], func=AF.Square),
                          reads=[x], writes=[sq])
                    P.add('pe', lambda e, ps=ps, sq=sq, c=c, n=n: e.matmul(ps.ap[:, :n], onesD.ap, sq.ap[:, :n],
                                                                          start=(c == 0), stop=(c == KC - 1)),
                          reads=[sq, onesD], writes=[ps])
                rs = rss.next()
                P.add('act', lambda e, ps=ps, rs=rs, n=n: e.activation(out=rs.ap[:, :n], in_=ps.ap[:, :n], func=AF.Sqrt, bias=epsb.ap[:, 0:1], scale=1.0), reads=[ps, epsb], writes=[rs])
                P.add('dve', lambda e, ps=ps, rs=rs, n=n: e.reciprocal(out=rs.ap[:, :n], in_=rs.ap[:, :n]), reads=[rs], writes=[rs])
                for c in range(KC):
                    x = xs.next()
                    P.dma('sp', x.ap[:, :n], xsrc[c, :, t0:t0 + n], x, reads=xregs(c, t0, n), writes=[x])
                    t = ts.next()
                    P.add('dve', lambda e, x=x, t=t, rs=rs, c=c, n=n: e.scalar_tensor_tensor(
                        out=t.ap[:, :n], in0=x.ap[:, :n], scalar=fing_s.ap[:, c:c + 1], in1=rs.ap[:, :n],
                        op0=ALU.mult, op1=ALU.mult), reads=[x, rs, fing_s, consts], writes=[t])
                    P.dma('sp', out_d[c, :, t0:t0 + n], t.ap[:, :n], t, reads=[t], accum=[outreg])

        def copy_out():
            xsrc = state["xsrc"]
            sc = scr_carve([("x0", 512, F32), ("x1", 512, F32), ("x2", 512, F32)])
            xs = Rot([sc["x0"], sc["x1"], sc["x2"]])
            for tb, (t0, n) in enumerate(TBS[:4]):
                for c in range(KC):
                    x = xs.next()
                    P.dma('sp', x.ap[:, :n], xsrc[c, :, t0:t0 + n], x, reads=xregs(c, t0, n), writes=[x])
                    P.dma('sp', out_d[c, :, t0:t0 + n], x.ap[:, :n], x, reads=[x], accum=[outreg])

        outreg = Buf("outreg")

        for i in range(NL):
            j = i // 2
            hreg = new_hreg()
            modulate(i, 0, 1, hreg)
            if i % 2 == 0:
                even_inproj(i, j, hreg)
                P.end_phase(dummy)
                fourier(i)
                P.end_phase(dummy)
                attention(12,
                          lambda h: [(kT_d[h // 3], 128, 0)],
                          lambda h: [(qT_d[h], 128, 0)],
                          lambda h: v_d[:, :, (h // 3) * 128:(h // 3 + 1) * 128],
                          lambda h: h // 3, 128 ** -0.5, 4)
                P.end_phase(dummy)
                outproj(i, ev_w_out[j], 2)
            else:
                if stop == 'mod':
                    P.end_phase(dummy)
                    break
                odd_inproj(i, j, hreg)
                P.end_phase(dummy)
                if stop in ('inproj', 'ip_a', 'ip_b', 'ip_b1', 'ip_b2'):
                    break
                odd_upproj(i, j)
                P.end_phase(dummy)
                if stop == 'upproj':
                    break
                odd_conv(i, j)
                P.end_phase(dummy)
                if stop == 'conv':
                    break
                attention(12,
                          lambda h: [(kT_d[h], 128, 0), (kr_d[(h % 2) * 64:(h % 2) * 64 + 64, :], 64, (h % 2) * 64)],
                          lambda h: [(qT_d[h], 128, 0), (qr_d[h // 2, (h % 2) * 64:(h % 2) * 64 + 64, :], 64, (h % 2) * 64)],
                          lambda h: v_d[:, :, h * 128:(h + 1) * 128],
                          lambda h: h, 192 ** -0.5, 4)
                P.end_phase(dummy)
                if stop == 'attn':
                    break
                outproj(i, od_w_out[j], 2)
                if stop == 'outproj':
                    P.end_phase(dummy)
                    break
            P.end_phase(dummy)
            hreg = new_hreg()
            modulate(i, 3, 4, hreg)
            ffn_up(i, hreg)
            P.end_phase(dummy)
            ffn_down(i, 5)
            P.end_phase(dummy)
        if final:
            final_norm()
        else:
            copy_out()
        P.add('sp', lambda e: e.nop(), reads=[outreg])
        P.emit()
    return nc, dt_in


def _host_consts():
    bf = ml_dtypes.bfloat16
    c = {}
    c["ident_bf"] = np.eye(128, dtype=np.float32).astype(bf)
    c["ident_f"] = np.eye(128, dtype=np.float32)
    n = np.arange(128)
    a = 2 * np.pi * np.outer(n, n) / 128.0
    c["dft128"] = (np.concatenate([np.cos(a), np.sin(a)], axis=1) / np.sqrt(128.0)).astype(np.float32).astype(bf)

    def dft(N):
        k = np.arange(N, dtype=np.int64)
        ph = (np.outer(k, k) % N).astype(np.float64) * (2 * np.pi / N)
        return np.stack([np.cos(ph), -np.sin(ph)], 0) / np.sqrt(float(N))
    c["dftL"] = dft(L).astype(np.float32).astype(bf)
    c["dftC"] = dft(CL).astype(np.float32).astype(bf)

    def tables(rot_dim):
        rows = L // 64
        row = np.repeat(np.arange(rows, dtype=np.float32), 64)
        col = np.tile(np.arange(64, dtype=np.float32), rows)
        nn = rot_dim // 4
        inv = np.power(np.float32(10000.0), -np.arange(nn, dtype=np.float32) / nn).astype(np.float32)
        ang = np.concatenate([row[:, None] * inv, col[:, None] * inv], axis=-1).astype(np.float32)
        return np.cos(ang).astype(np.float32), np.sin(ang).astype(np.float32)
    ca, sa = tables(128)
    c["ropeA"] = np.concatenate([ca, sa], axis=1).reshape(16, 128, 128).astype(np.float32)
    cm, sm = tables(64)
    cosT = np.ones((64, T), np.float32)
    sinT = np.zeros((64, T), np.float32)
    cosT[:32, :L] = cm.T
    cosT[32:, :L] = cm.T
    sinT[:32, :L] = -sm.T
    sinT[32:, :L] = sm.T
    c["ropeM"] = np.stack([np.concatenate([cosT, cosT], 0), np.concatenate([sinT, sinT], 0)], 0)
    return c


def _col(v, nchunk):
    return np.ascontiguousarray(v.reshape(nchunk, 128).T)


_CACHE = {}
ACTIVE = [0, 1, 4, 5]


def make_in_maps(inputs, ncores=8):
    f = lambda a: np.ascontiguousarray(np.asarray(a, dtype=np.float32))
    x, c, ctx, c_ctx = f(inputs["x"]), f(inputs["c"]), f(inputs["ctx"]), f(inputs["c_ctx"])
    shared = dict(_host_consts())
    shared["ada_w"] = f(inputs["ada_w"])
    ab = f(inputs["ada_b"])
    shared["ada_bc"] = np.ascontiguousarray(ab.reshape(DEPTH, 96, 128).transpose(2, 0, 1).reshape(128, DEPTH * 96))
    shared["ev_w_in"] = f(inputs["ev_w_in"])
    shared["ev_w_out"] = f(inputs["ev_w_out"])
    g = np.stack([f(inputs["ev_q_gain"]), f(inputs["ev_k_gain"])], 1)
    shared["gains"] = np.ascontiguousarray(np.broadcast_to(g.reshape(1, 512), (128, 512)))
    shared["od_w_in"] = f(inputs["od_w_in"])
    shared["od_w_uq"] = f(inputs["od_w_uq"])
    shared["od_w_ukv"] = f(inputs["od_w_ukv"])
    shared["od_w_out"] = f(inputs["od_w_out"])
    cw = f(inputs["od_conv_w"])
    shared["od_cw"] = np.ascontiguousarray(cw.reshape(2, 31, 4, 128).transpose(3, 0, 2, 1).reshape(128, 2 * 4 * 31))
    vecs = np.stack([f(inputs[k]) for k in ("od_conv_b", "od_ln_g", "od_ln_b", "od_q_norm", "od_kv_norm")], 1)
    shared["od_vec"] = np.ascontiguousarray(vecs.reshape(2, 5, 4, 128).transpose(3, 0, 1, 2).reshape(128, 40))
    shared["ffn_w_up"] = f(inputs["ffn_w_up"])
    shared["ffn_w_down"] = f(inputs["ffn_w_down"])
    fcw = f(inputs["ffn_conv_w"])
    shared["ffn_cw"] = np.ascontiguousarray(fcw.reshape(DEPTH, 3, NJ, 128).transpose(3, 0, 2, 1).reshape(128, DEPTH * NJ * 3))
    fcb = f(inputs["ffn_conv_b"])
    shared["ffn_cb"] = np.ascontiguousarray(fcb.reshape(DEPTH, NJ, 128).transpose(2, 0, 1).reshape(128, DEPTH * NJ))
    shared["fin_g"] = _col(f(inputs["final_norm"]), KC)
    maps = []
    zeros = {k: np.zeros_like(v) for k, v in shared.items()}
    for core in range(ncores):
        if core not in ACTIVE:
            m = dict(zeros)
            m["xT_in"] = np.zeros((KC, 128, T), np.float32)
            m["c2"] = np.zeros((128, KC * 2), np.float32)
            maps.append(m)
            continue
        b = ACTIVE.index(core)
        m = dict(shared)
        xa = np.concatenate([x[b], ctx[b]], axis=0)
        m["xT_in"] = np.ascontiguousarray(xa.T.reshape(KC, 128, T))
        cc = np.stack([_col(c[b], KC), _col(c_ctx, KC)], -1)
        m["c2"] = np.ascontiguousarray(cc.reshape(128, KC * 2))
        maps.append(m)
    return maps


def kernel(**inputs):
    if "nc" not in _CACHE:
        _CACHE["nc"] = build_program()
    nc, _ = _CACHE["nc"]
    maps = make_in_maps(inputs)
    res = run_bass_kernel_spmd(nc, maps, core_ids=list(range(8)))
    outs = []
    for b in range(4):
        yT = np.asarray(res.results[ACTIVE[b]]["yT"], dtype=np.float32)
        outs.append(yT.reshape(D, L).T)
    return np.ascontiguousarray(np.stack(outs, 0)).astype(np.float32)
```

```python
import numpy as np
import ml_dtypes
from contextlib import ExitStack
import concourse.bass as bass
import concourse.mybir as mybir
from concourse.bass_utils import run_bass_kernel_spmd

F32 = mybir.dt.float32
BF16 = mybir.dt.bfloat16
AF = mybir.ActivationFunctionType
ALU = mybir.AluOpType
AX = mybir.AxisListType

D = 2048
KC = 16
L = 2048
CL = 256
T = L + CL
NT = T // 128
DEPTH = 4
DFF = 5504
NJ = DFF // 128
EPS = 1e-6
TBS = [(0, 512), (512, 512), (1024, 512), (1536, 512), (2048, 256)]
TB2 = [(i * 256, 256) for i in range(9)]
WS_COLS = 22016
SAME_ENG_SYNC = True


class Buf:
    __slots__ = ('name', 'ap', 'w', 'r', 'sem', 'ndma', 'ws', 'phase')

    def __init__(self, name, ap=None):
        self.name = name
        self.ap = ap
        self.w = None
        self.r = {}
        self.ws = {}
        self.phase = False
        self.sem = None
        self.ndma = 0


class Op:
    __slots__ = ('eng', 'fn', 'deps', 'tok', 'sig', 'dma', 'waits', 'ph')


class Rec:
    def __init__(self):
        self.call = None

    def __getattr__(self, name):
        def f(*a, **k):
            self.call = (name, a, k)
            return self
        return f


class Prog:
    ENGS = ('pe', 'act', 'dve', 'pool', 'sp')

    def __init__(self, nc, stack):
        self.nc = nc
        self.stack = stack
        self.ops = []
        self.barrier = None
        self.esem = {e: [stack.enter_context(nc.semaphore('s_%s%d' % (e, r))) for r in range(3)] for e in self.ENGS}
        self.phase_no = 0
        self.nsem = 0
        self.phase_bufs = []
        self.scr_used = False
        self.h_used = False
        self.dummy = None
        self.sem_pool = []
        self.pool_idx = 0
        self.semcnt = {}

    def sbuf(self, name, cols, dt):
        t = self.stack.enter_context(self.nc.sbuf_tensor(name, [128, cols], dt))
        return Buf(name, t[:])

    def psum(self, name, cols, dt):
        t = self.stack.enter_context(self.nc.psum_tensor(name, [128, cols], dt))
        return Buf(name, t[:])

    def region(self, name, ap=None, phase=True):
        b = Buf(name, ap)
        b.w = self.barrier
        if phase:
            b.phase = True
            self.phase_bufs.append(b)
        return b

    def end_phase(self, dummy=None):
        dummy = dummy or self.dummy
        bufs = self.phase_bufs
        self.phase_bufs = []
        self.scr_used = False
        self.h_used = False
        i = self.add('dve', lambda e: e.memset(dummy.ap[:, 0:1], 0.0), writes=[dummy] + bufs)
        self.barrier = i
        self.phase_no += 1
        self.pool_idx = (self.phase_no % 4) * 12

    def add(self, eng, fn, reads=(), writes=(), dma=None, accum=()):
        i = len(self.ops)
        deps = set()
        for b in reads:
            if b.w is not None:
                deps.add(b.w)
            deps.update(b.ws.values())
        for b in writes:
            if b.w is not None:
                deps.add(b.w)
            deps.update(b.r.values())
            deps.update(b.ws.values())
        for b in accum:
            if b.w is not None:
                deps.add(b.w)
            deps.update(b.r.values())
        op = Op()
        op.eng = eng
        rec = Rec()
        fn(rec)
        op.fn = rec.call
        op.dma = dma
        op.ph = self.phase_no % 3
        op.sig = False
        op.tok = None
        op.waits = None
        if dma is not None:
            if dma.sem is None:
                if dma.phase:
                    while self.pool_idx >= len(self.sem_pool):
                        self.sem_pool.append(self.stack.enter_context(self.nc.semaphore('q%d' % len(self.sem_pool))))
                    assert self.pool_idx < (self.phase_no % 4) * 12 + 12
                    dma.sem = self.sem_pool[self.pool_idx]
                    self.pool_idx += 1
                else:
                    dma.sem = self.stack.enter_context(self.nc.semaphore('d%d' % self.nsem))
                    self.nsem += 1
            k_ = id(dma.sem)
            self.semcnt[k_] = self.semcnt.get(k_, 0) + 1
            op.tok = (dma.sem, 16 * self.semcnt[k_])
            key = ('dma', id(dma))
        else:
            key = eng
        for b in writes:
            b.w = i
            b.r = {}
            b.ws = {}
        for b in accum:
            b.ws[key] = i
        for b in reads:
            if b.w != i:
                b.r[key] = i
        deps.discard(i)
        op.deps = deps
        self.ops.append(op)
        return i

    def dma(self, eng, out_ap, in_ap, tile, reads=(), writes=(), accum=(), **kw):
        return self.add(eng, lambda e: e.dma_start(out=out_ap, in_=in_ap, **kw),
                        reads=reads, writes=writes, dma=tile, accum=accum)

    def finalize(self):
        ops = self.ops
        for op in ops:
            for d in op.deps:
                dop = ops[d]
                if dop.dma is None:
                    if dop.eng == op.eng and op.dma is None and (op.eng == 'pe' or not SAME_ENG_SYNC):
                        continue
                    dop.sig = True
        cnt = {(e, r): 0 for e in self.ENGS for r in range(3)}
        for op in ops:
            if op.dma is None and op.sig:
                cnt[(op.eng, op.ph)] += 1
                op.tok = (self.esem[op.eng][op.ph], cnt[(op.eng, op.ph)])
        self.max_cnt = max(cnt.values())
        known = {e: {} for e in self.ENGS}
        for op in ops:
            need = {}
            for d in op.deps:
                dop = ops[d]
                if dop.tok is None:
                    continue
                if dop.dma is None and dop.eng == op.eng and op.dma is None and (op.eng == 'pe' or not SAME_ENG_SYNC):
                    continue
                sem, val = dop.tok
                k = id(sem)
                if known[op.eng].get(k, 0) >= val:
                    continue
                if k not in need or need[k][1] < val:
                    need[k] = (sem, val)
            for k, (sem, val) in need.items():
                known[op.eng][k] = val
            op.waits = list(need.values())

    def emit(self):
        self.finalize()
        mx = {}
        for op in self.ops:
            if op.tok is not None:
                mx[op.tok[0].name] = max(mx.get(op.tok[0].name, 0), op.tok[1])
        self.max_tok = mx
        print("[kernel] ops=%d max_sem_value=%d nsems=%d" % (len(self.ops), max(mx.values()), len(mx)))
        nc = self.nc
        ops = self.ops

        def run(eng, e):
            for op in ops:
                if op.eng != eng:
                    continue
                for sem, val in op.waits:
                    e.wait_ge(sem, val)
                name, a, k = op.fn
                ins = getattr(e, name)(*a, **k)
                if op.dma is not None:
                    ins.then_inc(op.tok[0], 16)
                elif op.sig:
                    ins.then_inc(op.tok[0], 1)

        with nc.Block() as block:
            @block.tensor
            def _(e):
                run('pe', e)

            @block.scalar
            def _(e):
                run('act', e)

            @block.vector
            def _(e):
                run('dve', e)

            @block.gpsimd
            def _(e):
                run('pool', e)

            @block.sync
            def _(e):
                run('sp', e)


class Rot:
    def __init__(self, items):
        self.items = items
        self.i = 0

    def next(self):
        b = self.items[self.i % len(self.items)]
        self.i += 1
        return b


def build_program(NL=DEPTH, final=True, stop=None):
    nc = bass.Bass("TRN2", target_bir_lowering=False)
    dt_in = {}

    def ein(name, shape, dt=F32):
        dt_in[name] = dt
        return nc.dram_tensor(name, list(shape), dt, kind="ExternalInput").ap()

    xT_in = ein("xT_in", [KC, 128, T])
    c2 = ein("c2", [128, KC * 2])
    ada_w = ein("ada_w", [DEPTH, D, 6 * D])
    ada_bc = ein("ada_bc", [128, DEPTH * 96])
    ev_w_in = ein("ev_w_in", [2, D, 3072])
    ev_w_out = ein("ev_w_out", [2, D, D])
    gains = ein("gains", [128, 2 * 2 * 128])
    od_w_in = ein("od_w_in", [2, D, 2112])
    od_w_uq = ein("od_w_uq", [2, 512, 2304])
    od_w_ukv = ein("od_w_ukv", [2, 512, 3072])
    od_w_out = ein("od_w_out", [2, D, D])
    od_cw = ein("od_cw", [128, 2 * 4 * 31])
    od_vec = ein("od_vec", [128, 2 * 5 * 4])
    ffn_w_up = ein("ffn_w_up", [DEPTH, D, 2 * DFF])
    ffn_w_down = ein("ffn_w_down", [DEPTH, DFF, D])
    ffn_cw = ein("ffn_cw", [128, DEPTH * NJ * 3])
    ffn_cb = ein("ffn_cb", [128, DEPTH * NJ])
    fin_g = ein("fin_g", [128, KC])
    ident_bf_d = ein("ident_bf", [128, 128], BF16)
    ident_f_d = ein("ident_f", [128, 128])
    dft128_d = ein("dft128", [128, 256], BF16)
    dftL_d = ein("dftL", [2, L, L], BF16)
    dftC_d = ein("dftC", [2, CL, CL], BF16)
    ropeA_d = ein("ropeA", [16, 128, 128])
    ropeM_d = ein("ropeM", [2, 128, T])
    out_d = nc.dram_tensor("yT", [KC, 128, L], F32, kind="ExternalOutput").ap()

    xT_d = nc.dram_tensor("xT_s", [KC, 128, T], F32).ap()
    aT_d = nc.dram_tensor("aT_s", [KC, 128, T], BF16).ap()
    qT_d = nc.dram_tensor("qT_s", [12, 128, T], BF16).ap()
    kT_d = nc.dram_tensor("kT_s", [12, 128, T], BF16).ap()
    v_d = nc.dram_tensor("v_s", [NT, 128, 1536], BF16).ap()
    fT_d = nc.dram_tensor("fT_s", [4, 128, T], BF16).ap()
    ffa_d = nc.dram_tensor("ffa_s", [NJ, 128, T], BF16).ap()
    cqn_d = nc.dram_tensor("cqn_s", [8, 128, T], BF16).ap()
    gl_d = nc.dram_tensor("gl_s", [4, 128, T], BF16).ap()
    kr_d = nc.dram_tensor("kr_s", [128, T], BF16).ap()
    qr_d = nc.dram_tensor("qr_s", [6, 128, T], BF16).ap()

    with ExitStack() as stack:
        P = Prog(nc, stack)

        H = P.sbuf("H", 36864, BF16)
        WS = [P.sbuf("WS0", WS_COLS, BF16), P.sbuf("WS1", WS_COLS, BF16)]
        SCR = P.sbuf("SCR", 13312, BF16)
        wrot = Rot(WS)
        modc = P.sbuf("modc", DEPTH * 96 * 2, F32)
        modv = modc.ap.rearrange("p (i f s) -> p i f s", i=DEPTH, f=96)
        c2s = P.sbuf("c2s", KC * 2, F32)
        adab = P.sbuf("adab", DEPTH * 96, F32)
        gains_s = P.sbuf("gains_s", 512, F32)
        odcw_s = P.sbuf("odcw_s", 2 * 4 * 31, F32)
        odvec_s = P.sbuf("odvec_s", 40, F32)
        ffcw_s = P.sbuf("ffcw_s", DEPTH * NJ * 3, F32)
        ffcb_s = P.sbuf("ffcb_s", DEPTH * NJ, F32)
        fing_s = P.sbuf("fing_s", KC, F32)
        ident_bf = P.sbuf("ident_bf_s", 128, BF16)
        ident_f = P.sbuf("ident_f_s", 128, F32)
        dft128 = P.sbuf("dft128_s", 256, BF16)
        onesD = P.sbuf("onesD", 128, BF16)
        ones1 = P.sbuf("ones1", 128, BF16)
        ones512 = P.sbuf("ones512", 128, BF16)
        ones512f = P.sbuf("ones512f", 128, F32)
        epsb = P.sbuf("epsb", 4, F32)
        dummy = P.sbuf("dummy_t", 4, F32)
        consts = Buf("consts")
        P.dummy = dummy

        PS = [P.psum("ps%d" % i, 512, F32) for i in range(8)]

        for dst, src in ((c2s, c2), (adab, ada_bc), (gains_s, gains), (odcw_s, od_cw), (odvec_s, od_vec),
                         (ffcw_s, ffn_cw), (ffcb_s, ffn_cb), (fing_s, fin_g), (ident_bf, ident_bf_d),
                         (ident_f, ident_f_d), (dft128, dft128_d)):
            P.dma('sp', dst.ap, src, consts, writes=[consts, dst])
        P.add('dve', lambda e: e.memset(onesD.ap, 1.0 / D), writes=[onesD])
        P.add('dve', lambda e: e.memset(ones1.ap, 1.0), writes=[ones1])
        P.add('dve', lambda e: e.memset(ones512.ap, 1.0 / 512), writes=[ones512])
        P.add('dve', lambda e: e.memset(ones512f.ap, 1.0 / 512), writes=[ones512f])
        P.add('dve', lambda e: e.memset(epsb.ap, EPS), writes=[epsb])

        def scr_carve(specs):
            if P.scr_used:
                P.end_phase()
            P.scr_used = True
            out = {}
            off = 0
            for name, cols, dt in specs:
                n16 = cols * (2 if dt == F32 else 1)
                ap = SCR.ap[:, off:off + n16]
                if dt == F32:
                    ap = ap.bitcast(F32)
                out[name] = P.region(name, ap)
                off += n16
            assert off <= 13312, off
            return out

        def h_carve(specs):
            if P.h_used:
                P.end_phase()
            P.h_used = True
            out = {}
            off = 0
            for name, cols, dt in specs:
                n16 = cols * (2 if dt == F32 else 1)
                ap = H.ap[:, off:off + n16]
                if dt == F32:
                    ap = ap.bitcast(F32)
                out[name] = P.region(name, ap)
                off += n16
            assert off <= 36864, off
            return out

        def w_load(parts, f32=False):
            slab = wrot.next()
            for off, n, src, shape in parts:
                base = slab.ap.bitcast(F32) if f32 else slab.ap
                dst = base[:, off:off + n]
                if shape is not None:
                    dst = dst.rearrange(shape[0], **shape[1])
                P.dma('sp' if f32 else 'pool', dst, src, slab, writes=[slab])
            return slab

        sc = scr_carve([("silu", KC * 2, F32), ("silub", KC * 2, BF16),
                        ("wb0", 4096, BF16), ("wb1", 4096, BF16), ("wb2", 4096, BF16)])
        silu = sc["silu"]
        silub = sc["silub"]
        P.add('act', lambda e: e.activation(out=silu.ap, in_=c2s.ap, func=AF.Silu), reads=[c2s], writes=[silu])
        P.add('dve', lambda e: e.tensor_copy(out=silub.ap, in_=silu.ap), reads=[silu], writes=[silub])
        siluv = silub.ap.rearrange("p (k s) -> p k s", k=KC)
        adabv = adab.ap.rearrange("p (i f) -> p i f", i=DEPTH)
        wbs = Rot([sc["wb0"], sc["wb1"], sc["wb2"]])
        for i in range(NL):
            for nb in range(24):
                src = ada_w[i, :, nb * 512:(nb + 1) * 512].rearrange("(k p) n -> p k n", p=128)
                slab = wrot.next()
                sv = slab.ap.bitcast(F32)[:, 0:8192].rearrange("p (k n) -> p k n", k=KC)
                for q in range(4):
                    P.dma('sp', sv[:, q * 4:(q + 1) * 4, :], src[:, q * 4:(q + 1) * 4, :], slab, writes=[slab])
                pt = PS[nb % 2]
                for half in range(2):
                    wb = wbs.next()
                    wbv = wb.ap.rearrange("p (k n) -> p k n", k=KC)
                    if half == 0:
                        P.add('dve', lambda e, wbv=wbv, sv=sv: e.tensor_copy(out=wbv, in_=sv[:, :, 0:256]),
                              reads=[slab], writes=[wb])
                    else:
                        P.add('act', lambda e, wbv=wbv, sv=sv: e.copy(out=wbv, in_=sv[:, :, 256:512]),
                              reads=[slab], writes=[wb])
                    for q2 in range(2):
                        q = half * 2 + q2
                        for k in range(KC):
                            P.add('pe', lambda e, pt=pt, k=k, q=q, q2=q2, wbv=wbv: e.matmul(
                                pt.ap[:, q * 2:q * 2 + 2], wbv[:, k, q2 * 128:(q2 + 1) * 128], siluv[:, k, :],
                                start=(k == 0), stop=(k == KC - 1)), reads=[silub, wb], writes=[pt])
                P.add('dve', lambda e, pt=pt, i=i, nb=nb: e.tensor_tensor(
                    out=modv[:, i, nb * 4:nb * 4 + 4, :],
                    in0=pt.ap[:, 0:8].rearrange("p (f s) -> p f s", f=4),
                    in1=adabv[:, i, nb * 4:nb * 4 + 4].unsqueeze(2).broadcast_to([128, 4, 2]), op=ALU.add),
                    reads=[pt, adab, consts], writes=[modc])
            for part in (1, 4):
                P.add('dve', lambda e, i=i, part=part: e.tensor_scalar_add(
                    out=modv[:, i, part * 16:(part + 1) * 16, :], in0=modv[:, i, part * 16:(part + 1) * 16, :],
                    scalar1=1.0), reads=[modc], writes=[modc])
        P.end_phase(dummy)

        xreg = [[Buf("x%d_%d" % (c, b)) for b in range(9)] for c in range(KC)]

        def xregs(c, t0, n):
            return [xreg[c][b] for b in range(t0 // 256, (t0 + n) // 256)]

        aTreg = [[Buf("a%d_%d" % (c, b)) for b in range(5)] for c in range(KC)]
        state = {"xsrc": xT_in}

        hT = H.ap.rearrange("p (k t) -> p k t", k=KC)

        def modulate(i, part_sh, part_sc, hreg):
            xsrc = state["xsrc"]
            sc = scr_carve([("x0", 512, F32), ("x1", 512, F32), ("x2", 512, F32), ("x3", 512, F32),
                            ("sq0", 512, BF16), ("sq1", 512, BF16),
                            ("rs0", 512, F32), ("rs1", 512, F32), ("t0", 512, F32), ("t1", 512, F32)])
            xs = Rot([sc["x0"], sc["x1"], sc["x2"], sc["x3"]])
            sqs = Rot([sc["sq0"], sc["sq1"]])
            rss = Rot([sc["rs0"], sc["rs1"]])
            ts = Rot([sc["t0"], sc["t1"]])
            for tb, (t0, n) in enumerate(TBS):
                s = 1 if tb == 4 else 0
                ps = PS[tb % 2]
                for c in range(KC):
                    x = xs.next()
                    P.dma('sp', x.ap[:, :n], xsrc[c, :, t0:t0 + n], x, reads=xregs(c, t0, n), writes=[x])
                    sq = sqs.next()
                    P.add('act', lambda e, x=x, sq=sq: e.activation(out=sq.ap[:, :n], in_=x.ap[:, :n], func=AF.Square),
                          reads=[x], writes=[sq])
                    P.add('pe', lambda e, ps=ps, sq=sq, c=c: e.matmul(ps.ap[:, :n], onesD.ap, sq.ap[:, :n],
                                                                     start=(c == 0), stop=(c == KC - 1)),
                          reads=[sq, onesD], writes=[ps])
                rs = rss.next()
                P.add('act', lambda e, ps=ps, rs=rs: e.activation(out=rs.ap[:, :n], in_=ps.ap[:, :n], func=AF.Sqrt, bias=epsb.ap[:, 0:1], scale=1.0), reads=[ps, epsb], writes=[rs])
                P.add('dve', lambda e, ps=ps, rs=rs: e.reciprocal(out=rs.ap[:, :n], in_=rs.ap[:, :n]), reads=[rs], writes=[rs])
                for c in range(KC):
                    x = xs.next()
                    P.dma('sp', x.ap[:, :n], xsrc[c, :, t0:t0 + n], x, reads=xregs(c, t0, n), writes=[x])
                    t = ts.next()
                    P.add('dve', lambda e, x=x, t=t, rs=rs, c=c, s=s: e.scalar_tensor_tensor(
                        out=t.ap[:, :n], in0=x.ap[:, :n], scalar=modv[:, i, part_sc * 16 + c, s:s + 1],
                        in1=rs.ap[:, :n], op0=ALU.mult, op1=ALU.mult), reads=[x, rs, modc], writes=[t])
                    P.add('act', lambda e, t=t, c=c, s=s: e.activation(
                        out=hT[:, c, t0:t0 + n], in_=t.ap[:, :n], func=AF.Identity,
                        bias=modv[:, i, part_sh * 16 + c, s:s + 1], scale=1.0),
                        reads=[t, modc], writes=[hreg[c][tb]])

        def new_hreg():
            if P.h_used:
                P.end_phase()
            P.h_used = True
            return [[P.region("h%d_%d" % (c, tb)) for tb in range(5)] for c in range(KC)]

        def wslab_cols(wsrc, c0, ncols, k_chunks=KC, dst_off=0, slab=None):
            pass

        def load_cols(slab, off, wsrc, c0, ncols, kc=KC, step=4):
            dstv = slab.ap[:, off:off + kc * ncols].rearrange("p (k n) -> p k n", k=kc)
            srcv = wsrc[:, c0:c0 + ncols].rearrange("(k p) n -> p k n", p=128)
            for q in range(0, kc, step):
                P.dma('pool', dstv[:, q:q + step, :], srcv[:, q:q + step, :], slab, writes=[slab])
            return dstv

        def even_inproj(i, j, hreg):
            w = ev_w_in[j]
            sc = scr_carve([("st0", 512, BF16), ("st1", 512, BF16), ("st2", 512, BF16),
                            ("sq", 512, F32), ("zn", 512, F32), ("ss", 8, F32), ("rstd", 8, F32),
                            ("ta", 256, F32), ("tb", 256, F32), ("zr0", 512, BF16), ("zr1", 512, BF16),
                            ("rope", 16 * 128, F32)])
            sts = Rot([sc["st0"], sc["st1"], sc["st2"]])
            zrs = Rot([sc["zr0"], sc["zr1"]])
            ropeb = sc["rope"]
            ropev = ropeb.ap.rearrange("p (t d) -> p t d", t=16)
            P.dma('sp', ropev, ropeA_d.rearrange("t p d -> p t d"), ropeb, writes=[ropeb])
            sq, zn, ss, rstd, ta, tbb = sc["sq"], sc["zn"], sc["ss"], sc["rstd"], sc["ta"], sc["tb"]
            gv = gains_s.ap.rearrange("p (j q d) -> p j q d", j=2, q=2)
            psr = Rot([PS[0], PS[1], PS[2]])
            ptr = Rot([PS[3], PS[4]])
            allh = [hreg[c][tb] for c in range(KC) for tb in range(5)]
            slab = wrot.next()
            wv = load_cols(slab, 0, w, 0, 512)
            for tb, (t0, n) in enumerate(TBS):
                for m in range(4):
                    ps = psr.next()
                    for k in range(KC):
                        P.add('pe', lambda e, ps=ps, k=k, m=m, wv=wv: e.matmul(
                            ps.ap[:, :n], wv[:, k, m * 128:(m + 1) * 128], hT[:, k, t0:t0 + n],
                            start=(k == 0), stop=(k == KC - 1)), reads=[slab, hreg[k][tb]], writes=[ps])
                    st = sts.next()
                    P.add('act', lambda e, ps=ps, st=st: e.copy(out=st.ap[:, :n], in_=ps.ap[:, :n]),
                          reads=[ps], writes=[st])
                    P.dma('sp', fT_d[m, :, t0:t0 + n], st.ap[:, :n], st, reads=[st], accum=[fTreg])
            for blk in range(1, 5):
                slab = wrot.next()
                wv = load_cols(slab, 0, w, blk * 512, 512)
                isk = (blk == 4)
                for tt in range(NT):
                    ps = psr.next()
                    tbi = min(tt // 4, 4)
                    for k in range(KC):
                        P.add('pe', lambda e, ps=ps, k=k, wv=wv, tt=tt: e.matmul(
                            ps.ap, hT[:, k, tt * 128:(tt + 1) * 128], wv[:, k, :],
                            start=(k == 0), stop=(k == KC - 1)), reads=[slab, hreg[k][tbi]], writes=[ps])
                    P.add('act', lambda e, ps=ps: e.activation(out=sq.ap, in_=ps.ap, func=AF.Square),
                          reads=[ps], writes=[sq])
                    P.add('dve', lambda e: e.reduce_sum(out=ss.ap[:, 0:4], in_=sq.ap.rearrange("p (h d) -> p h d", h=4),
                                                        axis=AX.X), reads=[sq], writes=[ss])
                    P.add('act', lambda e: e.activation(out=rstd.ap[:, 0:4], in_=ss.ap[:, 0:4], func=AF.Sqrt, bias=epsb.ap[:, 0:1], scale=1.0 / 128),
                          reads=[ss, epsb], writes=[rstd])
                    P.add('dve', lambda e: e.reciprocal(out=rstd.ap[:, 0:4], in_=rstd.ap[:, 0:4]), reads=[rstd], writes=[rstd])
                    zn3 = zn.ap.rearrange("p (h d) -> p h d", h=4)
                    P.add('dve', lambda e, ps=ps, zn3=zn3: e.tensor_tensor(
                        out=zn3, in0=ps.ap.rearrange("p (h d) -> p h d", h=4),
                        in1=rstd.ap[:, 0:4].unsqueeze(2).broadcast_to([128, 4, 128]), op=ALU.mult),
                        reads=[ps, rstd], writes=[zn])
                    zr = zrs.next()
                    zr3 = zr.ap.rearrange("p (h d) -> p h d", h=4)
                    gsel = gv[:, j, 1 if isk else 0, :].unsqueeze(1).broadcast_to([128, 4, 128])
                    if tt < 16:
                        P.add('dve', lambda e, zn3=zn3, gsel=gsel: e.tensor_tensor(out=zn3, in0=zn3, in1=gsel, op=ALU.mult),
                              reads=[zn, gains_s, consts], writes=[zn])
                        cs = ropeb
                        cosb = ropev[:, tt, 0:64].unsqueeze(1).broadcast_to([128, 4, 64])
                        sinb = ropev[:, tt, 64:128].unsqueeze(1).broadcast_to([128, 4, 64])
                        ta3 = ta.ap.rearrange("p (h d) -> p h d", h=4)
                        tb3 = tbb.ap.rearrange("p (h d) -> p h d", h=4)
                        x1 = zn3[:, :, 0:64]
                        x2 = zn3[:, :, 64:128]
                        P.add('dve', lambda e, x1=x1, cosb=cosb, ta3=ta3: e.tensor_tensor(out=ta3, in0=x1, in1=cosb, op=ALU.mult),
                              reads=[zn, cs], writes=[ta])
                        P.add('dve', lambda e, x2=x2, sinb=sinb, tb3=tb3: e.tensor_tensor(out=tb3, in0=x2, in1=sinb, op=ALU.mult),
                              reads=[zn, cs], writes=[tbb])
                        P.add('dve', lambda e, zr3=zr3, ta3=ta3, tb3=tb3: e.tensor_tensor(out=zr3[:, :, 0:64], in0=ta3, in1=tb3, op=ALU.subtract),
                              reads=[ta, tbb], writes=[zr])
                        P.add('dve', lambda e, x2=x2, cosb=cosb, ta3=ta3: e.tensor_tensor(out=ta3, in0=x2, in1=cosb, op=ALU.mult),
                              reads=[zn, cs], writes=[ta])
                        P.add('dve', lambda e, x1=x1, sinb=sinb, tb3=tb3: e.tensor_tensor(out=tb3, in0=x1, in1=sinb, op=ALU.mult),
                              reads=[zn, cs], writes=[tbb])
                        P.add('dve', lambda e, zr3=zr3, ta3=ta3, tb3=tb3: e.tensor_tensor(out=zr3[:, :, 64:128], in0=ta3, in1=tb3, op=ALU.add),
                              reads=[ta, tbb], writes=[zr])
                    else:
                        P.add('dve', lambda e, zn3=zn3, gsel=gsel, zr3=zr3: e.tensor_tensor(out=zr3, in0=zn3, in1=gsel, op=ALU.mult),
                              reads=[zn, gains_s, consts], writes=[zr])
                    pt = ptr.next()
                    ptb = pt.ap.bitcast(BF16)
                    for h in range(4):
                        P.add('pe', lambda e, ptb=ptb, zr=zr, h=h: e.transpose(ptb[:, h * 128:(h + 1) * 128],
                                                                             zr.ap[:, h * 128:(h + 1) * 128], ident_bf.ap),
                              reads=[zr, ident_bf, consts], writes=[pt])
                    st = sts.next()
                    P.add('act', lambda e, ptb=ptb, st=st: e.copy(out=st.ap, in_=ptb[:, 0:512]), reads=[pt], writes=[st])
                    dstT = kT_d if isk else qT_d
                    h0 = 0 if isk else (blk - 1) * 4
                    P.dma('sp', dstT[h0:h0 + 4, :, tt * 128:(tt + 1) * 128].rearrange("h p t -> p h t"),
                          st.ap.rearrange("p (h t) -> p h t", h=4), st, reads=[st], accum=[qkreg])
            slab = wrot.next()
            wv = load_cols(slab, 0, w, 2560, 512)
            for tt in range(NT):
                ps = psr.next()
                tbi = min(tt // 4, 4)
                for k in range(KC):
                    P.add('pe', lambda e, ps=ps, k=k, wv=wv, tt=tt: e.matmul(
                        ps.ap, hT[:, k, tt * 128:(tt + 1) * 128], wv[:, k, :],
                        start=(k == 0), stop=(k == KC - 1)), reads=[slab, hreg[k][tbi]], writes=[ps])
                st = sts.next()
                P.add('act', lambda e, ps=ps, st=st: e.copy(out=st.ap, in_=ps.ap), reads=[ps], writes=[st])
                P.dma('sp', v_d[tt, :, 0:512], st.ap, st, reads=[st], accum=[qkreg])

        fTreg = Buf("fTreg")
        qkreg = Buf("qkreg")
        aTall = Buf("aTall")

        def fourier(i):
            hc = h_carve([("f", 4 * T, BF16), ("A", 4 * NT * 256, BF16)])
            fb, Ab = hc["f"], hc["A"]
            fv = fb.ap.rearrange("p (g t) -> p g t", g=4)
            Av = Ab.ap.rearrange("p (g n c) -> p g n c", g=4, n=NT)
            sc = scr_carve([("st0", 512, BF16), ("st1", 512, BF16)])
            sts = Rot([sc["st0"], sc["st1"]])
            for g in range(4):
                P.dma('sp', fv[:, g, :], fT_d[g], fb, reads=[fTreg], writes=[fb])
            psr = Rot([PS[0], PS[1], PS[2]])
            for g in range(4):
                for n in range(NT):
                    ps = psr.next()
                    P.add('pe', lambda e, ps=ps, g=g, n=n: e.matmul(ps.ap[:, 0:256], fv[:, g, n * 128:(n + 1) * 128], dft128.ap,
                                                                   start=True, stop=True),
                          reads=[fb, dft128, consts], writes=[ps])
                    P.add('act', lambda e, ps=ps, g=g, n=n: e.copy(out=Av[:, g, n, :], in_=ps.ap[:, 0:256]),
                          reads=[ps], writes=[Ab])
            for kb in range(5):
                slab = wrot.next()
                if kb < 4:
                    nn, ncol, n0 = 16, 512, 0
                    for cs_ in range(2):
                        dst = slab.ap[:, cs_ * 8192:(cs_ + 1) * 8192].rearrange("p (n k) -> p n k", n=16)
                        src = dftL_d[cs_, :, kb * 512:(kb + 1) * 512].rearrange("(n p) k -> p n k", p=128)
                        for q in range(0, 16, 4):
                            P.dma('sp', dst[:, q:q + 4, :], src[:, q:q + 4, :], slab, writes=[slab])
                else:
                    nn, ncol, n0 = 2, 256, 16
                    for cs_ in range(2):
                        dst = slab.ap[:, cs_ * 8192:cs_ * 8192 + 512].rearrange("p (n k) -> p n k", n=2)
                        src = dftC_d[cs_].rearrange("(n p) k -> p n k", p=128)
                        P.dma('sp', dst, src, slab, writes=[slab])
                t0 = kb * 512
                for g in range(4):
                    ps = psr.next()
                    cnt = 0
                    for cs_ in range(2):
                        sv = slab.ap[:, cs_ * 8192:cs_ * 8192 + nn * ncol].rearrange("p (n k) -> p n k", n=nn)
                        for n in range(nn):
                            P.add('pe', lambda e, ps=ps, g=g, n=n, cs_=cs_, sv=sv, cnt=cnt, nn=nn, ncol=ncol, n0=n0: e.matmul(
                                ps.ap[:, :ncol], Av[:, g, n0 + n, cs_ * 128:(cs_ + 1) * 128], sv[:, n, :],
                                start=(cnt == 0), stop=(cnt == 2 * nn - 1)), reads=[Ab, slab], writes=[ps])
                            cnt += 1
                    st = sts.next()
                    P.add('act', lambda e, ps=ps, st=st, ncol=ncol: e.copy(out=st.ap[:, :ncol], in_=ps.ap[:, :ncol]),
                          reads=[ps], writes=[st])
                    P.dma('sp', aT_d[g, :, t0:t0 + ncol], st.ap[:, :ncol], st, reads=[st], writes=[aTreg[g][kb]])

        def attention(nheads, kparts_of, qparts_of, v_of, kvkey_of, scale, out_chunk0):
            hc = h_carve([("k0", 2 * T, BF16), ("k1", 2 * T, BF16), ("v0", NT * 128, BF16), ("v1", NT * 128, BF16)])
            kbufs = Rot([hc["k0"], hc["k1"]])
            vbufs = Rot([hc["v0"], hc["v1"]])
            sc = scr_carve([("q0", 1024, BF16), ("q1", 1024, BF16), ("q2", 1024, BF16),
                            ("p0", 512, BF16), ("p1", 512, BF16), ("p2", 512, BF16),
                            ("o0", 512, BF16), ("o1", 512, BF16), ("r0", 512, F32), ("r1", 512, F32)])
            qbufs = Rot([sc["q0"], sc["q1"], sc["q2"]])
            pbufs = Rot([sc["p0"], sc["p1"], sc["p2"]])
            obufs = Rot([sc["o0"], sc["o1"]])
            rbufs = Rot([sc["r0"], sc["r1"]])
            pss = Rot([PS[0], PS[1], PS[2]])
            pso = Rot([PS[3], PS[4]])
            psm = Rot([PS[5], PS[6]])
            blocks = []
            cur_kv = None
            for h in range(nheads):
                for tb, (t0, n) in enumerate(TBS):
                    blocks.append((h, tb, t0, n))
            st = {"kb": None, "vb": None, "cur": None}
            loaded = {}

            def do_loads(bi):
                h, tb, t0, n = blocks[bi]
                key = kvkey_of(h)
                if key != st["cur"]:
                    st["cur"] = key
                    kb = kbufs.next()
                    vb = vbufs.next()
                    st["kb"], st["vb"] = kb, vb
                    for pi, (src, rows, base) in enumerate(kparts_of(h)):
                        P.dma('sp', kb.ap[base:base + rows, pi * T:(pi + 1) * T], src, kb, reads=[qkreg], writes=[kb])
                    P.dma('sp', vb.ap.rearrange("p (t d) -> p t d", t=NT), v_of(h).rearrange("t p d -> p t d"), vb,
                          reads=[qkreg], writes=[vb])
                qb = qbufs.next()
                for pi, (src, rows, base) in enumerate(qparts_of(h)):
                    P.dma('sp', qb.ap[base:base + rows, pi * 512:pi * 512 + n], src[:, t0:t0 + n], qb,
                          reads=[qkreg], writes=[qb])
                loaded[bi] = (st["kb"], st["vb"], qb)

            def block_steps(bi):
                h, tb, t0, n = blocks[bi]
                kb, vb, qb = loaded[bi]
                kts = list(range(NT)) if tb < 4 else [16, 17]
                po = pso.next()
                pm = psm.next()
                return [(h, tb, t0, n, kt, ki == 0, ki == len(kts) - 1, kb, vb, qb, po, pm) for ki, kt in enumerate(kts)]
            nparts = len(kparts_of(0))
            pend = None

            def emit_qk(s):
                h, tb, t0, n, kt, first, last, kb, vb, qb, po, pm = s
                ps = pss.next()
                parts = kparts_of(h)
                for pi, (src, rows, base) in enumerate(parts):
                    P.add('pe', lambda e, ps=ps, kb=kb, qb=qb, pi=pi, rows=rows, base=base, kt=kt, n=n: e.matmul(
                        ps.ap[:, :n], kb.ap[base:base + rows, pi * T + kt * 128:pi * T + (kt + 1) * 128],
                        qb.ap[base:base + rows, pi * 512:pi * 512 + n],
                        start=(pi == 0), stop=(pi == len(parts) - 1)), reads=[kb, qb], writes=[ps])
                pb = pbufs.next()
                P.add('act', lambda e, ps=ps, pb=pb, n=n: e.activation(out=pb.ap[:, :n], in_=ps.ap[:, :n], func=AF.Exp, scale=scale),
                      reads=[ps], writes=[pb])
                return pb

            def emit_pv(s, pb):
                h, tb, t0, n, kt, first, last, kb, vb, qb, po, pm = s
                P.add('pe', lambda e, po=po, vb=vb, pb=pb, kt=kt, n=n, first=first, last=last: e.matmul(
                    po.ap[:, :n], vb.ap[:, kt * 128:(kt + 1) * 128], pb.ap[:, :n], start=first, stop=last),
                    reads=[vb, pb], writes=[po])
                P.add('pe', lambda e, pm=pm, pb=pb, n=n, first=first, last=last: e.matmul(
                    pm.ap[:, :n], ones1.ap, pb.ap[:, :n], start=first, stop=last), reads=[ones1, pb], writes=[pm])
                if last:
                    rb = rbufs.next()
                    ob = obufs.next()
                    P.add('dve', lambda e, pm=pm, rb=rb, n=n: e.reciprocal(out=rb.ap[:, :n], in_=pm.ap[:, :n]),
                          reads=[pm], writes=[rb])
                    P.add('dve', lambda e, po=po, rb=rb, ob=ob, n=n: e.tensor_tensor(out=ob.ap[:, :n], in0=po.ap[:, :n],
                                                                                    in1=rb.ap[:, :n], op=ALU.mult),
                          reads=[po, rb], writes=[ob])
                    P.dma('sp', aT_d[out_chunk0 + h, :, t0:t0 + n], ob.ap[:, :n], ob, reads=[ob],
                          writes=[aTreg[out_chunk0 + h][tb]])

            do_loads(0)
            for bi in range(len(blocks)):
                if bi + 1 < len(blocks):
                    do_loads(bi + 1)
                for s in block_steps(bi):
                    pb = emit_qk(s)
                    if pend is not None:
                        emit_pv(*pend)
                    pend = (s, pb)
            emit_pv(*pend)

        def outproj(i, w, gpart):
            xsrc = state["xsrc"]
            hc = h_carve([("a0", KC * 512, BF16), ("a1", KC * 512, BF16)])
            abufs = Rot([hc["a0"], hc["a1"]])
            sc = scr_carve([("x0", 512, F32), ("x1", 512, F32), ("x2", 512, F32),
                            ("o0", 512, F32), ("o1", 512, F32), ("o2", 512, F32)])
            xs = Rot([sc["x0"], sc["x1"], sc["x2"]])
            os_ = Rot([sc["o0"], sc["o1"], sc["o2"]])
            psr = Rot([PS[0], PS[1], PS[2], PS[3]])
            op_loaded = {}

            def op_load(bi):
                tb_ = bi % 5
                t0_, n_ = TBS[tb_]
                ab_ = abufs.next()
                abv_ = ab_.ap.rearrange("p (k t) -> p k t", k=KC)
                P.dma('sp', abv_[:, :, :n_], aT_d[:, :, t0_:t0_ + n_].rearrange("k p t -> p k t"), ab_,
                      reads=[aTreg[c][tb_] for c in range(KC)], writes=[ab_])
                op_loaded[bi] = ab_

            for mb in range(4):
                slab = wrot.next()
                wv = load_cols(slab, 0, w, mb * 512, 512)
                for tb, (t0, n) in enumerate(TBS):
                    s = 1 if tb == 4 else 0
                    if mb == 0 and tb == 0:
                        op_load(0)
                    ab = op_loaded.pop(mb * 5 + tb)
                    abv = ab.ap.rearrange("p (k t) -> p k t", k=KC)
                    if mb * 5 + tb + 1 < 20:
                        op_load(mb * 5 + tb + 1)
                    for m4 in range(4):
                        m = mb * 4 + m4
                        ps = psr.next()
                        for k in range(KC):
                            P.add('pe', lambda e, ps=ps, k=k, m4=m4, wv=wv, abv=abv, n=n: e.matmul(
                                ps.ap[:, :n], wv[:, k, m4 * 128:(m4 + 1) * 128], abv[:, k, :n],
                                start=(k == 0), stop=(k == KC - 1)), reads=[slab, ab], writes=[ps])
                        x = xs.next()
                        P.dma('sp', x.ap[:, :n], xsrc[m, :, t0:t0 + n], x, reads=xregs(m, t0, n), writes=[x])
                        o = os_.next()
                        P.add('dve', lambda e, ps=ps, x=x, o=o, m=m, s=s, n=n: e.scalar_tensor_tensor(
                            out=o.ap[:, :n], in0=ps.ap[:, :n], scalar=modv[:, i, gpart * 16 + m, s:s + 1],
                            in1=x.ap[:, :n], op0=ALU.mult, op1=ALU.add), reads=[ps, x, modc], writes=[o])
                        P.dma('sp', xT_d[m, :, t0:t0 + n], o.ap[:, :n], o, reads=[o], writes=xregs(m, t0, n))
            state["xsrc"] = xT_d

        GW = 2308

        def ffn_up(i, hreg):
            w = ffn_w_up[i]
            sc = scr_carve([("g0", GW, BF16), ("g1", GW, BF16), ("sg0", T, BF16), ("sg1", T, BF16),
                            ("acc0", 512, F32), ("acc1", 512, F32), ("a0", 512, BF16), ("a1", 512, BF16), ("a2", 512, BF16)])
            gbufs = Rot([sc["g0"], sc["g1"]])
            sgbufs = Rot([sc["sg0"], sc["sg1"]])
            accs = Rot([sc["acc0"], sc["acc1"]])
            asts = Rot([sc["a0"], sc["a1"], sc["a2"]])
            for gb in (sc["g0"], sc["g1"]):
                P.add('dve', lambda e, gb=gb: e.memset(gb.ap, 0.0), writes=[gb])
            cwv = ffcw_s.ap.rearrange("p (i j k) -> p i j k", i=DEPTH, j=NJ)
            cbv = ffcb_s.ap.rearrange("p (i j) -> p i j", i=DEPTH)
            psg = Rot([PS[0], PS[1], PS[2]])
            psu = Rot([PS[3], PS[4], PS[5]])

            def goff(tb, t0):
                return 1 + t0 if tb < 4 else 2051

            slabs = {}

            def get_slab(jg):
                if jg not in slabs:
                    nj = min(4, NJ - jg * 4)
                    slab = wrot.next()
                    gv_ = load_cols(slab, 0, w, jg * 512, nj * 128)
                    uv_ = load_cols(slab, 8192, w, DFF + jg * 512, nj * 128)
                    slabs[jg] = (slab, gv_, uv_)
                return slabs[jg]

            def emit_g(j):
                slab, gv_, uv_ = get_slab(j // 4)
                jj = j % 4
                gb = gbufs.next()
                sg = sgbufs.next()
                for tb, (t0, n) in enumerate(TBS):
                    ps = psg.next()
                    for k in range(KC):
                        P.add('pe', lambda e, ps=ps, k=k, jj=jj, gv_=gv_, t0=t0, n=n: e.matmul(
                            ps.ap[:, :n], gv_[:, k, jj * 128:(jj + 1) * 128], hT[:, k, t0:t0 + n],
                            start=(k == 0), stop=(k == KC - 1)), reads=[slab, hreg[k][tb]], writes=[ps])
                    off = goff(tb, t0)
                    P.add('dve', lambda e, ps=ps, gb=gb, off=off, n=n: e.tensor_copy(out=gb.ap[:, off:off + n], in_=ps.ap[:, :n]),
                          reads=[ps], writes=[gb])
                for tb, (t0, n) in enumerate(TBS):
                    off = goff(tb, t0)
                    acc = accs.next()
                    P.add('dve', lambda e, gb=gb, acc=acc, off=off, n=n, j=j: e.tensor_scalar(
                        out=acc.ap[:, :n], in0=gb.ap[:, off - 1:off - 1 + n], scalar1=cwv[:, i, j, 0:1], scalar2=None,
                        op0=ALU.mult), reads=[gb, ffcw_s, consts], writes=[acc])
                    for kk in (1, 2):
                        P.add('dve', lambda e, gb=gb, acc=acc, off=off, n=n, j=j, kk=kk: e.scalar_tensor_tensor(
                            out=acc.ap[:, :n], in0=gb.ap[:, off - 1 + kk:off - 1 + kk + n], scalar=cwv[:, i, j, kk:kk + 1],
                            in1=acc.ap[:, :n], op0=ALU.mult, op1=ALU.add), reads=[gb, acc, ffcw_s], writes=[acc])
                    P.add('act', lambda e, acc=acc, sg=sg, t0=t0, n=n, j=j: e.activation(
                        out=sg.ap[:, t0:t0 + n], in_=acc.ap[:, :n], func=AF.Silu, bias=cbv[:, i, j:j + 1], scale=1.0),
                        reads=[acc, ffcb_s, consts], writes=[sg])
                return sg

            def emit_u(j, sg):
                slab, gv_, uv_ = get_slab(j // 4)
                jj = j % 4
                for tb, (t0, n) in enumerate(TBS):
                    ps = psu.next()
                    for k in range(KC):
                        P.add('pe', lambda e, ps=ps, k=k, jj=jj, uv_=uv_, t0=t0, n=n: e.matmul(
                            ps.ap[:, :n], uv_[:, k, jj * 128:(jj + 1) * 128], hT[:, k, t0:t0 + n],
                            start=(k == 0), stop=(k == KC - 1)), reads=[slab, hreg[k][tb]], writes=[ps])
                    a = asts.next()
                    P.add('dve', lambda e, ps=ps, sg=sg, a=a, t0=t0, n=n: e.tensor_tensor(
                        out=a.ap[:, :n], in0=ps.ap[:, :n], in1=sg.ap[:, t0:t0 + n], op=ALU.mult),
                        reads=[ps, sg], writes=[a])
                    P.dma('sp', ffa_d[j, :, t0:t0 + n], a.ap[:, :n], a, reads=[a], accum=[ffareg])

            pend = None
            for j in range(NJ):
                sg = emit_g(j)
                if pend is not None:
                    emit_u(*pend)
                pend = (j, sg)
            emit_u(*pend)

        ffareg = Buf("ffareg")

        def ffn_down(i, gpart):
            w = ffn_w_down[i]
            xsrc = state["xsrc"]
            JS = 29
            hc = h_carve([("a0", NJ * 512, BF16), ("a1h", JS * 512, BF16)])
            sc = scr_carve([("a1s", (NJ - JS) * 512, BF16),
                            ("x0", 512, F32), ("x1", 512, F32), ("x2", 512, F32),
                            ("o0", 512, F32), ("o1", 512, F32), ("o2", 512, F32)])
            a0v = hc["a0"].ap.rearrange("p (j t) -> p j t", j=NJ)
            a1hv = hc["a1h"].ap.rearrange("p (j t) -> p j t", j=JS)
            a1sv = sc["a1s"].ap.rearrange("p (j t) -> p j t", j=NJ - JS)
            abufs = Rot([[(hc["a0"], a0v, 0, NJ)], [(hc["a1h"], a1hv, 0, JS), (sc["a1s"], a1sv, JS, NJ)]])
            xs = Rot([sc["x0"], sc["x1"], sc["x2"]])
            os_ = Rot([sc["o0"], sc["o1"], sc["o2"]])
            psr = Rot([PS[0], PS[1], PS[2], PS[3]])
            fd_loaded = {}

            def fd_load(bi):
                t0_, n_ = TBS[bi % 5]
                parts_ = abufs.next()
                for (buf, view, j0, j1) in parts_:
                    for q in range(j0, j1, 11):
                        q1 = min(q + 11, j1)
                        P.dma('sp', view[:, q - j0:q1 - j0, :n_], ffa_d[q:q1, :, t0_:t0_ + n_].rearrange("j p t -> p j t"), buf,
                              reads=[ffareg], writes=[buf])
                fd_loaded[bi] = parts_

            for mb in range(4):
                slab = wrot.next()
                wv = slab.ap[:, 0:NJ * 512].rearrange("p (j n) -> p j n", j=NJ)
                srcv = w[:, mb * 512:(mb + 1) * 512].rearrange("(j p) n -> p j n", p=128)
                for q in range(0, NJ, 4):
                    q1 = min(q + 4, NJ)
                    P.dma('pool', wv[:, q:q1, :], srcv[:, q:q1, :], slab, writes=[slab])
                for tb, (t0, n) in enumerate(TBS):
                    s = 1 if tb == 4 else 0
                    if mb == 0 and tb == 0:
                        fd_load(0)
                    parts = fd_loaded.pop(mb * 5 + tb)
                    if mb * 5 + tb + 1 < 20:
                        fd_load(mb * 5 + tb + 1)
                    for m4 in range(4):
                        m = mb * 4 + m4
                        ps = psr.next()
                        for (buf, view, j0, j1) in parts:
                            for jx in range(j0, j1):
                                P.add('pe', lambda e, ps=ps, jx=jx, j0=j0, m4=m4, wv=wv, view=view, n=n: e.matmul(
                                    ps.ap[:, :n], wv[:, jx, m4 * 128:(m4 + 1) * 128], view[:, jx - j0, :n],
                                    start=(jx == 0), stop=(jx == NJ - 1)), reads=[slab, buf], writes=[ps])
                        x = xs.next()
                        P.dma('sp', x.ap[:, :n], xsrc[m, :, t0:t0 + n], x, reads=xregs(m, t0, n), writes=[x])
                        o = os_.next()
                        P.add('dve', lambda e, ps=ps, x=x, o=o, m=m, s=s, n=n: e.scalar_tensor_tensor(
                            out=o.ap[:, :n], in0=ps.ap[:, :n], scalar=modv[:, i, gpart * 16 + m, s:s + 1],
                            in1=x.ap[:, :n], op0=ALU.mult, op1=ALU.add), reads=[ps, x, modc], writes=[o])
                        P.dma('sp', xT_d[m, :, t0:t0 + n], o.ap[:, :n], o, reads=[o], writes=xregs(m, t0, n))

        def odd_inproj(i, j, hreg):
            w = od_w_in[j]
            ov = odvec_s.ap.rearrange("p (j v c) -> p j v c", j=2, v=5)
            sc = scr_carve([("stg", 4 * 512, F32), ("sq", 4 * 512, BF16), ("rs", 512, F32),
                            ("o0", 512, BF16), ("o1", 512, BF16), ("o2", 512, BF16),
                            ("cs", 512, F32), ("sn", 512, F32), ("t1", 512, F32), ("t2", 512, F32)])
            stg, sqb, rs = sc["stg"], sc["sq"], sc["rs"]
            outs = Rot([sc["o0"], sc["o1"], sc["o2"]])
            sgm, csb, snb, t1, t2 = sc["t1"], sc["cs"], sc["sn"], sc["t1"], sc["t2"]
            stgv = stg.ap.rearrange("p (c t) -> p c t", c=4)
            sqv = sqb.ap.rearrange("p (c t) -> p c t", c=4)
            psr = Rot([PS[0], PS[1], PS[2], PS[3], PS[4], PS[5]])
            slab = wrot.next()
            av_ = load_cols(slab, 0, w, 0, 512)
            gv_ = load_cols(slab, 8192, w, 512, 512)
            for tb, (t0, n) in enumerate(TBS):
                for c in range(4):
                    pa = psr.next()
                    pg = psr.next()
                    for (pp, vv) in ((pa, av_), (pg, gv_)):
                        for k in range(KC):
                            P.add('pe', lambda e, pp=pp, vv=vv, k=k, c=c, t0=t0, n=n: e.matmul(
                                pp.ap[:, :n], vv[:, k, c * 128:(c + 1) * 128], hT[:, k, t0:t0 + n],
                                start=(k == 0), stop=(k == KC - 1)), reads=[slab, hreg[k][tb]], writes=[pp])
                    P.add('act', lambda e, pg=pg, n=n: e.activation(out=sgm.ap[:, :n], in_=pg.ap[:, :n], func=AF.Sigmoid),
                          reads=[pg], writes=[sgm])
                    o = outs.next()
                    P.add('dve', lambda e, pa=pa, o=o, n=n: e.tensor_tensor(out=o.ap[:, :n], in0=pa.ap[:, :n], in1=sgm.ap[:, :n], op=ALU.mult),
                          reads=[pa, sgm], writes=[o])
                    P.dma('sp', gl_d[c, :, t0:t0 + n], o.ap[:, :n], o, reads=[o], accum=[glreg])
            if stop == 'ip_a':
                return
            for which in range(2):
                if stop == 'ip_b1' and which == 1:
                    return
                slab = wrot.next()
                wv = load_cols(slab, 0, w, 1024 + which * 512, 512)
                for tb, (t0, n) in enumerate(TBS):
                    pss_ = psr.next()
                    for c in range(4):
                        ps = psr.next()
                        for k in range(KC):
                            P.add('pe', lambda e, ps=ps, wv=wv, k=k, c=c, t0=t0, n=n: e.matmul(
                                ps.ap[:, :n], wv[:, k, c * 128:(c + 1) * 128], hT[:, k, t0:t0 + n],
                                start=(k == 0), stop=(k == KC - 1)), reads=[slab, hreg[k][tb]], writes=[ps])
                        P.add('dve', lambda e, ps=ps, c=c, n=n: e.tensor_copy(out=stgv[:, c, :n], in_=ps.ap[:, :n]),
                              reads=[ps], writes=[stg])
                        P.add('act', lambda e, c=c, n=n: e.activation(out=sqv[:, c, :n], in_=stgv[:, c, :n], func=AF.Square),
                              reads=[stg], writes=[sqb])
                    if stop == 'ip_b2':
                        continue
                    for c in range(4):
                        P.add('pe', lambda e, pss_=pss_, c=c, n=n: e.matmul(pss_.ap[:, :n], ones512.ap, sqv[:, c, :n],
                                                                           start=(c == 0), stop=(c == 3)),
                              reads=[sqb, ones512], writes=[pss_])
                    P.add('act', lambda e, pss_=pss_, n=n: e.activation(out=rs.ap[:, :n], in_=pss_.ap[:, :n], func=AF.Sqrt, bias=epsb.ap[:, 0:1], scale=1.0), reads=[pss_, epsb], writes=[rs])
                    P.add('dve', lambda e, pss_=pss_, n=n: e.reciprocal(out=rs.ap[:, :n], in_=rs.ap[:, :n]), reads=[rs], writes=[rs])
                    for c in range(4):
                        o = outs.next()
                        P.add('dve', lambda e, o=o, c=c, n=n, which=which: e.scalar_tensor_tensor(
                            out=o.ap[:, :n], in0=stgv[:, c, :n], scalar=ov[:, j, 3 + which, c:c + 1], in1=rs.ap[:, :n],
                            op0=ALU.mult, op1=ALU.mult), reads=[stg, rs, odvec_s, consts], writes=[o])
                        P.dma('sp', cqn_d[which * 4 + c, :, t0:t0 + n], o.ap[:, :n], o, reads=[o], accum=[cqnreg])
            if stop in ('ip_b', 'ip_b2'):
                return
            slab = wrot.next()
            krv = slab.ap[:, 0:KC * 256].rearrange("p (k n) -> p k n", k=KC)
            wsrc = w.rearrange("(k p) n -> p k n", p=128)
            for (d0, s0, nn_) in ((0, 2048, 64), (64, 2048, 64), (128, 2080, 32), (160, 2048, 32), (192, 2080, 32), (224, 2048, 32)):
                for q in range(0, KC, 4):
                    P.dma('pool', krv[:, q:q + 4, d0:d0 + nn_], wsrc[:, q:q + 4, s0:s0 + nn_], slab, writes=[slab])
            for tb, (t0, n) in enumerate(TBS):
                pk = psr.next()
                pr = psr.next()
                for (pp, c0) in ((pk, 0), (pr, 128)):
                    for k in range(KC):
                        P.add('pe', lambda e, pp=pp, c0=c0, k=k, t0=t0, n=n: e.matmul(
                            pp.ap[:, :n], krv[:, k, c0:c0 + 128], hT[:, k, t0:t0 + n],
                            start=(k == 0), stop=(k == KC - 1)), reads=[slab, hreg[k][tb]], writes=[pp])
                P.dma('sp', csb.ap[:, :n], ropeM_d[0, :, t0:t0 + n], csb, writes=[csb])
                P.dma('sp', snb.ap[:, :n], ropeM_d[1, :, t0:t0 + n], snb, writes=[snb])
                P.add('dve', lambda e, pk=pk, n=n: e.tensor_tensor(out=t1.ap[:, :n], in0=pk.ap[:, :n], in1=csb.ap[:, :n], op=ALU.mult),
                      reads=[pk, csb], writes=[t1])
                P.add('dve', lambda e, pr=pr, n=n: e.tensor_tensor(out=t2.ap[:, :n], in0=pr.ap[:, :n], in1=snb.ap[:, :n], op=ALU.mult),
                      reads=[pr, snb], writes=[t2])
                o = outs.next()
                P.add('dve', lambda e, o=o, n=n: e.tensor_tensor(out=o.ap[:, :n], in0=t1.ap[:, :n], in1=t2.ap[:, :n], op=ALU.add),
                      reads=[t1, t2], writes=[o])
                P.dma('sp', kr_d[:, t0:t0 + n], o.ap[:, :n], o, reads=[o], accum=[qkreg])

        glreg = Buf("glreg")
        cqnreg = Buf("cqnreg")

        def odd_upproj(i, j):
            hc = h_carve([("cq", 4 * T, BF16), ("ckv", 4 * T, BF16)])
            cqb, ckvb = hc["cq"], hc["ckv"]
            cqv = cqb.ap.rearrange("p (c t) -> p c t", c=4)
            ckvv = ckvb.ap.rearrange("p (c t) -> p c t", c=4)
            for c in range(4):
                P.dma('sp', cqv[:, c, :], cqn_d[c], cqb, reads=[cqnreg], writes=[cqb])
                P.dma('sp', ckvv[:, c, :], cqn_d[4 + c], ckvb, reads=[cqnreg], writes=[ckvb])
            sc = scr_carve([("o0", 512, BF16), ("o1", 512, BF16), ("o2", 512, BF16), ("o3", 512, BF16),
                            ("cs", 512, F32), ("sn", 512, F32), ("t1", 512, F32), ("t2", 512, F32)])
            outs = Rot([sc["o0"], sc["o1"], sc["o2"], sc["o3"]])
            csb, snb, t1, t2 = sc["cs"], sc["sn"], sc["t1"], sc["t2"]
            psr = Rot([PS[0], PS[1], PS[2], PS[3], PS[4], PS[5]])
            slab = wrot.next()
            wq = od_w_uq[j].rearrange("(k p) (h d) -> p k h d", p=128, d=192)
            qn = slab.ap[:, 0:4 * 1536].rearrange("p (k h d) -> p k h d", k=4, h=12)
            qr = slab.ap[:, 6144:6144 + 4 * 768].rearrange("p (k h d) -> p k h d", k=4, h=12)
            qx = slab.ap[:, 9216:9216 + 4 * 768].rearrange("p (k h d) -> p k h d", k=4, h=12)
            for k in range(4):
                for h0 in range(0, 12, 4):
                    P.dma('pool', qn[:, k, h0:h0 + 4, :], wq[:, k, h0:h0 + 4, 0:128], slab, writes=[slab])
                    P.dma('pool', qr[:, k, h0:h0 + 4, :], wq[:, k, h0:h0 + 4, 128:192], slab, writes=[slab])
                    P.dma('pool', qx[:, k, h0:h0 + 4, 0:32], wq[:, k, h0:h0 + 4, 160:192], slab, writes=[slab])
                    P.dma('pool', qx[:, k, h0:h0 + 4, 32:64], wq[:, k, h0:h0 + 4, 128:160], slab, writes=[slab])
            qnf = slab.ap[:, 0:6144].rearrange("p (k n) -> p k n", k=4)
            qrf = slab.ap[:, 6144:9216].rearrange("p (k n) -> p k n", k=4)
            qxf = slab.ap[:, 9216:12288].rearrange("p (k n) -> p k n", k=4)
            for tb, (t0, n) in enumerate(TBS):
                for h in range(12):
                    ps = psr.next()
                    for k in range(4):
                        P.add('pe', lambda e, ps=ps, k=k, h=h, t0=t0, n=n: e.matmul(
                            ps.ap[:, :n], qnf[:, k, h * 128:(h + 1) * 128], cqv[:, k, t0:t0 + n],
                            start=(k == 0), stop=(k == 3)), reads=[slab, cqb], writes=[ps])
                    o = outs.next()
                    P.add('act', lambda e, ps=ps, o=o, n=n: e.copy(out=o.ap[:, :n], in_=ps.ap[:, :n]), reads=[ps], writes=[o])
                    P.dma('sp', qT_d[h, :, t0:t0 + n], o.ap[:, :n], o, reads=[o], accum=[qkreg])
                P.dma('sp', csb.ap[:, :n], ropeM_d[0, :, t0:t0 + n], csb, writes=[csb])
                P.dma('sp', snb.ap[:, :n], ropeM_d[1, :, t0:t0 + n], snb, writes=[snb])
                for hp in range(6):
                    pk = psr.next()
                    pr = psr.next()
                    for (pp, vv) in ((pk, qrf), (pr, qxf)):
                        for k in range(4):
                            P.add('pe', lambda e, pp=pp, vv=vv, k=k, hp=hp, t0=t0, n=n: e.matmul(
                                pp.ap[:, :n], vv[:, k, hp * 128:(hp + 1) * 128], cqv[:, k, t0:t0 + n],
                                start=(k == 0), stop=(k == 3)), reads=[slab, cqb], writes=[pp])
                    P.add('dve', lambda e, pk=pk, n=n: e.tensor_tensor(out=t1.ap[:, :n], in0=pk.ap[:, :n], in1=csb.ap[:, :n], op=ALU.mult),
                          reads=[pk, csb], writes=[t1])
                    P.add('dve', lambda e, pr=pr, n=n: e.tensor_tensor(out=t2.ap[:, :n], in0=pr.ap[:, :n], in1=snb.ap[:, :n], op=ALU.mult),
                          reads=[pr, snb], writes=[t2])
                    o = outs.next()
                    P.add('dve', lambda e, o=o, n=n: e.tensor_tensor(out=o.ap[:, :n], in0=t1.ap[:, :n], in1=t2.ap[:, :n], op=ALU.add),
                          reads=[t1, t2], writes=[o])
                    P.dma('sp', qr_d[hp, :, t0:t0 + n], o.ap[:, :n], o, reads=[o], accum=[qkreg])
            slab = wrot.next()
            wkv = od_w_ukv[j].rearrange("(k p) (h d) -> p k h d", p=128, d=256)
            kn = slab.ap[:, 0:6144].rearrange("p (k h d) -> p k h d", k=4, h=12)
            vv_ = slab.ap[:, 6144:12288].rearrange("p (k h d) -> p k h d", k=4, h=12)
            for k in range(4):
                for h0 in range(0, 12, 4):
                    P.dma('pool', kn[:, k, h0:h0 + 4, :], wkv[:, k, h0:h0 + 4, 0:128], slab, writes=[slab])
                    P.dma('pool', vv_[:, k, h0:h0 + 4, :], wkv[:, k, h0:h0 + 4, 128:256], slab, writes=[slab])
            knf = slab.ap[:, 0:6144].rearrange("p (k n) -> p k n", k=4)
            vf = slab.ap[:, 6144:12288].rearrange("p (k n) -> p k n", k=4)
            for tb, (t0, n) in enumerate(TBS):
                for h in range(12):
                    ps = psr.next()
                    for k in range(4):
                        P.add('pe', lambda e, ps=ps, k=k, h=h, t0=t0, n=n: e.matmul(
                            ps.ap[:, :n], knf[:, k, h * 128:(h + 1) * 128], ckvv[:, k, t0:t0 + n],
                            start=(k == 0), stop=(k == 3)), reads=[slab, ckvb], writes=[ps])
                    o = outs.next()
                    P.add('act', lambda e, ps=ps, o=o, n=n: e.copy(out=o.ap[:, :n], in_=ps.ap[:, :n]), reads=[ps], writes=[o])
                    P.dma('sp', kT_d[h, :, t0:t0 + n], o.ap[:, :n], o, reads=[o], accum=[qkreg])
            for tt in range(NT):
                for vg in range(3):
                    ps = psr.next()
                    for k in range(4):
                        P.add('pe', lambda e, ps=ps, k=k, vg=vg, tt=tt: e.matmul(
                            ps.ap, ckvv[:, k, tt * 128:(tt + 1) * 128], vf[:, k, vg * 512:(vg + 1) * 512],
                            start=(k == 0), stop=(k == 3)), reads=[slab, ckvb], writes=[ps])
                    o = outs.next()
                    P.add('act', lambda e, ps=ps, o=o: e.copy(out=o.ap, in_=ps.ap), reads=[ps], writes=[o])
                    P.dma('sp', v_d[tt, :, vg * 512:(vg + 1) * 512], o.ap, o, reads=[o], accum=[qkreg])

        GP = 15
        LW = L + 2 * GP
        CW = CL + 2 * GP

        def odd_conv(i, j):
            hc = h_carve([("gl", 4 * (LW + CW), BF16), ("u", 4 * T, F32)])
            glb, ub = hc["gl"], hc["u"]
            glv = glb.ap.rearrange("p (c t) -> p c t", c=4)
            uv = ub.ap.rearrange("p (c t) -> p c t", c=4)
            sc = scr_carve([("acc0", 512, F32), ("acc1", 512, F32), ("usq", 4 * 512, F32), ("mu", 512, F32), ("var", 512, F32),
                            ("t", 512, F32), ("o0", 512, BF16), ("o1", 512, BF16)])
            usq, mu, var, tt_ = sc["usq"], sc["mu"], sc["var"], sc["t"]
            outs = Rot([sc["o0"], sc["o1"]])
            usqv = usq.ap.rearrange("p (c t) -> p c t", c=4)
            ov = odvec_s.ap.rearrange("p (j v c) -> p j v c", j=2, v=5)
            cwv = odcw_s.ap.rearrange("p (j c k) -> p j c k", j=2, c=4)
            P.add('dve', lambda e: e.memset(glb.ap, 0.0), writes=[glb])
            for c in range(4):
                P.dma('sp', glv[:, c, GP:GP + L], gl_d[c, :, 0:L], glb, reads=[glreg], writes=[glb])
                P.dma('sp', glv[:, c, LW + GP:LW + GP + CL], gl_d[c, :, L:T], glb, reads=[glreg], writes=[glb])
            accs = Rot([sc["acc0"], sc["acc1"]])
            for c in range(4):
                for tb, (t0, n) in enumerate(TBS):
                    base = t0 if tb < 4 else LW
                    acc = accs.next()
                    P.add('dve', lambda e, acc=acc, c=c, base=base, n=n: e.tensor_scalar(
                        out=acc.ap[:, :n], in0=glv[:, c, base:base + n], scalar1=cwv[:, j, c, 0:1], scalar2=None, op0=ALU.mult),
                        reads=[glb, odcw_s, consts], writes=[acc])
                    for k in range(1, 31):
                        P.add('dve', lambda e, acc=acc, c=c, k=k, base=base, n=n: e.scalar_tensor_tensor(
                            out=acc.ap[:, :n], in0=glv[:, c, base + k:base + k + n], scalar=cwv[:, j, c, k:k + 1],
                            in1=acc.ap[:, :n], op0=ALU.mult, op1=ALU.add), reads=[glb, acc, odcw_s], writes=[acc])
                    P.add('act', lambda e, acc=acc, c=c, t0=t0, n=n: e.activation(
                        out=uv[:, c, t0:t0 + n], in_=acc.ap[:, :n], func=AF.Identity, bias=ov[:, j, 0, c:c + 1], scale=1.0),
                        reads=[acc, odvec_s, consts], writes=[ub])
            pm = Rot([PS[3], PS[4]])
            pv = Rot([PS[5], PS[6]])
            for tb, (t0, n) in enumerate(TBS):
                p1 = pm.next()
                p2 = pv.next()
                for c in range(4):
                    P.add('act', lambda e, c=c, t0=t0, n=n: e.activation(out=usqv[:, c, :n], in_=uv[:, c, t0:t0 + n], func=AF.Square),
                          reads=[ub], writes=[usq])
                for c in range(4):
                    P.add('pe', lambda e, p1=p1, c=c, t0=t0, n=n: e.matmul(p1.ap[:, :n], ones512f.ap, uv[:, c, t0:t0 + n],
                                                                          start=(c == 0), stop=(c == 3)),
                          reads=[ub, ones512f], writes=[p1])
                for c in range(4):
                    P.add('pe', lambda e, p2=p2, c=c, n=n: e.matmul(p2.ap[:, :n], ones512f.ap, usqv[:, c, :n],
                                                                   start=(c == 0), stop=(c == 3)),
                          reads=[usq, ones512f], writes=[p2])
                P.add('dve', lambda e, p1=p1, n=n: e.tensor_copy(out=mu.ap[:, :n], in_=p1.ap[:, :n]), reads=[p1], writes=[mu])
                P.add('dve', lambda e, n=n: e.tensor_tensor(out=var.ap[:, :n], in0=mu.ap[:, :n], in1=mu.ap[:, :n], op=ALU.mult),
                      reads=[mu], writes=[var])
                P.add('dve', lambda e, p2=p2, n=n: e.tensor_tensor(out=var.ap[:, :n], in0=p2.ap[:, :n], in1=var.ap[:, :n], op=ALU.subtract),
                      reads=[p2, var], writes=[var])
                P.add('act', lambda e, n=n: e.activation(out=var.ap[:, :n], in_=var.ap[:, :n], func=AF.Sqrt, bias=epsb.ap[:, 0:1], scale=1.0), reads=[var, epsb], writes=[var])
                P.add('dve', lambda e, n=n: e.reciprocal(out=var.ap[:, :n], in_=var.ap[:, :n]), reads=[var], writes=[var])
                for c in range(4):
                    P.add('dve', lambda e, c=c, t0=t0, n=n: e.tensor_tensor(out=tt_.ap[:, :n], in0=uv[:, c, t0:t0 + n], in1=mu.ap[:, :n], op=ALU.subtract),
                          reads=[ub, mu], writes=[tt_])
                    P.add('dve', lambda e, n=n: e.tensor_tensor(out=tt_.ap[:, :n], in0=tt_.ap[:, :n], in1=var.ap[:, :n], op=ALU.mult),
                          reads=[tt_, var], writes=[tt_])
                    o = outs.next()
                    P.add('act', lambda e, o=o, c=c, n=n: e.activation(out=o.ap[:, :n], in_=tt_.ap[:, :n], func=AF.Silu,
                                                                      bias=ov[:, j, 2, c:c + 1], scale=ov[:, j, 1, c:c + 1]),
                          reads=[tt_, odvec_s, consts], writes=[o])
                    P.dma('sp', aT_d[c, :, t0:t0 + n], o.ap[:, :n], o, reads=[o], writes=[aTreg[c][tb]])

        def final_norm():
            xsrc = state["xsrc"]
            sc = scr_carve([("x0", 512, F32), ("x1", 512, F32), ("x2", 512, F32), ("x3", 512, F32),
                            ("sq0", 512, BF16), ("sq1", 512, BF16), ("rs0", 512, F32), ("rs1", 512, F32),
                            ("t0", 512, F32), ("t1", 512, F32), ("t2", 512, F32)])
            xs = Rot([sc["x0"], sc["x1"], sc["x2"], sc["x3"]])
            sqs = Rot([sc["sq0"], sc["sq1"]])
            rss = Rot([sc["rs0"], sc["rs1"]])
            ts = Rot([sc["t0"], sc["t1"], sc["t2"]])
            for tb, (t0, n) in enumerate(TBS[:4]):
                ps = PS[tb % 2]
                for c in range(KC):
                    x = xs.next()
                    P.dma('sp', x.ap[:, :n], xsrc[c, :, t0:t0 + n], x, reads=xregs(c, t0, n), writes=[x])
                    sq = sqs.next()
                    P.add('act', lambda e, x=x, sq=sq, n=n: e.activation(out=sq.ap[:, :n], in_=x.ap[:, :n], func=AF.Square),
                          reads=[x], writes=[sq])
                    P.add('pe', lambda e, ps=ps, sq=sq, c=c, n=n: e.matmul(ps.ap[:, :n], onesD.ap, sq.ap[:, :n],
                                                                          start=(c == 0), stop=(c == KC - 1)),
                          reads=[sq, onesD], writes=[ps])
                rs = rss.next()
                P.add('act', lambda e, ps=ps, rs=rs, n=n: e.activation(out=rs.ap[:, :n], in_=ps.ap[:, :n], func=AF.Sqrt, bias=epsb.ap[:, 0:1], scale=1.0), reads=[ps, epsb], writes=[rs])
                P.add('dve', lambda e, ps=ps, rs=rs, n=n: e.reciprocal(out=rs.ap[:, :n], in_=rs.ap[:, :n]), reads=[rs], writes=[rs])
                for c in range(KC):
                    x = xs.next()
                    P.dma('sp', x.ap[:, :n], xsrc[c, :, t0:t0 + n], x, reads=xregs(c, t0, n), writes=[x])
                    t = ts.next()
                    P.add('dve', lambda e, x=x, t=t, rs=rs, c=c, n=n: e.scalar_tensor_tensor(
                        out=t.ap[:, :n], in0=x.ap[:, :n], scalar=fing_s.ap[:, c:c + 1], in1=rs.ap[:, :n],
                        op0=ALU.mult, op1=ALU.mult), reads=[x, rs, fing_s, consts], writes=[t])
                    P.dma('sp', out_d[c, :, t0:t0 + n], t.ap[:, :n], t, reads=[t], accum=[outreg])

        def copy_out():
            xsrc = state["xsrc"]
            sc = scr_carve([("x0", 512, F32), ("x1", 512, F32), ("x2", 512, F32)])
            xs = Rot([sc["x0"], sc["x1"], sc["x2"]])
            for tb, (t0, n) in enumerate(TBS[:4]):
                for c in range(KC):
                    x = xs.next()
                    P.dma('sp', x.ap[:, :n], xsrc[c, :, t0:t0 + n], x, reads=xregs(c, t0, n), writes=[x])
                    P.dma('sp', out_d[c, :, t0:t0 + n], x.ap[:, :n], x, reads=[x], accum=[outreg])

        outreg = Buf("outreg")

        for i in range(NL):
            j = i // 2
            hreg = new_hreg()
            modulate(i, 0, 1, hreg)
            if i % 2 == 0:
                even_inproj(i, j, hreg)
                P.end_phase(dummy)
                fourier(i)
                P.end_phase(dummy)
                attention(12,
                          lambda h: [(kT_d[h // 3], 128, 0)],
                          lambda h: [(qT_d[h], 128, 0)],
                          lambda h: v_d[:, :, (h // 3) * 128:(h // 3 + 1) * 128],
                          lambda h: h // 3, 128 ** -0.5, 4)
                P.end_phase(dummy)
                outproj(i, ev_w_out[j], 2)
            else:
                if stop == 'mod':
                    P.end_phase(dummy)
                    break
                odd_inproj(i, j, hreg)
                P.end_phase(dummy)
                if stop in ('inproj', 'ip_a', 'ip_b', 'ip_b1', 'ip_b2'):
                    break
                odd_upproj(i, j)
                P.end_phase(dummy)
                if stop == 'upproj':
                    break
                odd_conv(i, j)
                P.end_phase(dummy)
                if stop == 'conv':
                    break
                attention(12,
                          lambda h: [(kT_d[h], 128, 0), (kr_d[(h % 2) * 64:(h % 2) * 64 + 64, :], 64, (h % 2) * 64)],
                          lambda h: [(qT_d[h], 128, 0), (qr_d[h // 2, (h % 2) * 64:(h % 2) * 64 + 64, :], 64, (h % 2) * 64)],
                          lambda h: v_d[:, :, h * 128:(h + 1) * 128],
                          lambda h: h, 192 ** -0.5, 4)
                P.end_phase(dummy)
                if stop == 'attn':
                    break
                outproj(i, od_w_out[j], 2)
                if stop == 'outproj':
                    P.end_phase(dummy)
                    break
            P.end_phase(dummy)
            hreg = new_hreg()
            modulate(i, 3, 4, hreg)
            ffn_up(i, hreg)
            P.end_phase(dummy)
            ffn_down(i, 5)
            P.end_phase(dummy)
        if final:
            final_norm()
        else:
            copy_out()
        P.add('sp', lambda e: e.nop(), reads=[outreg])
        P.emit()
    return nc, dt_in


def _host_consts():
    bf = ml_dtypes.bfloat16
    c = {}
    c["ident_bf"] = np.eye(128, dtype=np.float32).astype(bf)
    c["ident_f"] = np.eye(128, dtype=np.float32)
    n = np.arange(128)
    a = 2 * np.pi * np.outer(n, n) / 128.0
    c["dft128"] = (np.concatenate([np.cos(a), np.sin(a)], axis=1) / np.sqrt(128.0)).astype(np.float32).astype(bf)

    def dft(N):
        k = np.arange(N, dtype=np.int64)
        ph = (np.outer(k, k) % N).astype(np.float64) * (2 * np.pi / N)
        return np.stack([np.cos(ph), -np.sin(ph)], 0) / np.sqrt(float(N))
    c["dftL"] = dft(L).astype(np.float32).astype(bf)
    c["dftC"] = dft(CL).astype(np.float32).astype(bf)

    def tables(rot_dim):
        rows = L // 64
        row = np.repeat(np.arange(rows, dtype=np.float32), 64)
        col = np.tile(np.arange(64, dtype=np.float32), rows)
        nn = rot_dim // 4
        inv = np.power(np.float32(10000.0), -np.arange(nn, dtype=np.float32) / nn).astype(np.float32)
        ang = np.concatenate([row[:, None] * inv, col[:, None] * inv], axis=-1).astype(np.float32)
        return np.cos(ang).astype(np.float32), np.sin(ang).astype(np.float32)
    ca, sa = tables(128)
    c["ropeA"] = np.concatenate([ca, sa], axis=1).reshape(16, 128, 128).astype(np.float32)
    cm, sm = tables(64)
    cosT = np.ones((64, T), np.float32)
    sinT = np.zeros((64, T), np.float32)
    cosT[:32, :L] = cm.T
    cosT[32:, :L] = cm.T
    sinT[:32, :L] = -sm.T
    sinT[32:, :L] = sm.T
    c["ropeM"] = np.stack([np.concatenate([cosT, cosT], 0), np.concatenate([sinT, sinT], 0)], 0)
    return c


def _col(v, nchunk):
    return np.ascontiguousarray(v.reshape(nchunk, 128).T)


_CACHE = {}
ACTIVE = [0, 1, 4, 5]


def make_in_maps(inputs, ncores=8):
    f = lambda a: np.ascontiguousarray(np.asarray(a, dtype=np.float32))
    x, c, ctx, c_ctx = f(inputs["x"]), f(inputs["c"]), f(inputs["ctx"]), f(inputs["c_ctx"])
    shared = dict(_host_consts())
    shared["ada_w"] = f(inputs["ada_w"])
    ab = f(inputs["ada_b"])
    shared["ada_bc"] = np.ascontiguousarray(ab.reshape(DEPTH, 96, 128).transpose(2, 0, 1).reshape(128, DEPTH * 96))
    shared["ev_w_in"] = f(inputs["ev_w_in"])
    shared["ev_w_out"] = f(inputs["ev_w_out"])
    g = np.stack([f(inputs["ev_q_gain"]), f(inputs["ev_k_gain"])], 1)
    shared["gains"] = np.ascontiguousarray(np.broadcast_to(g.reshape(1, 512), (128, 512)))
    shared["od_w_in"] = f(inputs["od_w_in"])
    shared["od_w_uq"] = f(inputs["od_w_uq"])
    shared["od_w_ukv"] = f(inputs["od_w_ukv"])
    shared["od_w_out"] = f(inputs["od_w_out"])
    cw = f(inputs["od_conv_w"])
    shared["od_cw"] = np.ascontiguousarray(cw.reshape(2, 31, 4, 128).transpose(3, 0, 2, 1).reshape(128, 2 * 4 * 31))
    vecs = np.stack([f(inputs[k]) for k in ("od_conv_b", "od_ln_g", "od_ln_b", "od_q_norm", "od_kv_norm")], 1)
    shared["od_vec"] = np.ascontiguousarray(vecs.reshape(2, 5, 4, 128).transpose(3, 0, 1, 2).reshape(128, 40))
    shared["ffn_w_up"] = f(inputs["ffn_w_up"])
    shared["ffn_w_down"] = f(inputs["ffn_w_down"])
    fcw = f(inputs["ffn_conv_w"])
    shared["ffn_cw"] = np.ascontiguousarray(fcw.reshape(DEPTH, 3, NJ, 128).transpose(3, 0, 2, 1).reshape(128, DEPTH * NJ * 3))
    fcb = f(inputs["ffn_conv_b"])
    shared["ffn_cb"] = np.ascontiguousarray(fcb.reshape(DEPTH, NJ, 128).transpose(2, 0, 1).reshape(128, DEPTH * NJ))
    shared["fin_g"] = _col(f(inputs["final_norm"]), KC)
    maps = []
    zeros = {k: np.zeros_like(v) for k, v in shared.items()}
    for core in range(ncores):
        if core not in ACTIVE:
            m = dict(zeros)
            m["xT_in"] = np.zeros((KC, 128, T), np.float32)
            m["c2"] = np.zeros((128, KC * 2), np.float32)
            maps.append(m)
            continue
        b = ACTIVE.index(core)
        m = dict(shared)
        xa = np.concatenate([x[b], ctx[b]], axis=0)
        m["xT_in"] = np.ascontiguousarray(xa.T.reshape(KC, 128, T))
        cc = np.stack([_col(c[b], KC), _col(c_ctx, KC)], -1)
        m["c2"] = np.ascontiguousarray(cc.reshape(128, KC * 2))
        maps.append(m)
    return maps


def kernel(**inputs):
    if "nc" not in _CACHE:
        _CACHE["nc"] = build_program()
    nc, _ = _CACHE["nc"]
    maps = make_in_maps(inputs)
    res = run_bass_kernel_spmd(nc, maps, core_ids=list(range(8)))
    outs = []
    for b in range(4):
        yT = np.asarray(res.results[ACTIVE[b]]["yT"], dtype=np.float32)
        outs.append(yT.reshape(D, L).T)
    return np.ascontiguousarray(np.stack(outs, 0)).astype(np.float32)
```
